# Optimizing a Trainium2 kernel written in Bass

```python
import math
import jax, jax.numpy as jnp
from jax import lax
import numpy as np

D_MODEL = 1024
BATCH = 8
SEQ = 2048
DEPTH = 2
DEC_BATCH = 128
DEC_SEQ = 1
PAST_LEN = 16384
PAGE_SIZE = 128

N_EVEN = (DEPTH + 1) // 2
N_ODD = DEPTH // 2
CONV_W = 4
CHUNK = 64
EPS = 1e-6

GDN_HEADS = 8
GDN_DK = 128
GDN_DV = 128
GDN_KW = GDN_HEADS * GDN_DK
GDN_VW = GDN_HEADS * GDN_DV
GDN_CONV_DIM = 2 * GDN_KW + GDN_VW
GLA_HEADS = 4
GLA_DK = D_MODEL // 2 // GLA_HEADS
GLA_DV = D_MODEL // GLA_HEADS
GLA_KW = GLA_HEADS * GLA_DK
GLA_VW = GLA_HEADS * GLA_DV
GLA_RANK = 16
GLA_TAU = 16.0
SSD_DI = 2 * D_MODEL
SSD_P = 64
SSD_HEADS = SSD_DI // SSD_P
SSD_N = 128
SSD_G = 4
SSD_CONV_DIM = SSD_DI + 2 * SSD_G * SSD_N

EVEN_SPLITS = (GDN_KW, GDN_KW, GDN_VW, GDN_VW, GDN_HEADS, GDN_HEADS, GLA_KW, GLA_KW, GLA_VW, GLA_VW, GLA_RANK)
EVEN_IN = sum(EVEN_SPLITS)
EVEN_OUT = GDN_VW + GLA_VW
ODD_SPLITS = (SSD_DI, SSD_CONV_DIM, SSD_HEADS)
ODD_IN = sum(ODD_SPLITS)

kernel_name = 'hybrid_gdn_gla_ssd_decode_step'


def _split(t, sizes):
    return jnp.split(t, np.cumsum(sizes)[:-1].tolist(), axis=-1)


def rmsnorm(x, g):
    xf = x.astype(jnp.float32)
    y = xf * lax.rsqrt(jnp.mean(xf * xf, axis=-1, keepdims=True) + EPS)
    return (y * g.astype(jnp.float32)).astype(x.dtype)


def l2norm(x):
    return x * lax.rsqrt(jnp.sum(x * x, axis=-1, keepdims=True) + EPS)


def causal_conv(x, buf, w, b):
    L = x.shape[1]
    xp = jnp.concatenate([buf.astype(x.dtype), x], axis=1)
    out = b.astype(x.dtype)
    for i in range(CONV_W):
        out = out + xp[:, i:i + L] * w[i].astype(x.dtype)
    return jax.nn.silu(out), xp[:, L:]


def _chunk(t, C):
    L = t.shape[1]
    nc = -(-L // C)
    pad = nc * C - L
    if pad:
        t = jnp.pad(t, [(0, 0), (0, pad)] + [(0, 0)] * (t.ndim - 2))
    return t.reshape((t.shape[0], nc, C) + t.shape[2:])


def _heads_first(t):
    return jnp.moveaxis(jnp.moveaxis(t, 1, 0), 3, 2)


def _unchunk(o, L):
    nc, Bsz, H, C, E = o.shape
    o = jnp.swapaxes(jnp.moveaxis(o, 0, 1), 2, 3)
    return o.reshape(Bsz, nc * C, H, E)[:, :L]


def gated_delta_chunked(q, k, v, beta, log_g, S0):
    L, DV = v.shape[1], v.shape[-1]
    C = min(CHUNK, L)
    q, k, v, beta, log_g = [_heads_first(_chunk(t, C)) for t in (q, k, v, beta, log_g)]
    gam = jnp.cumsum(log_g, axis=-1)
    incl = jnp.tril(jnp.ones((C, C), dtype=bool))
    strict = jnp.tril(jnp.ones((C, C), dtype=bool), -1)
    decay = jnp.exp(jnp.where(incl, gam[..., :, None] - gam[..., None, :], -jnp.inf))
    kk = jnp.einsum('nbhtd,nbhjd->nbhtj', k, k)
    lmat = jnp.eye(C, dtype=jnp.float32) + beta[..., None] * jnp.where(strict, kk * decay, 0.0)
    eg = jnp.exp(gam)
    rhs = jnp.concatenate([beta[..., None] * v, (beta * eg)[..., None] * k], axis=-1)
    sol = lax.linalg.triangular_solve(lmat, rhs, left_side=True, lower=True, unit_diagonal=True)
    u_part, w = sol[..., :DV], sol[..., DV:]
    p = jnp.einsum('nbhtd,nbhjd->nbhtj', q, k) * decay
    qg = eg[..., None] * q
    kdec = jnp.exp(gam[..., -1:] - gam)[..., None] * k
    glast = jnp.exp(gam[..., -1])

    def step(S, inp):
        u_p, w_c, p_c, qg_c, kd_c, gl_c = inp
        u = u_p - jnp.einsum('bhcd,bhde->bhce', w_c, S)
        o = jnp.einsum('bhcd,bhde->bhce', qg_c, S) + jnp.einsum('bhtj,bhje->bhte', p_c, u)
        S = gl_c[..., None, None] * S + jnp.einsum('bhcd,bhce->bhde', kd_c, u)
        return S, o

    S, o = lax.scan(step, S0, (u_part, w, p, qg, kdec, glast))
    return _unchunk(o, L), S


def gla_chunked(q, k, v, log_a, S0):
    L = v.shape[1]
    C = min(CHUNK, L)
    q, k, v, log_a = [_heads_first(_chunk(t, C)) for t in (q, k, v, log_a)]
    b = jnp.cumsum(log_a, axis=-2)
    incl = jnp.tril(jnp.ones((C, C), dtype=bool))
    qe = q * jnp.exp(b)
    p = jnp.where(incl, jnp.einsum('nbhtd,nbhjd->nbhtj', qe, k * jnp.exp(-b)), 0.0)
    o_intra = jnp.einsum('nbhtj,nbhje->nbhte', p, v)
    kdec = k * jnp.exp(b[..., -1:, :] - b)
    alast = jnp.exp(b[..., -1, :])

    def step(S, inp):
        qe_c, kd_c, v_c, al_c = inp
        o = jnp.einsum('bhcd,bhde->bhce', qe_c, S)
        S = al_c[..., None] * S + jnp.einsum('bhcd,bhce->bhde', kd_c, v_c)
        return S, o

    S, o_inter = lax.scan(step, S0, (qe, kdec, v, alast))
    return _unchunk(o_intra + o_inter, L), S


def ssd_chunked(x, dt, A, Bm, Cm, h0):
    Bsz, L, H, P = x.shape
    G, N = Bm.shape[2], Bm.shape[3]
    K = H // G
    C = min(CHUNK, L)
    a = dt * A
    xdt = x * dt[..., None]
    xdt, a, Bm, Cm = [jnp.moveaxis(_chunk(t, C), 1, 0) for t in (xdt, a, Bm, Cm)]
    nc = a.shape[0]
    lam = jnp.cumsum(a, axis=2)
    lam_h = jnp.swapaxes(lam, 2, 3)
    incl = jnp.tril(jnp.ones((C, C), dtype=bool))
    decay = jnp.exp(jnp.where(incl, lam_h[..., :, None] - lam_h[..., None, :], -jnp.inf))
    cb = jnp.einsum('nbtgs,nbjgs->nbgtj', Cm, Bm)
    m = cb[:, :, :, None] * decay.reshape(nc, Bsz, G, K, C, C)
    xdt = xdt.reshape(nc, Bsz, C, G, K, P)
    y_intra = jnp.einsum('nbgktj,nbjgkp->nbtgkp', m, xdt)
    elam = jnp.exp(lam).reshape(nc, Bsz, C, G, K)
    xdec = xdt * jnp.exp(lam[:, :, -1:, :] - lam).reshape(nc, Bsz, C, G, K)[..., None]
    elast = jnp.exp(lam[:, :, -1, :]).reshape(nc, Bsz, G, K)

    def step(h, inp):
        c_c, b_c, xd_c, el_c, last_c = inp
        y = el_c[..., None] * jnp.einsum('btgs,bgkps->btgkp', c_c, h)
        h = last_c[..., None, None] * h + jnp.einsum('bjgkp,bjgs->bgkps', xd_c, b_c)
        return h, y

    h, y_inter = lax.scan(step, h0.reshape(Bsz, G, K, P, N), (Cm, Bm, xdec, elam, elast))
    y = jnp.moveaxis(y_intra + y_inter, 0, 1).reshape(Bsz, nc * C, H, P)[:, :L]
    return y, h.reshape(Bsz, H, P, N)


def even_layer(h, conv_buf, s_gdn, s_gla, pre_g, post_g, w_in, conv_w, conv_b, gdn_a_log, gdn_dt_bias,
               gdn_norm_g, gla_w_lr, gla_b_lr, gla_norm_g, w_out):
    Bsz, L, _ = h.shape
    f32 = jnp.float32
    u = rmsnorm(h, pre_g)
    proj = jnp.einsum('bld,de->ble', u, w_in).astype(f32)
    gq, gk, gv, g_gate, g_beta, g_a, lq, lk, lv, l_gate, l_lr = _split(proj, EVEN_SPLITS)
    qkv, new_buf = causal_conv(jnp.concatenate([gq, gk, gv], axis=-1), conv_buf, conv_w, conv_b)
    q, k, v = _split(qkv, (GDN_KW, GDN_KW, GDN_VW))
    q = l2norm(q.reshape(Bsz, L, GDN_HEADS, GDN_DK)) * (GDN_DK ** -0.5)
    k = l2norm(k.reshape(Bsz, L, GDN_HEADS, GDN_DK))
    v = v.reshape(Bsz, L, GDN_HEADS, GDN_DV)
    beta = jax.nn.sigmoid(g_beta)
    log_g = -jnp.exp(gdn_a_log.astype(f32)) * jax.nn.softplus(g_a + gdn_dt_bias.astype(f32))
    o1, s_gdn_new = gated_delta_chunked(q, k, v, beta, log_g, s_gdn.astype(f32))
    o1 = rmsnorm(o1, gdn_norm_g) * jax.nn.silu(g_gate.reshape(Bsz, L, GDN_HEADS, GDN_DV))
    log_a = jax.nn.log_sigmoid(l_lr @ gla_w_lr.astype(f32) + gla_b_lr.astype(f32)) / GLA_TAU
    o2, s_gla_new = gla_chunked(lq.reshape(Bsz, L, GLA_HEADS, GLA_DK) * (GLA_DK ** -0.5),
                                lk.reshape(Bsz, L, GLA_HEADS, GLA_DK),
                                lv.reshape(Bsz, L, GLA_HEADS, GLA_DV),
                                log_a.reshape(Bsz, L, GLA_HEADS, GLA_DK), s_gla.astype(f32))
    o2 = rmsnorm(o2, gla_norm_g) * jax.nn.silu(l_gate.reshape(Bsz, L, GLA_HEADS, GLA_DV))
    o = jnp.concatenate([o1.reshape(Bsz, L, GDN_VW), o2.reshape(Bsz, L, GLA_VW)], axis=-1).astype(h.dtype)
    o = jnp.einsum('ble,ed->bld', o, w_out)
    h = h + rmsnorm(o, post_g)
    return h, new_buf.astype(conv_buf.dtype), s_gdn_new.astype(s_gdn.dtype), s_gla_new.astype(s_gla.dtype)


def odd_layer(h, conv_buf, s_ssd, pre_g, post_g, w_in, conv_w, conv_b, dt_bias, a_log, d_skip, norm_g, w_out):
    Bsz, L, _ = h.shape
    f32 = jnp.float32
    u = rmsnorm(h, pre_g)
    proj = jnp.einsum('bld,de->ble', u, w_in).astype(f32)
    z, xbc, dt = _split(proj, ODD_SPLITS)
    xbc, new_buf = causal_conv(xbc, conv_buf, conv_w, conv_b)
    xs, Bm, Cm = _split(xbc, (SSD_DI, SSD_G * SSD_N, SSD_G * SSD_N))
    xs = xs.reshape(Bsz, L, SSD_HEADS, SSD_P)
    Bm = Bm.reshape(Bsz, L, SSD_G, SSD_N)
    Cm = Cm.reshape(Bsz, L, SSD_G, SSD_N)
    dt = jax.nn.softplus(dt + dt_bias.astype(f32))
    A = -jnp.exp(a_log.astype(f32))
    y, s_new = ssd_chunked(xs, dt, A, Bm, Cm, s_ssd.astype(f32))
    y = y + d_skip.astype(f32)[:, None] * xs
    y = (y.reshape(Bsz, L, SSD_DI) * jax.nn.silu(z)).reshape(Bsz, L, SSD_G, SSD_DI // SSD_G)
    y = rmsnorm(y, norm_g.reshape(SSD_G, SSD_DI // SSD_G)).reshape(Bsz, L, SSD_DI).astype(h.dtype)
    o = jnp.einsum('ble,ed->bld', y, w_out)
    h = h + rmsnorm(o, post_g)
    return h, new_buf.astype(conv_buf.dtype), s_new.astype(s_ssd.dtype)


def setup_inputs(seed: int = 0) -> dict:
    key = jax.random.key(seed)
    ks = iter(jax.random.split(key, 40))

    def nrm(shape, scale):
        return jax.random.normal(next(ks), shape, jnp.float32) * scale

    def gain(shape):
        return 1.0 + nrm(shape, 0.02)

    def dt_bias(shape):
        lo, hi = math.log(1e-3), math.log(1e-1)
        dt = jnp.exp(jax.random.uniform(next(ks), shape, jnp.float32) * (hi - lo) + lo)
        return dt + jnp.log(-jnp.expm1(-dt))

    def a_log(shape):
        return jnp.log(jax.random.uniform(next(ks), shape, jnp.float32, 1.0, 16.0))

    return {
        'x_prompt': nrm((BATCH, SEQ, D_MODEL), 1.0),
        'x_sample': nrm((DEC_BATCH, DEC_SEQ, D_MODEL), 1.0),
        'state_gdn_conv': nrm((N_EVEN, DEC_BATCH, CONV_W - 1, GDN_CONV_DIM), 1.0),
        'state_gdn': nrm((N_EVEN, DEC_BATCH, GDN_HEADS, GDN_DK, GDN_DV), 0.1),
        'state_gla': nrm((N_EVEN, DEC_BATCH, GLA_HEADS, GLA_DK, GLA_DV), 0.1),
        'state_ssd_conv': nrm((N_ODD, DEC_BATCH, CONV_W - 1, SSD_CONV_DIM), 1.0),
        'state_ssd': nrm((N_ODD, DEC_BATCH, SSD_HEADS, SSD_P, SSD_N), 0.1),
        'e_pre_g': gain((N_EVEN, D_MODEL)),
        'e_post_g': gain((N_EVEN, D_MODEL)),
        'e_w_in': nrm((N_EVEN, D_MODEL, EVEN_IN), D_MODEL ** -0.5),
        'e_conv_w': nrm((N_EVEN, CONV_W, GDN_CONV_DIM), CONV_W ** -0.5),
        'e_conv_b': nrm((N_EVEN, GDN_CONV_DIM), 0.02),
        'gdn_a_log': a_log((N_EVEN, GDN_HEADS)),
        'gdn_dt_bias': dt_bias((N_EVEN, GDN_HEADS)),
        'gdn_norm_g': gain((N_EVEN, GDN_DV)),
        'gla_w_lr': nrm((N_EVEN, GLA_RANK, GLA_KW), GLA_RANK ** -0.5),
        'gla_b_lr': nrm((N_EVEN, GLA_KW), 0.02),
        'gla_norm_g': gain((N_EVEN, GLA_DV)),
        'e_w_out': nrm((N_EVEN, EVEN_OUT, D_MODEL), EVEN_OUT ** -0.5),
        'o_pre_g': gain((N_ODD, D_MODEL)),
        'o_post_g': gain((N_ODD, D_MODEL)),
        'o_w_in': nrm((N_ODD, D_MODEL, ODD_IN), D_MODEL ** -0.5),
        'ssd_conv_w': nrm((N_ODD, CONV_W, SSD_CONV_DIM), CONV_W ** -0.5),
        'ssd_conv_b': nrm((N_ODD, SSD_CONV_DIM), 0.02),
        'ssd_dt_bias': dt_bias((N_ODD, SSD_HEADS)),
        'ssd_a_log': a_log((N_ODD, SSD_HEADS)),
        'ssd_d': gain((N_ODD, SSD_HEADS)),
        'ssd_norm_g': gain((N_ODD, SSD_DI)),
        'o_w_out': nrm((N_ODD, SSD_DI, D_MODEL), SSD_DI ** -0.5),
    }


def reference(x_prompt, x_sample, state_gdn_conv, state_gdn, state_gla, state_ssd_conv, state_ssd,
              e_pre_g, e_post_g, e_w_in, e_conv_w, e_conv_b, gdn_a_log, gdn_dt_bias, gdn_norm_g,
              gla_w_lr, gla_b_lr, gla_norm_g, e_w_out,
              o_pre_g, o_post_g, o_w_in, ssd_conv_w, ssd_conv_b, ssd_dt_bias, ssd_a_log, ssd_d,
              ssd_norm_g, o_w_out):
    hp, hs = x_prompt, x_sample
    bp = x_prompt.shape[0]
    gdn_conv_p, gdn_p, gla_p, ssd_conv_p, ssd_p = [], [], [], [], []
    gdn_conv_s, gdn_s, gla_s, ssd_conv_s, ssd_s = [], [], [], [], []
    for i in range(DEPTH):
        j = i // 2
        if i % 2 == 0:
            w = (e_pre_g[j], e_post_g[j], e_w_in[j], e_conv_w[j], e_conv_b[j], gdn_a_log[j], gdn_dt_bias[j],
                 gdn_norm_g[j], gla_w_lr[j], gla_b_lr[j], gla_norm_g[j], e_w_out[j])
            zc = jnp.zeros((bp, CONV_W - 1, GDN_CONV_DIM), state_gdn_conv.dtype)
            z1 = jnp.zeros((bp, GDN_HEADS, GDN_DK, GDN_DV), state_gdn.dtype)
            z2 = jnp.zeros((bp, GLA_HEADS, GLA_DK, GLA_DV), state_gla.dtype)
            hp, c, s1, s2 = even_layer(hp, zc, z1, z2, *w)
            gdn_conv_p.append(c); gdn_p.append(s1); gla_p.append(s2)
            hs, c, s1, s2 = even_layer(hs, state_gdn_conv[j], state_gdn[j], state_gla[j], *w)
            gdn_conv_s.append(c); gdn_s.append(s1); gla_s.append(s2)
        else:
            w = (o_pre_g[j], o_post_g[j], o_w_in[j], ssd_conv_w[j], ssd_conv_b[j], ssd_dt_bias[j], ssd_a_log[j],
                 ssd_d[j], ssd_norm_g[j], o_w_out[j])
            zc = jnp.zeros((bp, CONV_W - 1, SSD_CONV_DIM), state_ssd_conv.dtype)
            z1 = jnp.zeros((bp, SSD_HEADS, SSD_P, SSD_N), state_ssd.dtype)
            hp, c, s1 = odd_layer(hp, zc, z1, *w)
            ssd_conv_p.append(c); ssd_p.append(s1)
            hs, c, s1 = odd_layer(hs, state_ssd_conv[j], state_ssd[j], *w)
            ssd_conv_s.append(c); ssd_s.append(s1)
    return (hp, hs,
            jnp.stack(gdn_conv_p), jnp.stack(gdn_p), jnp.stack(gla_p), jnp.stack(ssd_conv_p), jnp.stack(ssd_p),
            jnp.stack(gdn_conv_s), jnp.stack(gdn_s), jnp.stack(gla_s), jnp.stack(ssd_conv_s), jnp.stack(ssd_s))
```

```python
import numpy as np
from contextlib import ExitStack
import concourse.bass as bass
import concourse.mybir as mybir
from concourse.bass_utils import run_bass_kernel_spmd

F32 = mybir.dt.float32
BF16 = mybir.dt.bfloat16
AF = mybir.ActivationFunctionType
ALU = mybir.AluOpType
AX = mybir.AxisListType
CENGS = ["pe", "act", "dve", "pool"]
ENGS = CENGS + ["sp"]
EPS = 1e-6
NT = 2064
GRPS = [(0, 512), (512, 512), (1024, 512), (1536, 512), (2048, 16)]
SEGS = [(128 * i, 128) for i in range(16)] + [(2048 + s, 1) for s in range(16)]
class Prog:
    def __init__(self, nc):
        self.nc = nc
        self.es = ExitStack()
        self.streams = {e: [] for e in ENGS}
        self.cnt = {e: 0 for e in CENGS}
        self.waited = {e: {} for e in ENGS}
        self.recs = {}
        self.fsz = {}
        self.dma_cnt = {}
        self.nwaits = 0
        self.psum_names = set()
        self.scopes = []

    def sb(self, name, shape, dt):
        st = self.scopes[-1][0] if self.scopes else self.es
        t = st.enter_context(self.nc.sbuf_tensor(name, list(shape), dt))
        self.fsz[name] = int(np.prod(shape[1:]))
        if self.scopes:
            self.scopes[-1][1].append(name)
        return t

    def push(self):
        self.scopes.append((ExitStack(), []))

    def pop(self):
        st, names = self.scopes.pop()
        for e in ENGS:
            waits = {}
            for o in CENGS:
                if o != e and self.cnt[o] > 0 and self.waited[e].get(("E", o), 0) < self.cnt[o]:
                    waits[("E", o)] = self.cnt[o]
            for slot, n in self.dma_cnt.items():
                if self.waited[e].get(("D", slot), 0) < 16 * n:
                    waits[("D", slot)] = 16 * n
            for k_, v in waits.items():
                self.waited[e][k_] = v
            if waits:
                self.streams[e].append((waits, None, None, 0))
        for nm in names:
            self.recs.pop(nm, None)
        st.close()

    def ps(self, name, shape, dt=F32):
        t = self.es.enter_context(self.nc.psum_tensor(name, list(shape), dt))
        self.fsz[name] = int(np.prod(shape[1:]))
        self.psum_names.add(name)
        return t

    def box(self, a):
        name = a.tensor.name
        ap = a.ap
        off = a.offset
        if name in self.fsz:
            F = self.fsz[name]
            p0 = off // F
            f0 = off % F
            ext = sum((c - 1) * abs(s) for s, c in ap[1:])
            return (name, p0, p0 + ap[0][1], f0, f0 + ext + 1)
        ext = sum((c - 1) * abs(s) for s, c in ap)
        return (name, 0, 1, off, off + ext + 1)

    def _deps(self, eng, reads, writes):
        waits = {}
        boxes = []
        for a in reads:
            b = self.box(a)
            if b[0] in self.psum_names:
                boxes.append(((b[0], 0, 128, 0, self.fsz[b[0]]), True))
            else:
                boxes.append((b, False))
        for a in writes:
            b = self.box(a)
            if b[0] in self.psum_names:
                b = (b[0], 0, 128, 0, self.fsz[b[0]])
            boxes.append((b, True))
        for (name, p0, p1, f0, f1), isw in boxes:
            for rec in self.recs.get(name, ()):
                rk, rv, risw, rp0, rp1, rf0, rf1, reng = rec
                if rp0 < p1 and p0 < rp1 and rf0 < f1 and f0 < rf1 and (isw or risw):
                    if reng == eng:
                        if eng == "pe":
                            continue
                        if not (risw and not isw):
                            continue
                    if waits.get(rk, 0) < rv:
                        waits[rk] = rv
        out = {}
        w = self.waited[eng]
        for k, v in waits.items():
            if w.get(k, 0) < v:
                w[k] = v
                out[k] = v
        self.nwaits += len(out)
        return out, boxes

    def _record(self, boxes, semkey, val, eng):
        for (name, p0, p1, f0, f1), isw in boxes:
            lst = self.recs.setdefault(name, [])
            if isw:
                lst[:] = [r for r in lst if not (p0 <= r[3] and r[4] <= p1 and f0 <= r[5] and r[6] <= f1)]
                lst.append((semkey, val, True, p0, p1, f0, f1, eng))
            else:
                for i, r in enumerate(lst):
                    if (not r[2]) and r[7] == eng and r[0] == semkey and r[3:7] == (p0, p1, f0, f1):
                        lst[i] = (semkey, val, False, p0, p1, f0, f1, eng)
                        break
                else:
                    lst.append((semkey, val, False, p0, p1, f0, f1, eng))

    def op(self, eng, fn, reads, writes):
        waits, boxes = self._deps(eng, reads, writes)
        self.cnt[eng] += 1
        val = self.cnt[eng]
        key = ("E", eng)
        self.streams[eng].append((waits, fn, key, 1))
        self._record(boxes, key, val, eng)

    def dma(self, pairs, slot, queue="sp"):
        key = ("D", slot)
        n0 = self.dma_cnt.get(slot, 0)
        final = 16 * (n0 + len(pairs))
        self.dma_cnt[slot] = n0 + len(pairs)
        for o, i in pairs:
            waits, boxes = self._deps(queue, [i], [o])
            self.streams[queue].append(
                (waits, (lambda e, o=o, i=i: e.dma_start(out=o, in_=i)), key, 16))
            self._record(boxes, key, final, "dma")

    def mm(self, out, lhsT, rhs, start=True, stop=True):
        self.op("pe", lambda e: e.matmul(out, lhsT, rhs, start=start, stop=stop), [lhsT, rhs], [out])

    def tr(self, out, in_, ident):
        self.op("pe", lambda e: e.transpose(out, in_, ident), [in_, ident], [out])

    def actf(self, out, in_, func, bias=None, scale=None, eng="act", accum_out=None):
        kw = {}
        rd = [in_]
        wr = [out]
        if bias is not None:
            kw["bias"] = bias
            if not isinstance(bias, (int, float)):
                rd.append(bias)
        if scale is not None:
            kw["scale"] = scale
            if not isinstance(scale, (int, float)):
                rd.append(scale)
        if accum_out is not None:
            kw["accum_out"] = accum_out
            wr.append(accum_out)
        self.op("act", lambda e: e.activation(out, in_, func, **kw), rd, wr)

    def tt(self, out, in0, in1, op, eng="dve"):
        self.op(eng, lambda e: e.tensor_tensor(out, in0, in1, op), [in0, in1], [out])

    def ts(self, out, in0, s1, op0, s2=None, op1=None, eng="dve", accum_out=None):
        rd = [in0] + [s for s in (s1, s2) if s is not None and not isinstance(s, (int, float))]
        wr = [out] + ([accum_out] if accum_out is not None else [])
        kw = {}
        if op1 is not None:
            kw["op1"] = op1
        if accum_out is not None:
            kw["accum_out"] = accum_out
        self.op(eng, lambda e: e.tensor_scalar(out, in0, s1, s2, op0, **kw), rd, wr)

    def stt(self, out, in0, scalar, in1, op0, op1, eng="dve", accum_out=None):
        rd = [in0, in1] + ([scalar] if not isinstance(scalar, (int, float)) else [])
        wr = [out] + ([accum_out] if accum_out is not None else [])
        kw = {}
        if accum_out is not None:
            kw["accum_out"] = accum_out
        self.op(eng, lambda e: e.scalar_tensor_tensor(out, in0, scalar, in1, op0, op1, **kw), rd, wr)

    def cp(self, out, in_, eng="dve"):
        if eng == "act":
            self.op("act", lambda e: e.copy(out, in_), [in_], [out])
        else:
            self.op(eng, lambda e: e.tensor_copy(out, in_), [in_], [out])

    def memset(self, out, val, eng="pool"):
        self.op(eng, lambda e: e.memset(out, val), [], [out])

    def red(self, out, in_, op=None, eng="dve"):
        self.op(eng, lambda e: e.tensor_reduce(out, in_, AX.X, op if op is not None else ALU.add), [in_], [out])

    def recip(self, out, in_):
        self.op("dve", lambda e: e.reciprocal(out, in_), [in_], [out])

    def finish(self):
        for slot, n in self.dma_cnt.items():
            k = ("D", slot)
            v = 16 * n
            if self.waited["sp"].get(k, 0) < v:
                self.waited["sp"][k] = v
                self.streams["sp"].append(({k: v}, None, None, 0))
        nc = self.nc
        keys = [("E", e) for e in CENGS] + [("D", s) for s in self.dma_cnt]
        sems = {}
        for i, k in enumerate(keys):
            sems[k] = self.es.enter_context(nc.semaphore("s%d" % i))
        emap = {"pe": "tensor", "act": "scalar", "dve": "vector", "pool": "gpsimd", "sp": "sync"}

        def replay(name, e):
            for waits, fn, key, inc in self.streams[name]:
                for k, v in waits.items():
                    e.wait_ge(sems[k], v)
                if fn is not None:
                    fn(e).then_inc(sems[key], inc)

        with nc.Block() as block:
            for name in ENGS:
                if not self.streams[name]:
                    continue
                getattr(block, emap[name])(lambda e, name=name: replay(name, e))
        self.es.close()


def consts_np():
    j = np.arange(128)
    c = {}
    c["ident_f"] = np.eye(128, dtype=np.float32)
    c["U_f"] = (j[:, None] <= j[None, :]).astype(np.float32)
    c["Un16_f"] = ((j[:, None] <= j[None, :]).astype(np.float32) / -16.0).astype(np.float32)
    c["negmask_f"] = np.where(j[:, None] <= j[None, :], 0.0, -30000.0).astype(np.float32)
    c["nsmask_f"] = np.where(j[:, None] < j[None, :], -1.0, 0.0).astype(np.float32)
    c["imask_f"] = (j[:, None] <= j[None, :]).astype(np.float32)
    c["ones_f"] = np.ones((128, 128), np.float32)
    c["e16_f"] = np.tile(np.eye(16, dtype=np.float32).reshape(1, 256), (128, 1))
    return c


class K:
    pass


def build(stage=99):
    nc = bass.Bass("TRN2", target_bir_lowering=False)
    P = Prog(nc)
    k = K()

    def din(name, shape):
        return nc.dram_tensor(name, list(shape), F32, kind="ExternalInput").ap()

    def dout(name, shape):
        return nc.dram_tensor(name, list(shape), F32, kind="ExternalOutput").ap()

    def bcast(ap1):
        return ap1.partition_broadcast(128).rearrange("p o f -> p (o f)")

    xp = din("xp", [2048, 1024]); xs = din("xs", [16, 1024])
    w0 = din("w0", [57, 128, 1024]); w1 = din("w1", [41, 128, 1024])
    wo0 = din("wo0", [16, 128, 1024]); wo1 = din("wo1", [16, 128, 1024])
    e_pre_g = din("e_pre_g", [1, 1024]); e_post_g = din("e_post_g", [1, 1024])
    o_pre_g = din("o_pre_g", [1, 1024]); o_post_g = din("o_post_g", [1, 1024])
    e_cw = din("e_cw", [128, 96]); e_cb = din("e_cb", [128, 24])
    s_cw = din("s_cw", [128, 96]); s_cb = din("s_cb", [128, 24])
    g_alog = din("g_alog", [1, 8]); g_dtb = din("g_dtb", [1, 8]); g_ng = din("g_ng", [128, 1])
    l_wlr = din("l_wlr", [17, 512]); l_ng = din("l_ng", [128, 2])
    s_alog = din("s_alog", [1, 32]); s_dtb = din("s_dtb", [1, 32]); s_d = din("s_d", [1, 32])
    s_ng = din("s_ng", [1, 2048])
    s_dcol = din("s_dcol", [128, 16]); s_ngcol = din("s_ngcol", [128, 16])
    st_gconv = din("st_gconv", [48, 3072]); st_gdn = din("st_gdn", [16, 8, 128, 128])
    st_gla = din("st_gla", [16, 4, 128, 256])
    st_sconv = din("st_sconv", [48, 3072]); st_ssd = din("st_ssd", [16, 32 * 64, 128])
    cn = consts_np()
    cd = {n: din("c_" + n, list(a.shape)) for n, a in cn.items()}
    h1scr = nc.dram_tensor("h1scr", [NT, 1024], F32, kind="Internal").ap()

    o_yp = dout("o_yp", [2048, 1024]); o_ys = dout("o_ys", [16, 1024])
    o_gconv_p = dout("o_gconv_p", [3, 3072]); o_gdn_p = dout("o_gdn_p", [8, 128, 128])
    o_gla_p = dout("o_gla_p", [4, 128, 256])
    o_sconv_p = dout("o_sconv_p", [3, 3072]); o_ssd_p = dout("o_ssd_p", [32 * 64, 128])
    o_gconv_s = dout("o_gconv_s", [48, 3072]); o_gdn_s = dout("o_gdn_s", [16, 8, 128, 128])
    o_gla_s = dout("o_gla_s", [16, 4, 128, 256])
    o_sconv_s = dout("o_sconv_s", [48, 3072]); o_ssd_s = dout("o_ssd_s", [16, 32 * 64, 128])

    ident_f = P.sb("ident_f", [128, 128], F32); U_f = P.sb("U_f", [128, 128], F32)
    Un16 = P.sb("Un16_f", [128, 128], F32)
    negmask = P.sb("negmask_f", [128, 128], F32); nsmask = P.sb("nsmask_f", [128, 128], F32)
    imask = P.sb("imask_f", [128, 128], F32)
    ones_f = P.sb("ones_f", [128, 128], F32)
    ident_b = P.sb("ident_b", [128, 128], BF16); ones_b = P.sb("ones_b", [128, 128], BF16)
    E16 = P.sb("e16_f", [128, 16, 16], F32)
    P.dma([(E16[:].rearrange("p a b -> p (a b)"), cd["e16_f"][:])], "c_e16")
    for t, n in [(ident_f, "ident_f"), (U_f, "U_f"), (Un16, "Un16_f"), (negmask, "negmask_f"), (nsmask, "nsmask_f"),
                 (imask, "imask_f"), (ones_f, "ones_f")]:
        P.dma([(t[:], cd[n][:])], "c_" + n)
    P.cp(ident_b[:], ident_f[:], eng="pool"); P.cp(ones_b[:], ones_f[:], eng="pool")

    cw = P.sb("cw", [128, 96], F32); cb = P.sb("cb", [128, 24], F32)
    P.dma([(cw[:], e_cw[:]), (cb[:], e_cb[:])], "c_cw")

    pA = [P.ps("pA0", [128, 512]), P.ps("pA1", [128, 512])]
    pT = P.ps("pT", [128, 1024], BF16)
    pS = P.ps("pS", [128, 512])
    pC = [P.ps("pC%d" % i, [128, 512]) for i in range(4)]

    uT = P.sb("uT", [128, 8, NT], BF16)
    oT = P.sb("oT", [128, 16, NT], BF16)
    ub = [P.sb("ub0", [128, 1024], BF16), P.sb("ub1", [128, 1024], BF16)]
    ss = P.sb("ss", [128, 4], F32)
    wst = [P.sb("wst0", [128, 1024], F32), P.sb("wst1", [128, 1024], F32)]
    wbf = [P.sb("wbf0", [128, 8, 128], BF16), P.sb("wbf1", [128, 8, 128], BF16)]
    k.ui = 0

    def rstd_of(s_, scale):
        P.actf(s_, s_, AF.Ln, bias=EPS, scale=scale)
        P.actf(s_, s_, AF.Exp, scale=-0.5)

    def norm_to_uT(x_, n, col0, gbc):
        i = k.ui % 2; k.ui += 1
        u_ = ub[i]
        s_ = ss[0:n, i:i + 1]
        P.memset(s_, 0.0, eng="dve")
        P.actf(u_[0:n, :], x_[0:n, :], AF.Square, accum_out=s_)
        rstd_of(s_, 1.0 / 1024)
        P.stt(u_[0:n, :], x_[0:n, :], s_, gbc[0:n, :], ALU.mult, ALU.mult)
        for kc in range(8):
            P.tr(pT[:, kc * 128:kc * 128 + n], u_[0:n, kc * 128:(kc + 1) * 128], ident_b[0:n, 0:n])
        P.cp(uT[:, :, col0:col0 + n], pT[:].rearrange("p (k t) -> p k t", k=8)[:, :, 0:n], eng="act")

    ROWT = [(xp[i * 128:(i + 1) * 128, :], 128, i * 128) for i in range(16)] + [(xs[:, :], 16, 2048)]
    P.push()
    xt = [P.sb("xt0", [128, 1024], F32), P.sb("xt1", [128, 1024], F32)]
    gbc0 = P.sb("gbc0", [128, 1024], F32)
    P.dma([(gbc0[:], bcast(e_pre_g[0:1, :]))], "c_gbc")
    for i, (src, n, col0) in enumerate(ROWT):
        P.dma([(xt[i % 2][0:n, :], src)], "xt%d" % (i % 2))
        norm_to_uT(xt[i % 2], n, col0, gbc0)
    P.pop()

    if stage <= -3:
        P.finish(); return nc, cn
    k.wi = 0

    def load_w(wsrc, blk, dst=None):
        i = k.wi % 2; k.wi += 1
        P.dma([(wst[i][:], wsrc[blk])], "wst%d" % i)
        if dst is None:
            dst = wbf[i][:]
        P.cp(dst, wst[i][:].rearrange("p (k c) -> p k c", k=8), eng="pool")
        return wbf[i]

    class WQ:
        def __init__(self, items):
            self.items = list(items); self.nd = 0; self.nc_ = 0; self.dq = []; self.cq = []

        def _dma(self):
            if self.nd < len(self.items):
                wsrc, blk, dst = self.items[self.nd]; self.nd += 1
                bi = k.wi % 2; k.wi += 1
                P.dma([(wst[bi][:], wsrc[blk])], "wst%d" % bi)
                self.dq.append((bi, dst))

        def _cast(self):
            if self.dq:
                bi, dst = self.dq.pop(0)
                d_ = wbf[bi][:] if dst is None else dst
                P.cp(d_, wst[bi][:].rearrange("p (k c) -> p k c", k=8), eng="pool")
                self.cq.append(wbf[bi])

        def get(self):
            if not self.cq:
                self._dma(); self._cast()
            cur = self.cq.pop(0)
            if not self.dq:
                self._dma()
            self._cast()
            self._dma()
            return cur

    k.pi = 0

    def proj_fm(wb, evac, M=128, c0=0):
        for (t0, n) in GRPS:
            pa = pA[k.pi % 2]; k.pi += 1
            for kc in range(8):
                P.mm(pa[0:M, 0:n], wb[:, kc, c0:c0 + M], uT[:, kc, t0:t0 + n], start=(kc == 0), stop=(kc == 7))
            evac(pa[0:M, 0:n], t0, n)

    def load_hist(hist_, src):
        for pc in range(3):
            hb = wst[k.wi % 2]
            P.dma([(hb[0:48, :], src[:, pc * 1024:(pc + 1) * 1024])], "wst%d" % (k.wi % 2))
            k.wi += 1
            for c8 in range(8):
                ct = pc * 8 + c8
                P.tr(pS[:, 256:304], hb[0:48, c8 * 128:(c8 + 1) * 128], ident_f[0:48, 0:48])
                P.cp(hist_[:, ct, :], pS[:, 256:304])

    hist = P.sb("hist", [128, 24, 48], F32)
    Xc = P.sb("Xc", [128, 3 + 2048], F32)
    Xs = P.sb("Xs", [128, 16, 4], F32)
    Yc = P.sb("Yc", [128, NT], F32)
    nst = P.sb("nst", [128, 51], F32)
    nso = [P.sb("nso0", [51, 128], F32), P.sb("nso1", [51, 128], F32)]
    k.ni = 0
    P.memset(Xc[:, 0:3], 0.0)

    def conv_tile(wq, ct, ocp, ocs):
        wb = wq.get()

        def ev(pa, t0, n):
            if t0 < 2048:
                P.cp(Xc[:, 3 + t0:3 + t0 + n], pa, eng="act")
            else:
                P.cp(Xs[:, :, 3], pa, eng="act")
        proj_fm(wb, ev)
        P.cp(Xs[:, :, 0:3], hist[:, ct, :].rearrange("p (s r) -> p s r", r=3), eng="pool")
        for (xin, yout) in [(lambda i: Xc[:, i:i + 2048], Yc[:, 0:2048]), (lambda i: Xs[:, :, i], Yc[:, 2048:2064])]:
            P.ts(yout, xin(0), cw[:, ct * 4:ct * 4 + 1], ALU.mult, cb[:, ct:ct + 1], ALU.add)
            for i in (1, 2, 3):
                P.stt(yout, xin(i), cw[:, ct * 4 + i:ct * 4 + i + 1], yout, ALU.mult, ALU.add)
        P.actf(Yc[:], Yc[:], AF.Silu)
        P.cp(nst[:, 0:3], Xc[:, 2048:2051], eng="pool")
        P.cp(nst[:, 3:51].rearrange("p (s r) -> p s r", r=3), Xs[:, :, 1:4], eng="pool")
        P.tr(pS[0:51, 384:512], nst[:], ident_f[:])
        no = nso[k.ni % 2]
        P.cp(no[:], pS[0:51, 384:512])
        P.dma([(ocp[:, ct * 128:(ct + 1) * 128], no[0:3, :]), (ocs[:, ct * 128:(ct + 1) * 128], no[3:51, :])], "nso%d" % (k.ni % 2))
        k.ni += 1

    load_hist(hist, st_gconv)

    P.push()
    lrT = P.sb("lrT", [17, NT], BF16)
    P.push()
    wsm = P.sb("wsm", [128, 8, 128], BF16)
    load_w(w0, 56, wsm[:])
    alog_bc = P.sb("alog_bc", [128, 8], F32); dtb_bc = P.sb("dtb_bc", [128, 8], F32)
    negA = P.sb("negA", [128, 8], F32); ng_col = P.sb("ng_col", [128, 1], F32)
    P.dma([(ng_col[:], g_ng[:]), (alog_bc[:], bcast(g_alog[0:1, :])), (dtb_bc[:], bcast(g_dtb[0:1, :]))], "c_misc")
    P.actf(negA[:], alog_bc[:], AF.Exp)
    P.ts(negA[:], negA[:], -1.0, ALU.mult)
    SC = P.sb("SC", [128, 16, 64], F32)
    SCs = P.sb("SCs", [16, 32], F32)
    GL = P.sb("GL", [128, 16, 16], F32)
    tmp8 = P.sb("tmp8", [128, 16], F32)
    for kc in range(8):
        P.mm(pS[0:16, 0:32], uT[:, kc, 2048:2064], wsm[:, kc, 0:32], start=(kc == 0), stop=(kc == 7))
    P.actf(SCs[:, 0:8], pS[0:16, 0:8], AF.Sigmoid)
    P.tt(tmp8[0:16, 0:8], pS[0:16, 8:16], dtb_bc[0:16, :], ALU.add)
    P.actf(tmp8[0:16, 0:8], tmp8[0:16, 0:8], AF.Exp)
    P.actf(tmp8[0:16, 0:8], tmp8[0:16, 0:8], AF.Ln, bias=1.0)
    P.tt(SCs[:, 8:16], tmp8[0:16, 0:8], negA[0:16, :], ALU.mult)
    P.actf(SCs[:, 16:24], SCs[:, 8:16], AF.Exp)
    P.ts(SCs[:, 24:32], SCs[:, 16:24], -1.0, ALU.mult)
    for g, (c0, n) in enumerate(SEGS[:16]):
        ps = pS[0:n, 0:32]
        for kc in range(8):
            P.mm(ps, uT[:, kc, c0:c0 + n], wsm[:, kc, 0:32], start=(kc == 0), stop=(kc == 7))
        sc = SC[0:n, g, :]
        P.actf(sc[:, 0:8], ps[:, 0:8], AF.Sigmoid)
        P.tt(tmp8[0:n, 0:8], ps[:, 8:16], dtb_bc[0:n, :], ALU.add)
        P.actf(tmp8[0:n, 0:8], tmp8[0:n, 0:8], AF.Exp)
        P.actf(tmp8[0:n, 0:8], tmp8[0:n, 0:8], AF.Ln, bias=1.0)
        P.tt(sc[:, 8:16], tmp8[0:n, 0:8], negA[0:n, :], ALU.mult)
        P.mm(pS[0:n, 32:40], U_f[0:n, 0:n], sc[:, 8:16])
        P.mm(pS[:, 40:48], ones_f[0:n, :], sc[:, 8:16])
        P.cp(sc[:, 16:24], pS[0:n, 32:40])
        P.ts(sc[:, 24:32], pS[0:n, 32:40], -1.0, ALU.mult)
        P.actf(sc[:, 32:40], pS[0:n, 32:40], AF.Exp)
        P.ts(sc[:, 40:48], sc[:, 32:40], -1.0, ALU.mult)
        P.cp(GL[:, g, 8:16], pS[:, 40:48])
        P.actf(GL[:, g, 0:8], pS[:, 40:48], AF.Exp)
        P.tt(tmp8[0:n, 8:16], GL[0:n, g, 8:16], sc[:, 16:24], ALU.subtract)
        P.actf(sc[:, 48:56], tmp8[0:n, 8:16], AF.Exp)
    P.memset(lrT[:], 1.0)
    for (t0, n) in GRPS:
        for kc in range(8):
            P.mm(pS[0:16, 0:n], wsm[:, kc, 16:32], uT[:, kc, t0:t0 + n], start=(kc == 0), stop=(kc == 7))
        P.cp(lrT[0:16, t0:t0 + n], pS[0:16, 0:n], eng="act")

    sq = P.sb("sq", [128, NT], BF16)
    rn = P.sb("rn", [128, 512], F32)

    def l2norm_to(dsts, scale):
        P.tt(sq[:], Yc[:], Yc[:], ALU.mult, eng="pool")
        for (t0, n) in GRPS:
            P.mm(pS[:, 0:n], ones_b[:], sq[:, t0:t0 + n])
            P.actf(rn[:, 0:n], pS[:, 0:n], AF.Ln, bias=EPS)
            P.actf(rn[:, 0:n], rn[:, 0:n], AF.Exp, scale=-0.5)
            for d_ in dsts:
                P.stt(d_[:, t0:t0 + n], Yc[:, t0:t0 + n], scale, rn[:, 0:n], ALU.mult, ALU.mult)

    qT = P.sb("qT", [128, NT], BF16); vT = P.sb("vT", [128, NT], BF16)
    kT32 = P.sb("kT32", [128, NT], F32)
    gsT = P.sb("gsT", [128, NT], BF16)
    S = [P.sb("S0", [128, 128], F32), P.sb("S1", [128, 128], F32)]
    Sb = [P.sb("Sb0", [128, 128], BF16), P.sb("Sb1", [128, 128], BF16)]
    vtok = [P.sb("vtk_%d" % i, [128, 128], BF16) for i in range(3)]; kdtok = [P.sb("kdtk_%d" % i, [128, 128], F32) for i in range(3)]
    decTs = [P.sb("decT%d" % i, [128, 128], F32) for i in range(2)]; pTm = [P.sb("pTmm_%d" % i, [128, 128], F32) for i in range(3)]
    Mis = [P.sb("Mi%d" % i, [128, 128], F32) for i in range(2)]; kcbs = [P.sb("kcb%d" % i, [128, 128], BF16) for i in range(2)]
    Abs = [[P.sb("Ab%d_%d" % (t, i), [128, 128], F32) for i in range(2)] for t in range(2)]
    ATbs = [[P.sb("ATb%d_%d" % (t, i), [128, 128], F32) for i in range(2)] for t in range(2)]
    P32 = [P.sb("P32_%d" % i, [128, 128], F32) for i in range(3)]
    R32 = P.sb("R32", [128, 128], F32); u32 = P.sb("u32", [128, 128], F32)
    o2s = P.sb("o2s", [128, 128], F32); osb = P.sb("osb", [128, 128], F32); onb = P.sb("onb", [128, 128], BF16)
    oss = P.sb("oss", [128, 1], F32)
    k.si = 0

    def interleave(gens):
        gens = [g_ for g_ in gens if g_ is not None]
        while gens:
            for g_ in list(gens):
                try:
                    next(g_)
                except StopIteration:
                    gens.remove(g_)

    def gdn_prep(h, g, c0, n, pb, tb):
        sc = SC[0:n, g, :]
        Ab = Abs[tb]; ATb = ATbs[tb]; decT = decTs[tb]; Mi = Mis[tb]; kcb = kcbs[tb]
        if tb == 0:
            r_kk = pC[0][:, 0:128]; r_qk = pC[0][:, 128:256]; r_G = pC[0][:, 256:384]
            r_s0 = pC[1][:, 0:128]; r_s1 = pC[1][:, 128:256]; r_P = pA[0][:, 0:128]
            r_kt = pA[1][:, 0:128]; r_vt = pT[:, 0:128]
        else:
            r_kk = pC[1][:, 256:384]; r_qk = pC[1][:, 384:512]; r_G = pA[0][:, 128:256]
            r_s0 = pA[1][:, 128:256]; r_s1 = pA[1][:, 256:384]; r_P = pA[0][:, 256:384]
            r_kt = pA[1][:, 384:512]; r_vt = pT[:, 128:256]
        qc_ = qT[:, c0:c0 + n]; k32 = kT32[:, c0:c0 + n]
        kc_ = kcb[:, 0:n]
        P.cp(kc_, k32, eng="pool")
        P.tr(r_vt[0:n, :], vT[:, c0:c0 + n], ident_b[:])
        P.cp(vtok[pb][0:n, :], r_vt[0:n, :], eng="act")
        P.tr(r_kt[0:n, :], k32, ident_f[:])
        P.actf(kdtok[pb][0:n, :], r_kt[0:n, :], AF.Copy, scale=sc[:, 48 + h:49 + h])
        yield
        P.mm(r_qk[0:n, 0:n], kc_, qc_)
        P.mm(r_G[0:n, 0:n], sc[:, 8 + h:9 + h].to_broadcast([n, n]), U_f[0:n, 0:n], start=True, stop=False)
        P.mm(r_G[0:n, 0:n], ident_f[0:n, 0:n], negmask[0:n, 0:n], start=False, stop=True)
        if n > 1:
            P.mm(r_kk, kc_, kc_)
        yield
        P.actf(decT[0:n, 0:n], r_G[0:n, 0:n], AF.Exp, bias=sc[:, 24 + h:25 + h])
        P.tt(pTm[pb][0:n, 0:n], r_qk[0:n, 0:n], decT[0:n, 0:n], ALU.mult)
        yield
        if n > 1:
            Pf = P32[pb]
            P.stt(Mi[:], r_kk, sc[:, h:h + 1], decT[:], ALU.mult, ALU.mult)
            P.tt(Ab[0][:], Mi[:], nsmask[:], ALU.mult, eng="pool")
            yield
            P.tr(r_s0, Ab[0][:], ident_f[:])
            P.cp(ATb[0][:], r_s0, eng="act")
            P.tt(Pf[:], Ab[0][:], ident_f[:], ALU.add, eng="pool")
            yield
            cur = 0
            for it in range(6):
                nx = 1 - cur
                P.mm(r_s0, Ab[cur][:], ATb[cur][:])
                if it < 5:
                    P.mm(r_s1, ATb[cur][:], Ab[cur][:])
                yield
                P.cp(ATb[nx][:], r_s0, eng="act")
                if it < 5:
                    P.cp(Ab[nx][:], r_s1, eng="dve")
                yield
                P.mm(r_P, ATb[nx][:], Pf[:])
                yield
                P.tt(Pf[:], r_P, Pf[:], ALU.add)
                yield
                cur = nx

    def gdn_seq(h, g, c0, n, first, last, s_src, s_dst, pb):
        sc = SC[0:n, g, :]
        if first:
            k.si += 1
        Sf = S[k.si % 2]; Sbb = Sb[k.si % 2]
        if first:
            if s_src is None:
                P.memset(Sf[:], 0.0)
            else:
                P.dma([(Sf[:], s_src)], "Sld%d" % (k.si % 2))
            P.cp(Sbb[:], Sf[:], eng="pool")
            yield
        qc_ = qT[:, c0:c0 + n]; k32 = kT32[:, c0:c0 + n]
        Pfin = P32[pb] if n > 1 else ident_f
        P.mm(pC[3][0:n, 0:128], k32, Sf[:])
        yield
        P.stt(R32[0:n, :], pC[3][0:n, 0:128], sc[:, 40 + h:41 + h], vtok[pb][0:n, :], ALU.mult, ALU.add)
        yield
        P.mm(pC[2][0:n, 0:128], Pfin[0:n, 0:n], R32[0:n, :])
        yield
        P.actf(u32[0:n, :], pC[2][0:n, 0:128], AF.Copy, scale=sc[:, h:h + 1])
        yield
        P.mm(pC[3][:, 128:256], kdtok[pb][0:n, :], u32[0:n, :])
        P.mm(pC[2][0:n, 256:384], qc_, Sbb[:])
        P.mm(pC[2][0:n, 384:512], pTm[pb][0:n, 0:n], u32[0:n, :])
        yield
        P.stt(Sf[:], Sf[:], GL[:, g, h:h + 1], pC[3][:, 128:256], ALU.mult, ALU.add)
        yield
        if not last:
            P.cp(Sbb[:], Sf[:], eng="pool")
        else:
            P.dma([(s_dst, Sf[:])], "Sst%d" % (k.si % 2))
        P.cp(o2s[0:n, :], pC[2][0:n, 384:512], eng="act")
        yield
        P.stt(osb[0:n, :], pC[2][0:n, 256:384], sc[:, 32 + h:33 + h], o2s[0:n, :], ALU.mult, ALU.add)
        P.memset(oss[0:n, :], 0.0, eng="dve")
        yield
        P.actf(onb[0:n, :], osb[0:n, :], AF.Square, accum_out=oss[0:n, :])
        rstd_of(oss[0:n, :], 1.0 / 128)
        P.actf(osb[0:n, :], osb[0:n, :], AF.Copy, scale=oss[0:n, :])
        yield
        P.tr(pS[:, 0:n], osb[0:n, :], ident_f[0:n, 0:n])
        yield
        P.tt(oT[:, h, c0:c0 + n], pS[:, 0:n], gsT[:, c0:c0 + n], ALU.mult)
        yield

    def gdn_head_segs(h):
        segl = []
        for g, (c0, n) in enumerate(SEGS[:16]):
            segl.append((g, c0, n, g == 0, g == 15, None, o_gdn_p[h]))
        preps = [gdn_prep(h, sg[0], sg[1], sg[2], i % 3, i % 2) for i, sg in enumerate(segl)]
        done = [False] * len(segl)

        def step(i):
            if i < len(segl) and not done[i]:
                try:
                    next(preps[i])
                except StopIteration:
                    done[i] = True

        while not done[0]:
            step(0)
        for i, (g, c0, n, fi, la, src, dst) in enumerate(segl):
            sq_ = gdn_seq(h, g, c0, n, fi, la, src, dst, i % 3)
            sq_done = False
            while not (sq_done and (i + 1 >= len(segl) or done[i + 1])):
                if not sq_done:
                    try:
                        next(sq_)
                    except StopIteration:
                        sq_done = True
                step(i + 1)
                step(i + 2)

    Sall = P.sb("Sall", [128, 16, 128], F32)
    Km = P.sb("Km", [128, 16, 16], F32); Qm = P.sb("Qm", [128, 16, 16], F32); q32c = P.sb("q32c", [128, 16], F32)
    vts = P.sb("vts", [16, 128], BF16); Rs = P.sb("Rs", [16, 128], F32); us = P.sb("us", [16, 128], F32)
    dg = P.sb("dg", [16, 16], F32); egbc = P.sb("egbc", [128, 16, 1], F32)
    osbs = P.sb("osbs", [16, 128], F32); onbs = P.sb("onbs", [16, 128], F32); osss = P.sb("osss", [16, 1], F32)

    def gdn_sample(h):
        P.dma([(Sall[:], st_gdn[:, h].rearrange("s d e -> d s e"))], "Sall")
        kc32 = kT32[:, 2048:2064]
        P.cp(q32c[:], qT[:, 2048:2064], eng="pool")
        P.tt(Km[:], kc32.unsqueeze(2).to_broadcast([128, 16, 16]), E16[:], ALU.mult, eng="pool")
        P.tt(Qm[:], q32c[:].unsqueeze(2).to_broadcast([128, 16, 16]), E16[:], ALU.mult, eng="pool")
        P.tr(pT[0:16, 0:128], vT[:, 2048:2064], ident_b[:])
        P.cp(vts[:], pT[0:16, 0:128], eng="act")
        P.ts(dg[:], ident_f[0:16, 0:16], SCs[:, 16 + h:17 + h], ALU.mult)
        P.mm(pC[1][:, 0:16], ones_f[0:16, :], dg[:])
        P.cp(egbc[:].rearrange("p s o -> p (s o)"), pC[1][:, 0:16], eng="act")
        for s in range(16):
            P.mm(pC[0][0:16, 0:128], Km[:, s, :], Sall[:, s, :], start=(s == 0), stop=(s == 15))
        P.stt(Rs[:], pC[0][0:16, 0:128], SCs[:, 24 + h:25 + h], vts[:], ALU.mult, ALU.add)
        P.ts(us[:], Rs[:], SCs[:, h:h + 1], ALU.mult)
        P.tt(Sall[:], Sall[:], egbc[:].to_broadcast([128, 16, 128]), ALU.mult)
        for s in range(16):
            bk = pC[2] if (s // 4) % 2 == 0 else pC[3]
            r0 = (s % 4) * 128
            P.mm(bk[:, r0:r0 + 128], ident_f[0:16, s:s + 1].to_broadcast([16, 128]), us[:])
            P.stt(Sall[:, s, :], bk[:, r0:r0 + 128], kc32[:, s:s + 1], Sall[:, s, :], ALU.mult, ALU.add)
        for s in range(16):
            P.mm(pC[0][0:16, 128:256], Qm[:, s, :], Sall[:, s, :], start=(s == 0), stop=(s == 15))
        P.dma([(o_gdn_s[:, h].rearrange("s d e -> d s e"), Sall[:])], "Sall")
        P.cp(osbs[:], pC[0][0:16, 128:256], eng="act")
        P.memset(osss[:], 0.0, eng="dve")
        P.actf(onbs[:], osbs[:], AF.Square, accum_out=osss[:])
        rstd_of(osss[:], 1.0 / 128)
        P.actf(onbs[:], osbs[:], AF.Copy, scale=osss[:])
        P.tr(pS[:, 0:16], onbs[:], ident_f[0:16, 0:16])
        P.tt(oT[:, h, 2048:2064], pS[:, 0:16], gsT[:, 2048:2064], ALU.mult)

    wq0 = WQ([(w0, b_, None) for h in range(8) for b_ in (h, 8 + h, 16 + h, 24 + h)])
    for h in range(8 if stage >= 1 else 0):
        conv_tile(wq0, h, o_gconv_p, o_gconv_s); l2norm_to([qT], 128 ** -0.5)
        conv_tile(wq0, 8 + h, o_gconv_p, o_gconv_s); l2norm_to([kT32], 1.0)
        conv_tile(wq0, 16 + h, o_gconv_p, o_gconv_s); P.cp(vT[:], Yc[:], eng="pool")
        wb = wq0.get()
        proj_fm(wb, lambda pa, t0, n: P.actf(Yc[:, t0:t0 + n], pa, AF.Silu))
        P.ts(gsT[:], Yc[:], ng_col[:], ALU.mult)
        gdn_head_segs(h)
        gdn_sample(h)
    P.pop()
    if stage <= 1:
        P.pop()
        print('sbuf_remaining', nc.sbuf_bytes_remaining)
        print('counts', P.cnt, 'waits', P.nwaits, 'dma slots', len(P.dma_cnt))
        P.finish(); return nc, cn

    P.push()
    wlr = P.sb("wlr", [17, 512], BF16)
    P.dma([(wst[k.wi % 2][0:17, 0:512], l_wlr[:])], "wst%d" % (k.wi % 2))
    P.cp(wlr[:], wst[k.wi % 2][0:17, 0:512], eng="pool")
    k.wi += 1
    lng = P.sb("lng", [128, 2], F32)
    P.dma([(lng[:], l_ng[:])], "c_misc")
    lqT = P.sb("lqT", [128, NT], BF16); lkT = P.sb("lkT", [128, NT], BF16)
    gs2 = P.sb("gs2", [128, 2, NT], BF16)
    wv = P.sb("wv", [128, 8, 256], BF16)
    S2 = [P.sb("S2a", [128, 256], F32), P.sb("S2b", [128, 256], F32)]
    S2h = [P.sb("S2ha", [128, 256], BF16), P.sb("S2hb", [128, 256], BF16)]
    sp32 = P.sb("sp32", [128, 128], F32)
    ebt = P.sb("ebt", [128, 128], F32); enbt = P.sb("enbt", [128, 128], F32); ekdt = P.sb("ekdt", [128, 128], F32)
    bl = P.sb("bl", [128, 2], F32)
    qeb = P.sb("qeb", [128, 128], BF16); keb = P.sb("keb", [128, 128], BF16); kdTb = P.sb("kdTb", [128, 128], BF16)
    kdtok2 = P.sb("kdtok2", [128, 128], BF16); pTm2 = P.sb("pTm2", [128, 128], BF16)
    vtok2 = P.sb("vtok2", [128, 256], BF16)
    osb2 = P.sb("osb2", [128, 256], F32); onb2 = P.sb("onb2", [128, 256], BF16); oss2 = P.sb("oss2", [128, 1], F32)
    k.s2 = 0

    def gla_seg(h, g, c0, n, first, last, s_src, s_dst):
        if first:
            k.s2 += 1
        Sf = S2[k.s2 % 2]; Sbb = S2h[k.s2 % 2]
        if first:
            if s_src is None:
                P.memset(Sf[:], 0.0)
            else:
                P.dma([(Sf[:], s_src)], "S2ld%d" % (k.s2 % 2))
            P.cp(Sbb[:], Sf[:], eng="pool")
        P.mm(pS[0:n, 0:128], lrT[0:17, c0:c0 + n], wlr[0:17, h * 128:(h + 1) * 128])
        P.actf(sp32[0:n, :], pS[0:n, 0:128], AF.Exp, scale=-1.0)
        P.actf(sp32[0:n, :], sp32[0:n, :], AF.Ln, bias=1.0)
        P.mm(pC[0][:, 0:n], sp32[0:n, :], Un16[0:n, 0:n])
        P.actf(ebt[:, 0:n], pC[0][:, 0:n], AF.Exp)
        P.actf(enbt[:, 0:n], pC[0][:, 0:n], AF.Exp, scale=-1.0)
        P.cp(bl[:, 0:1], pC[0][:, n - 1:n])
        P.actf(ekdt[:, 0:n], pC[0][:, 0:n], AF.Exp, scale=-1.0, bias=bl[:, 0:1])
        P.actf(bl[:, 1:2], bl[:, 0:1], AF.Exp)
        P.stt(qeb[:, 0:n], lqT[:, c0:c0 + n], 128 ** -0.5, ebt[:, 0:n], ALU.mult, ALU.mult)
        P.tt(keb[:, 0:n], lkT[:, c0:c0 + n], enbt[:, 0:n], ALU.mult)
        P.tt(kdTb[:, 0:n], lkT[:, c0:c0 + n], ekdt[:, 0:n], ALU.mult, eng="pool")
        P.tr(pT[0:n, 0:128], kdTb[:, 0:n], ident_b[:])
        P.cp(kdtok2[0:n, :], pT[0:n, 0:128], eng="act")
        P.mm(pC[1][0:n, 0:n], keb[:, 0:n], qeb[:, 0:n])
        P.tt(pTm2[0:n, 0:n], pC[1][0:n, 0:n], imask[0:n, 0:n], ALU.mult)
        for kc in range(8):
            P.mm(pA[0][0:n, 0:256], uT[:, kc, c0:c0 + n], wv[:, kc, :], start=(kc == 0), stop=(kc == 7))
        P.cp(vtok2[0:n, :], pA[0][0:n, 0:256], eng="act")
        P.mm(pC[2][0:n, 0:256], qeb[:, 0:n], Sbb[:], start=True, stop=False)
        P.mm(pC[2][0:n, 0:256], pTm2[0:n, 0:n], vtok2[0:n, :], start=False, stop=True)
        P.mm(pC[3][:, 0:256], kdtok2[0:n, :], vtok2[0:n, :])
        P.stt(Sf[:], Sf[:], bl[:, 1:2], pC[3][:, 0:256], ALU.mult, ALU.add)
        if not last:
            P.cp(Sbb[:], Sf[:], eng="pool")
        else:
            P.dma([(s_dst, Sf[:])], "S2st%d" % (k.s2 % 2))
        P.cp(osb2[0:n, :], pC[2][0:n, 0:256], eng="act")
        P.memset(oss2[0:n, :], 0.0, eng="dve")
        P.actf(onb2[0:n, :], osb2[0:n, :], AF.Square, accum_out=oss2[0:n, :])
        rstd_of(oss2[0:n, :], 1.0 / 256)
        P.actf(onb2[0:n, :], osb2[0:n, :], AF.Copy, scale=oss2[0:n, :])
        for j in range(2):
            P.tr(pT[:, 256 + j * 128:256 + j * 128 + n], onb2[0:n, j * 128:(j + 1) * 128], ident_b[0:n, 0:n])
        for j in range(2):
            P.tt(oT[:, 8 + 2 * h + j, c0:c0 + n], pT[:, 256 + j * 128:256 + j * 128 + n], gs2[:, j, c0:c0 + n], ALU.mult)

    Sg = P.sb("Sg", [128, 8, 256], F32)
    Qm2 = P.sb("Qm2", [128, 16, 16], F32); q32g = P.sb("q32g", [128, 16], F32); k32g = P.sb("k32g", [128, 16], F32)
    aTs = P.sb("aTs", [128, 16, 1], F32); sps = P.sb("sps", [16, 128], F32)
    vts2 = P.sb("vts2", [16, 256], BF16)
    osg = P.sb("osg", [16, 256], F32); ong = P.sb("ong", [16, 256], F32); ossg = P.sb("ossg", [16, 1], F32)

    def gla_sample(h):
        P.mm(pS[0:16, 0:128], lrT[0:17, 2048:2064], wlr[0:17, h * 128:(h + 1) * 128])
        P.actf(sps[:], pS[0:16, 0:128], AF.Exp, scale=-1.0)
        P.actf(sps[:], sps[:], AF.Ln, bias=1.0)
        P.actf(sps[:], sps[:], AF.Exp, scale=-1.0 / 16)
        P.tr(pS[:, 128:144], sps[:], ident_f[0:16, 0:16])
        P.cp(aTs[:].rearrange("p s o -> p (s o)"), pS[:, 128:144], eng="act")
        P.cp(k32g[:], lkT[:, 2048:2064], eng="pool")
        P.ts(q32g[:], lqT[:, 2048:2064], 128 ** -0.5, ALU.mult)
        P.tt(Qm2[:], q32g[:].unsqueeze(2).to_broadcast([128, 16, 16]), E16[:], ALU.mult, eng="pool")
        for kc in range(8):
            P.mm(pA[0][0:16, 0:256], uT[:, kc, 2048:2064], wv[:, kc, :], start=(kc == 0), stop=(kc == 7))
        P.cp(vts2[:], pA[0][0:16, 0:256], eng="act")
        for hf in range(2):
            P.dma([(Sg[:], st_gla[hf * 8:(hf + 1) * 8, h].rearrange("s d e -> d s e"))], "Sg")
            for s8 in range(8):
                s = hf * 8 + s8
                bk = pC[2] if s % 2 == 0 else pC[3]
                P.mm(bk[:, 0:256], ident_b[0:16, s:s + 1].to_broadcast([16, 128]), vts2[:])
                P.tt(Sg[:, s8, :], Sg[:, s8, :], aTs[:, s, :].to_broadcast([128, 256]), ALU.mult, eng="pool")
                P.stt(Sg[:, s8, :], bk[:, 0:256], k32g[:, s:s + 1], Sg[:, s8, :], ALU.mult, ALU.add)
            for s8 in range(8):
                s = hf * 8 + s8
                P.mm(pC[0][0:16, 0:256], Qm2[:, s, :], Sg[:, s8, :], start=(s == 0), stop=(s == 15), )
            P.dma([(o_gla_s[hf * 8:(hf + 1) * 8, h].rearrange("s d e -> d s e"), Sg[:])], "Sg")
        P.cp(osg[:], pC[0][0:16, 0:256], eng="act")
        P.memset(ossg[:], 0.0, eng="dve")
        P.actf(ong[:], osg[:], AF.Square, accum_out=ossg[:])
        rstd_of(ossg[:], 1.0 / 256)
        P.actf(ong[:], osg[:], AF.Copy, scale=ossg[:])
        for j in range(2):
            P.tr(pS[:, 256 + j * 16:256 + (j + 1) * 16], ong[:, j * 128:(j + 1) * 128], ident_f[0:16, 0:16])
        for j in range(2):
            P.tt(oT[:, 8 + 2 * h + j, 2048:2064], pS[:, 256 + j * 16:256 + (j + 1) * 16], gs2[:, j, 2048:2064], ALU.mult)

    for h in range(4):
        proj_fm(load_w(w0, 32 + h), lambda pa, t0, n: P.cp(lqT[:, t0:t0 + n], pa, eng="act"))
        proj_fm(load_w(w0, 36 + h), lambda pa, t0, n: P.cp(lkT[:, t0:t0 + n], pa, eng="act"))
        for j in range(2):
            proj_fm(load_w(w0, 48 + 2 * h + j), lambda pa, t0, n: P.actf(Yc[:, t0:t0 + n], pa, AF.Silu))
            P.ts(gs2[:, j, :], Yc[:], lng[:, j:j + 1], ALU.mult)
            load_w(w0, 40 + 2 * h + j, wv[:, :, j * 128:(j + 1) * 128])
        for g, (c0, n) in enumerate(SEGS[:16]):
            gla_seg(h, g, c0, n, g == 0, g == 15, None, o_gla_p[h])
        gla_sample(h)
    P.pop()
    P.pop()
    if stage <= 2:
        print('sbuf_remaining', nc.sbuf_bytes_remaining)
        print('counts', P.cnt, 'waits', P.nwaits, 'dma slots', len(P.dma_cnt))
        P.finish(); return nc, cn

    def out_proj(wo_src, post_g_src, res_src, dst_fn, next_pre_g):
        P.push()
        tag = "L%d" % k.ui
        wo = P.sb("wo" + tag, [128, 16, 1024], BF16)
        gpo = P.sb("gpo" + tag, [128, 1024], F32)
        h1t = [P.sb("h1a" + tag, [128, 1024], F32), P.sb("h1b" + tag, [128, 1024], F32)]
        xt = [P.sb("xta" + tag, [128, 1024], F32), P.sb("xtb" + tag, [128, 1024], F32)]
        ss2 = P.sb("ss2" + tag, [128, 4], F32)
        junk2 = [P.sb("jka" + tag, [128, 512], BF16), P.sb("jkb" + tag, [128, 512], BF16)]
        P.dma([(gpo[:], bcast(post_g_src[0:1, :]))], "c_gpo")
        if next_pre_g is not None:
            gbc = P.sb("gbc" + tag, [128, 1024], F32)
            P.dma([(gbc[:], bcast(next_pre_g[0:1, :]))], "c_gbc")
        for f in range(16):
            i = k.wi % 2; k.wi += 1
            P.dma([(wst[i][:], wo_src[f])], "wst%d" % i)
            P.cp(wo[:, f, :], wst[i][:], eng="pool")
        rows = [(i * 128, 128, i * 128) for i in range(16)] + [(2048, 16, 2048)]
        for i, (r0, n, col0) in enumerate(rows):
            ht = h1t[i % 2]; xr = xt[i % 2]
            P.dma([(xr[0:n, :], res_src(r0, n))], "xr%d" % (i % 2))
            ssx = ss2[0:n, 2 * (i % 2):2 * (i % 2) + 2]
            pb2 = (pA[0], pA[1]) if i % 2 == 0 else (pC[0], pC[1])
            P.memset(ssx, 0.0, eng="dve")
            for hf in range(2):
                for f in range(16):
                    P.mm(pb2[hf][0:n, :], oT[:, f, col0:col0 + n], wo[:, f, hf * 512:(hf + 1) * 512], start=(f == 0), stop=(f == 15))
                P.actf(junk2[i % 2][0:n, :], pb2[hf][0:n, :], AF.Square, accum_out=ssx[:, hf:hf + 1])
            P.tt(ssx[:, 0:1], ssx[:, 0:1], ssx[:, 1:2], ALU.add)
            rstd_of(ssx[:, 0:1], 1.0 / 1024)
            for hf in range(2):
                sl = slice(hf * 512, (hf + 1) * 512)
                P.stt(ht[0:n, sl], pb2[hf][0:n, :], ssx[:, 0:1], gpo[0:n, sl], ALU.mult, ALU.mult)
            P.tt(ht[0:n, :], ht[0:n, :], xr[0:n, :], ALU.add, eng="pool")
            dst_fn(ht, r0, n, i)
            if next_pre_g is not None:
                norm_to_uT(ht, n, col0, gbc)
        P.pop()

    def l0_dst(ht, r0, n, i):
        P.dma([(h1scr[r0:r0 + n, :], ht[0:n, :])], "h1st%d" % (i % 2))

    out_proj(wo0, e_post_g, lambda r0, n: (xp[r0:r0 + n, :] if r0 < 2048 else xs[:, :]), l0_dst, o_pre_g)
    if stage <= 3:
        print('sbuf_remaining', nc.sbuf_bytes_remaining)
        print('counts', P.cnt, 'waits', P.nwaits, 'dma slots', len(P.dma_cnt))
        P.finish(); return nc, cn

    P.dma([(cw[:], s_cw[:]), (cb[:], s_cb[:])], "c_cw")
    load_hist(hist, st_sconv)
    P.push()
    wsm1 = P.sb("wsm1", [128, 8, 128], BF16)
    load_w(w1, 40, wsm1[:])
    sAl = P.sb("sAl", [128, 32], F32); sdtb = P.sb("sdtb", [128, 32], F32); negA1 = P.sb("negA1", [128, 32], F32)
    sD = P.sb("sD", [128, 32, 1], F32)
    sng = P.sb("sng", [128, 512], F32)
    P.dma([(sAl[:], bcast(s_alog[0:1, :])), (sdtb[:], bcast(s_dtb[0:1, :])),
           (sD[:].rearrange("p h o -> p (h o)"), bcast(s_d[0:1, :]))], "c_misc")
    P.actf(negA1[:], sAl[:], AF.Exp)
    P.ts(negA1[:], negA1[:], -1.0, ALU.mult)
    BcT = P.sb("BcT", [128, NT], BF16); CcT = P.sb("CcT", [128, NT], BF16)
    xT = P.sb("xT", [128, 4, NT], BF16)
    wz = P.sb("wz", [128, 8, 512], BF16)
    hT = [P.sb("hTa", [128, 8, 64], F32), P.sb("hTb", [128, 8, 64], F32)]
    hTh = [P.sb("hTha", [128, 512], BF16), P.sb("hThb", [128, 512], BF16)]
    hin = [P.sb("hin0", [128, 4, 128], F32), P.sb("hin1", [128, 4, 128], F32)]
    hld = hin[0]
    sc1s = [P.sb("sc1_%d" % i, [128, 48, 1], F32) for i in range(2)]
    lls = [P.sb("ll_%d" % i, [128, 16, 1], F32) for i in range(2)]
    t8s = [P.sb("t8_%d" % i, [128, 8], F32) for i in range(2)]
    xdt = P.sb("xdt", [128, 8, 64], BF16); xdec = P.sb("xdec", [128, 8, 64], BF16); xtk = P.sb("xtk", [128, 8, 64], BF16)
    Btok = P.sb("Btok", [128, 128], BF16)
    cbT = P.sb("cbT", [128, 128], F32)
    dec1 = [P.sb("dec1_%d" % i, [128, 128], BF16) for i in range(8)]
    mT = [P.sb("mT_%d" % i, [128, 128], BF16) for i in range(8)]
    yis = P.sb("yis", [128, 8, 64], F32); yv = P.sb("yv", [128, 8, 64], F32); zs = P.sb("zs", [128, 512], F32)
    ynb = P.sb("ynb", [128, 512], BF16); oss3 = P.sb("oss3", [128, 1], F32)
    k.s3 = 0

    def bc3(ap_n81, n):
        return ap_n81.to_broadcast([n, 8, 64])

    def ssd_scal(g, c0, n, pb):
        h0 = 8 * g
        sc = sc1s[pb][0:n]; ll = lls[pb]; t8 = t8s[pb]
        for kc in range(8):
            P.mm(pC[0][0:n, 256:264], uT[:, kc, c0:c0 + n], wsm1[:, kc, h0:h0 + 8], start=(kc == 0), stop=(kc == 7))
        yield
        P.tt(t8[0:n, :], pC[0][0:n, 256:264], sdtb[0:n, h0:h0 + 8], ALU.add)
        yield
        P.actf(t8[0:n, :], t8[0:n, :], AF.Exp)
        P.actf(sc[:, 0:8, 0], t8[0:n, :], AF.Ln, bias=1.0)
        yield
        P.tt(sc[:, 8:16, 0], sc[:, 0:8, 0], negA1[0:n, h0:h0 + 8], ALU.mult)
        yield
        P.mm(pC[0][0:n, 288:296], U_f[0:n, 0:n], sc[:, 8:16, 0])
        P.mm(pC[0][:, 296:304], ones_f[0:n, :], sc[:, 8:16, 0])
        yield
        P.cp(sc[:, 16:24, 0], pC[0][0:n, 288:296])
        P.ts(sc[:, 24:32, 0], pC[0][0:n, 288:296], -1.0, ALU.mult)
        P.cp(ll[:, 0:8, 0], pC[0][:, 296:304])
        yield
        P.actf(sc[:, 32:40, 0], pC[0][0:n, 288:296], AF.Exp)
        P.actf(ll[:, 8:16, 0], pC[0][:, 296:304], AF.Exp)
        yield
        P.tt(t8[0:n, :], ll[0:n, 0:8, 0], sc[:, 16:24, 0], ALU.subtract)
        yield
        P.actf(sc[:, 40:48, 0], t8[0:n, :], AF.Exp)
        yield

    def ssd_seg(g, gi, c0, n, first, last, s_src, s_dst, pb):
        if first:
            k.s3 += 1
        hf = hT[k.s3 % 2]; hb = hTh[k.s3 % 2]
        hf2 = hf[:].rearrange("p h q -> p (h q)")
        if first:
            if s_src is None:
                P.memset(hf[:], 0.0)
            else:
                P.dma([(hld[:], s_src.rearrange("(j p) n -> p j n", p=128))], "hld")
                for j in range(4):
                    P.tr(pC[0][:, j * 128:(j + 1) * 128], hld[:, j, :], ident_f[:])
                P.cp(hf2, pC[0][:, :], eng="act")
            P.cp(hb[:], hf2, eng="pool")
        h0 = 8 * g
        sc = sc1s[pb][0:n]; ll = lls[pb]
        yield
        for j in range(4):
            P.tr(pT[0:n, j * 128:(j + 1) * 128], xT[:, j, c0:c0 + n], ident_b[:])
        P.tr(pT[0:n, 512:640], BcT[:, c0:c0 + n], ident_b[:])
        ptx = pT[0:n, 0:512].rearrange("p (h q) -> p h q", h=8)
        P.cp(xtk[0:n], ptx, eng="act")
        P.cp(Btok[0:n, :], pT[0:n, 512:640], eng="act")
        yield
        P.tt(xdt[0:n], xtk[0:n], bc3(sc[:, 0:8, :], n), ALU.mult)
        P.tt(xdec[0:n], xdt[0:n], bc3(sc[:, 40:48, :], n), ALU.mult, eng="pool")
        yield
        P.mm(pC[0][0:n, 0:n], BcT[:, c0:c0 + n], CcT[:, c0:c0 + n])
        P.cp(cbT[0:n, 0:n], pC[0][0:n, 0:n], eng="act")
        P.mm(pC[3][0:n, :], CcT[:, c0:c0 + n], hb[:])
        P.cp(yis[0:n].rearrange("p h q -> p (h q)"), pC[3][0:n, :], eng="act")
        for kc in range(8):
            P.mm(pA[0][0:n, :], uT[:, kc, c0:c0 + n], wz[:, kc, :], start=(kc == 0), stop=(kc == 7))
        P.actf(zs[0:n, :], pA[0][0:n, :], AF.Silu)
        yield
        P.mm(pA[1][:, :], Btok[0:n, :], xdec[0:n].rearrange("p h q -> p (h q)"))
        P.tt(hf[:], hf[:], ll[:, 8:16, :].to_broadcast([128, 8, 64]), ALU.mult)
        P.tt(hf2, hf2, pA[1][:, :], ALU.add)
        if not last:
            P.cp(hb[:], hf2, eng="pool")
        else:
            for j in range(4):
                P.tr(pC[0][:, j * 128:(j + 1) * 128], hf2[:, j * 128:(j + 1) * 128], ident_f[:])
            P.cp(hld[:].rearrange("p j n -> p (j n)"), pC[0][:, :], eng="act")
            P.dma([(s_dst.rearrange("(j p) n -> p j n", p=128), hld[:])], "hst")
        yield
        if n == 1:
            P.cp(mT[0][0:1, 0:1], cbT[0:1, 0:1], eng="pool")
            P.mm(pC[2][0:1, :], mT[0][0:1, 0:1], xdt[0:1].rearrange("p h q -> p (h q)"))
        else:
            for hh in range(8):
                gb = pC[1] if hh < 4 else pS
                r0 = (hh % 4) * 128
                P.mm(gb[:, r0:r0 + 128], sc[:, 8 + hh, :].to_broadcast([128, 128]), U_f[:], start=True, stop=False)
                P.mm(gb[:, r0:r0 + 128], ident_f[:], negmask[:], start=False, stop=True)
            for hh in range(8):
                gb = pC[1] if hh < 4 else pS
                r0 = (hh % 4) * 128
                P.actf(dec1[hh][:], gb[:, r0:r0 + 128], AF.Exp, bias=sc[:, 24 + hh, :])
            for hh in range(8):
                P.tt(mT[hh][:], cbT[:], dec1[hh][:], ALU.mult, eng=("pool" if hh % 2 else "dve"))
            for hh in range(8):
                P.mm(pC[2][:, hh * 64:(hh + 1) * 64], mT[hh][:], xdt[:, hh, :])
        yield
        P.tt(yis[0:n], yis[0:n], bc3(sc[:, 32:40, :], n), ALU.mult)
        P.tt(yv[0:n].rearrange("p h q -> p (h q)"), yis[0:n].rearrange("p h q -> p (h q)"), pC[2][0:n, :], ALU.add)
        P.tt(yis[0:n], xtk[0:n], sD[0:n, h0:h0 + 8, :].to_broadcast([n, 8, 64]), ALU.mult, eng="pool")
        P.tt(yv[0:n], yv[0:n], yis[0:n], ALU.add)
        yv2 = yv[0:n].rearrange("p h q -> p (h q)")
        P.tt(yv2, yv2, zs[0:n, :], ALU.mult)
        P.memset(oss3[0:n, :], 0.0, eng="dve")
        P.actf(ynb[0:n, :], yv2, AF.Square, accum_out=oss3[0:n, :])
        rstd_of(oss3[0:n, :], 1.0 / 512)
        P.stt(ynb[0:n, :], yv2, oss3[0:n, :], sng[0:n, :], ALU.mult, ALU.mult)
        for j in range(4):
            P.tr(pT[:, j * 128:j * 128 + n], ynb[0:n, j * 128:(j + 1) * 128], ident_b[0:n, 0:n])
        P.cp(oT[:, 4 * g:4 * g + 4, c0:c0 + n], pT[:, 0:512].rearrange("p (j t) -> p j t", j=4)[:, :, 0:n], eng="act")

    dcol = P.sb("dcol", [128, 16], F32); ngcol = P.sb("ngcol", [128, 16], F32)
    P.dma([(dcol[:], s_dcol[:]), (ngcol[:], s_ngcol[:])], "c_cols")
    sdt = P.sb("sdt", [16, 32, 1], F32); sea = P.sb("sea", [16, 32, 1], F32)
    for kc in range(8):
        P.mm(pS[0:16, 0:32], uT[:, kc, 2048:2064], wsm1[:, kc, 0:32], start=(kc == 0), stop=(kc == 7))
    sdt2 = sdt[:].rearrange("p h o -> p (h o)"); sea2 = sea[:].rearrange("p h o -> p (h o)")
    P.tt(sdt2, pS[0:16, 0:32], sdtb[0:16, :], ALU.add)
    P.actf(sdt2, sdt2, AF.Exp)
    P.actf(sdt2, sdt2, AF.Ln, bias=1.0)
    P.tt(sea2, sdt2, negA1[0:16, :], ALU.mult)
    P.actf(sea2, sea2, AF.Exp)
    exE = P.sb("exE", [16, 8, 64], F32)
    DTc = P.sb("DTc", [128, 4, 16], F32); EAc = P.sb("EAc", [128, 4, 16], F32); DTX = P.sb("DTX", [128, 4, 16], F32)
    BCt = P.sb("BCt", [16, 256], BF16)
    t1s = yis[:].rearrange("p h q -> p (h q)").rearrange("p (j n) -> p j n", j=4)
    ycol = P.sb("ycol", [128, 4, 16], F32); yz = P.sb("yz", [128, 4, 16], F32); zsT = P.sb("zsT", [128, 4, 16], F32)
    rs16 = P.sb("rs16", [128, 16], F32)

    def ssd_sample(g):
        h0 = 8 * g
        for w, src_ in enumerate((sdt, sea)):
            P.cp(exE[:], src_[:, h0:h0 + 8, :].to_broadcast([16, 8, 64]), eng="pool")
            ex2 = exE[:].rearrange("p h q -> p (h q)")
            for j in range(4):
                P.tr(pS[:, (w * 4 + j) * 16:(w * 4 + j + 1) * 16], ex2[:, j * 128:(j + 1) * 128], ident_f[0:16, 0:16])
        P.cp(DTc[:].rearrange("p j s -> p (j s)"), pS[:, 0:64], eng="act")
        P.cp(EAc[:].rearrange("p j s -> p (j s)"), pS[:, 64:128], eng="act")
        P.tt(DTX[:], xT[:, :, 2048:2064], DTc[:], ALU.mult)
        P.tr(pT[0:16, 0:128], BcT[:, 2048:2064], ident_b[:])
        P.tr(pT[0:16, 128:256], CcT[:, 2048:2064], ident_b[:])
        P.cp(BCt[:], pT[0:16, 0:256], eng="act")
        for j in range(4):
            for kc in range(8):
                P.mm(pA[0][:, j * 16:(j + 1) * 16], wz[:, kc, j * 128:(j + 1) * 128], uT[:, kc, 2048:2064], start=(kc == 0), stop=(kc == 7))
        P.actf(zsT[:].rearrange("p j s -> p (j s)"), pA[0][:, 0:64], AF.Silu)
        for s in range(16):
            hi = hin[s % 2]; ho = hi
            pb = pC[s % 4]
            P.dma([(hi[:], st_ssd[s, g * 512:(g + 1) * 512, :].rearrange("(j p) n -> p j n", p=128))], "hin%d" % (s % 2))
            P.mm(pb[:, 0:256], ident_b[0:16, s:s + 1].to_broadcast([16, 128]), BCt[:])
            P.tt(t1s, pb[:, 0:128].unsqueeze(1).to_broadcast([128, 4, 128]), DTX[:, :, s:s + 1].to_broadcast([128, 4, 128]), ALU.mult)
            P.tt(hi[:], hi[:], EAc[:, :, s:s + 1].to_broadcast([128, 4, 128]), ALU.mult, eng="pool")
            P.tt(ho[:], hi[:], t1s, ALU.add, eng="pool")
            P.dma([(o_ssd_s[s, g * 512:(g + 1) * 512, :].rearrange("(j p) n -> p j n", p=128), ho[:])], "hin%d" % (s % 2))
            P.tt(t1s, ho[:], pb[:, 128:256].unsqueeze(1).to_broadcast([128, 4, 128]), ALU.mult)
            P.red(ycol[:, :, s], t1s)
        for j in range(4):
            P.stt(ycol[:, j, :], xT[:, j, 2048:2064], dcol[:, 4 * g + j:4 * g + j + 1], ycol[:, j, :], ALU.mult, ALU.add)
        P.tt(yz[:], ycol[:], zsT[:], ALU.mult)
        P.tt(ycol[:], yz[:], yz[:], ALU.mult)
        for j in range(4):
            P.mm(pS[:, 256:272], ones_f[:], ycol[:, j, :], start=(j == 0), stop=(j == 3))
        P.actf(rs16[:], pS[:, 256:272], AF.Ln, bias=EPS, scale=1.0 / 512)
        P.actf(rs16[:], rs16[:], AF.Exp, scale=-0.5)
        for j in range(4):
            P.stt(oT[:, 4 * g + j, 2048:2064], yz[:, j, :], ngcol[:, 4 * g + j:4 * g + j + 1], rs16[:], ALU.mult, ALU.mult)

    s_items = []
    for g in range(4):
        s_items += [(w1, 32 + g, None), (w1, 36 + g, None)]
        for j in range(4):
            s_items += [(w1, 16 + 4 * g + j, None), (w1, 4 * g + j, wz[:, :, j * 128:(j + 1) * 128])]
    wq1 = WQ(s_items)
    for g in range(4):
        P.dma([(sng[:], bcast(s_ng[0:1, g * 512:(g + 1) * 512]))], "c_sng")
        conv_tile(wq1, 16 + g, o_sconv_p, o_sconv_s); P.cp(BcT[:], Yc[:], eng="pool")
        conv_tile(wq1, 20 + g, o_sconv_p, o_sconv_s); P.cp(CcT[:], Yc[:], eng="pool")
        for j in range(4):
            conv_tile(wq1, 4 * g + j, o_sconv_p, o_sconv_s); P.cp(xT[:, j, :], Yc[:], eng="pool")
            wq1.get()
        interleave([ssd_scal(g, 0, 128, 0)])
        for gi, (c0, n) in enumerate(SEGS[:16]):
            nxt = ssd_scal(g, SEGS[gi + 1][0], 128, (gi + 1) % 2) if gi < 15 else None
            interleave([ssd_seg(g, gi, c0, n, gi == 0, gi == 15, None, o_ssd_p[g * 512:(g + 1) * 512, :], gi % 2), nxt])
        ssd_sample(g)
    P.pop()
    if stage <= 4:
        print('sbuf_remaining', nc.sbuf_bytes_remaining)
        print('counts', P.cnt, 'waits', P.nwaits, 'dma slots', len(P.dma_cnt))
        P.finish(); return nc, cn

    def l1_dst(ht, r0, n, i):
        if r0 < 2048:
            P.dma([(o_yp[r0:r0 + n, :], ht[0:n, :])], "yst%d" % (i % 2))
        else:
            P.dma([(o_ys[:, :], ht[0:n, :])], "yst%d" % (i % 2))

    out_proj(wo1, o_post_g, lambda r0, n: h1scr[r0:r0 + n, :], l1_dst, None)
    print('sbuf_remaining', nc.sbuf_bytes_remaining)
    print('counts', P.cnt, 'waits', P.nwaits, 'dma slots', len(P.dma_cnt))
    P.finish()
    return nc, cn

def _blocks(Wc):
    nb = Wc.shape[1] // 128
    return Wc.reshape(8, 128, nb, 128).transpose(2, 1, 0, 3).reshape(nb, 128, 1024)


def shared_inputs(inp):
    f = np.float32
    m = {}
    W = inp["e_w_in"][0]
    cols = list(range(0, 4096)) + list(range(4112, 7184))
    sm = np.zeros((1024, 128), f)
    sm[:, 0:16] = W[:, 4096:4112]
    sm[:, 16:32] = W[:, 7184:7200]
    m["w0"] = np.ascontiguousarray(np.concatenate([_blocks(W[:, cols]), _blocks(sm)], 0))
    W1 = inp["o_w_in"][0]
    sm1 = np.zeros((1024, 128), f)
    sm1[:, 0:32] = W1[:, 5120:5152]
    m["w1"] = np.ascontiguousarray(np.concatenate([_blocks(W1[:, 0:5120]), _blocks(sm1)], 0))
    m["wo0"] = np.ascontiguousarray(inp["e_w_out"][0].reshape(16, 128, 1024))
    m["wo1"] = np.ascontiguousarray(inp["o_w_out"][0].reshape(16, 128, 1024))
    for nme in ("e_pre_g", "e_post_g", "o_pre_g", "o_post_g"):
        m[nme] = np.ascontiguousarray(inp[nme].reshape(1, 1024))
    m["e_cw"] = np.ascontiguousarray(inp["e_conv_w"][0].T.reshape(24, 128, 4).transpose(1, 0, 2).reshape(128, 96))
    m["e_cb"] = np.ascontiguousarray(inp["e_conv_b"][0].reshape(24, 128).T)
    m["s_cw"] = np.ascontiguousarray(inp["ssd_conv_w"][0].T.reshape(24, 128, 4).transpose(1, 0, 2).reshape(128, 96))
    m["s_cb"] = np.ascontiguousarray(inp["ssd_conv_b"][0].reshape(24, 128).T)
    m["g_alog"] = np.ascontiguousarray(inp["gdn_a_log"].reshape(1, 8))
    m["g_dtb"] = np.ascontiguousarray(inp["gdn_dt_bias"].reshape(1, 8))
    m["g_ng"] = np.ascontiguousarray(inp["gdn_norm_g"].reshape(128, 1))
    m["l_wlr"] = np.ascontiguousarray(np.concatenate([inp["gla_w_lr"][0], inp["gla_b_lr"][0][None]], 0))
    m["l_ng"] = np.ascontiguousarray(inp["gla_norm_g"].reshape(2, 128).T)
    m["s_alog"] = np.ascontiguousarray(inp["ssd_a_log"].reshape(1, 32))
    m["s_dtb"] = np.ascontiguousarray(inp["ssd_dt_bias"].reshape(1, 32))
    m["s_d"] = np.ascontiguousarray(inp["ssd_d"].reshape(1, 32))
    m["s_ng"] = np.ascontiguousarray(inp["ssd_norm_g"].reshape(1, 2048))
    m["s_dcol"] = np.ascontiguousarray(np.repeat(inp["ssd_d"].reshape(32), 64).reshape(16, 128).T)
    m["s_ngcol"] = np.ascontiguousarray(inp["ssd_norm_g"].reshape(16, 128).T)
    for n_, a in consts_np().items():
        m["c_" + n_] = a
    return m


def make_in_maps(inp):
    sh = shared_inputs(inp)
    in_maps = []
    for c in range(8):
        m = dict(sh)
        sl = slice(c * 16, (c + 1) * 16)
        m["xp"] = np.ascontiguousarray(inp["x_prompt"][c])
        m["xs"] = np.ascontiguousarray(inp["x_sample"][sl, 0])
        m["st_gconv"] = np.ascontiguousarray(inp["state_gdn_conv"][0, sl].reshape(48, 3072))
        m["st_gdn"] = np.ascontiguousarray(inp["state_gdn"][0, sl])
        m["st_gla"] = np.ascontiguousarray(inp["state_gla"][0, sl])
        m["st_sconv"] = np.ascontiguousarray(inp["state_ssd_conv"][0, sl].reshape(48, 3072))
        m["st_ssd"] = np.ascontiguousarray(inp["state_ssd"][0, sl].reshape(16, 2048, 128))
        in_maps.append(m)
    return in_maps


def gather(R):
    f = np.float32
    st = lambda k_: np.stack([np.asarray(R[c][k_], f) for c in range(8)])
    cat = lambda k_: np.concatenate([np.asarray(R[c][k_], f) for c in range(8)])
    return (st("o_yp"), cat("o_ys").reshape(128, 1, 1024),
            st("o_gconv_p")[None], st("o_gdn_p")[None], st("o_gla_p")[None],
            st("o_sconv_p")[None], st("o_ssd_p").reshape(1, 8, 32, 64, 128),
            cat("o_gconv_s").reshape(1, 128, 3, 3072), cat("o_gdn_s")[None], cat("o_gla_s")[None],
            cat("o_sconv_s").reshape(1, 128, 3, 3072), cat("o_ssd_s").reshape(1, 128, 32, 64, 128))


def kernel(**inp):
    inp = {k_: np.asarray(v) for k_, v in inp.items()}
    nc, cn = build()
    res = run_bass_kernel_spmd(nc, make_in_maps(inp), core_ids=list(range(8)))
    return gather(res.results)
```

```python
import numpy as np
from contextlib import ExitStack
import concourse.bass as bass
import concourse.mybir as mybir
from concourse.bass_utils import run_bass_kernel_spmd

F32 = mybir.dt.float32
BF16 = mybir.dt.bfloat16
AF = mybir.ActivationFunctionType
ALU = mybir.AluOpType
AX = mybir.AxisListType
CENGS = ["pe", "act", "dve", "pool"]
ENGS = CENGS + ["sp"]
EPS = 1e-6
NT = 2064
GRPS = [(0, 512), (512, 512), (1024, 512), (1536, 512), (2048, 16)]
SEGS = [(128 * i, 128) for i in range(16)] + [(2048 + s, 1) for s in range(16)]
class Prog:
    def __init__(self, nc):
        self.nc = nc
        self.es = ExitStack()
        self.streams = {e: [] for e in ENGS}
        self.cnt = {e: 0 for e in CENGS}
        self.waited = {e: {} for e in ENGS}
        self.recs = {}
        self.fsz = {}
        self.dma_cnt = {}
        self.nwaits = 0
        self.psum_names = set()
        self.scopes = []

    def sb(self, name, shape, dt):
        st = self.scopes[-1][0] if self.scopes else self.es
        t = st.enter_context(self.nc.sbuf_tensor(name, list(shape), dt))
        self.fsz[name] = int(np.prod(shape[1:]))
        if self.scopes:
            self.scopes[-1][1].append(name)
        return t

    def push(self):
        self.scopes.append((ExitStack(), []))

    def pop(self):
        st, names = self.scopes.pop()
        for e in ENGS:
            waits = {}
            for o in CENGS:
                if o != e and self.cnt[o] > 0 and self.waited[e].get(("E", o), 0) < self.cnt[o]:
                    waits[("E", o)] = self.cnt[o]
            for slot, n in self.dma_cnt.items():
                if self.waited[e].get(("D", slot), 0) < 16 * n:
                    waits[("D", slot)] = 16 * n
            for k_, v in waits.items():
                self.waited[e][k_] = v
            if waits:
                self.streams[e].append((waits, None, None, 0))
        for nm in names:
            self.recs.pop(nm, None)
        st.close()

    def ps(self, name, shape, dt=F32):
        t = self.es.enter_context(self.nc.psum_tensor(name, list(shape), dt))
        self.fsz[name] = int(np.prod(shape[1:]))
        self.psum_names.add(name)
        return t

    def box(self, a):
        name = a.tensor.name
        ap = a.ap
        off = a.offset
        if name in self.fsz:
            F = self.fsz[name]
            p0 = off // F
            f0 = off % F
            ext = sum((c - 1) * abs(s) for s, c in ap[1:])
            return (name, p0, p0 + ap[0][1], f0, f0 + ext + 1)
        ext = sum((c - 1) * abs(s) for s, c in ap)
        return (name, 0, 1, off, off + ext + 1)

    def _deps(self, eng, reads, writes):
        waits = {}
        boxes = []
        for a in reads:
            b = self.box(a)
            if b[0] in self.psum_names:
                boxes.append(((b[0], 0, 128, 0, self.fsz[b[0]]), True))
            else:
                boxes.append((b, False))
        for a in writes:
            b = self.box(a)
            if b[0] in self.psum_names:
                b = (b[0], 0, 128, 0, self.fsz[b[0]])
            boxes.append((b, True))
        for (name, p0, p1, f0, f1), isw in boxes:
            for rec in self.recs.get(name, ()):
                rk, rv, risw, rp0, rp1, rf0, rf1, reng = rec
                if rp0 < p1 and p0 < rp1 and rf0 < f1 and f0 < rf1 and (isw or risw):
                    if reng == eng:
                        if eng == "pe":
                            continue
                        if not (risw and not isw):
                            continue
                    if waits.get(rk, 0) < rv:
                        waits[rk] = rv
        out = {}
        w = self.waited[eng]
        for k, v in waits.items():
            if w.get(k, 0) < v:
                w[k] = v
                out[k] = v
        self.nwaits += len(out)
        return out, boxes

    def _record(self, boxes, semkey, val, eng):
        for (name, p0, p1, f0, f1), isw in boxes:
            lst = self.recs.setdefault(name, [])
            if isw:
                lst[:] = [r for r in lst if not (p0 <= r[3] and r[4] <= p1 and f0 <= r[5] and r[6] <= f1)]
                lst.append((semkey, val, True, p0, p1, f0, f1, eng))
            else:
                for i, r in enumerate(lst):
                    if (not r[2]) and r[7] == eng and r[0] == semkey and r[3:7] == (p0, p1, f0, f1):
                        lst[i] = (semkey, val, False, p0, p1, f0, f1, eng)
                        break
                else:
                    lst.append((semkey, val, False, p0, p1, f0, f1, eng))

    def op(self, eng, fn, reads, writes):
        waits, boxes = self._deps(eng, reads, writes)
        self.cnt[eng] += 1
        val = self.cnt[eng]
        key = ("E", eng)
        self.streams[eng].append((waits, fn, key, 1))
        self._record(boxes, key, val, eng)

    def dma(self, pairs, slot, queue="sp"):
        key = ("D", slot)
        n0 = self.dma_cnt.get(slot, 0)
        final = 16 * (n0 + len(pairs))
        self.dma_cnt[slot] = n0 + len(pairs)
        for o, i in pairs:
            waits, boxes = self._deps(queue, [i], [o])
            self.streams[queue].append(
                (waits, (lambda e, o=o, i=i: e.dma_start(out=o, in_=i)), key, 16))
            self._record(boxes, key, final, "dma")

    def mm(self, out, lhsT, rhs, start=True, stop=True):
        self.op("pe", lambda e: e.matmul(out, lhsT, rhs, start=start, stop=stop), [lhsT, rhs], [out])

    def tr(self, out, in_, ident):
        self.op("pe", lambda e: e.transpose(out, in_, ident), [in_, ident], [out])

    def actf(self, out, in_, func, bias=None, scale=None, eng="act", accum_out=None):
        kw = {}
        rd = [in_]
        wr = [out]
        if bias is not None:
            kw["bias"] = bias
            if not isinstance(bias, (int, float)):
                rd.append(bias)
        if scale is not None:
            kw["scale"] = scale
            if not isinstance(scale, (int, float)):
                rd.append(scale)
        if accum_out is not None:
            kw["accum_out"] = accum_out
            wr.append(accum_out)
        self.op("act", lambda e: e.activation(out, in_, func, **kw), rd, wr)

    def tt(self, out, in0, in1, op, eng="dve"):
        self.op(eng, lambda e: e.tensor_tensor(out, in0, in1, op), [in0, in1], [out])

    def ts(self, out, in0, s1, op0, s2=None, op1=None, eng="dve", accum_out=None):
        rd = [in0] + [s for s in (s1, s2) if s is not None and not isinstance(s, (int, float))]
        wr = [out] + ([accum_out] if accum_out is not None else [])
        kw = {}
        if op1 is not None:
            kw["op1"] = op1
        if accum_out is not None:
            kw["accum_out"] = accum_out
        self.op(eng, lambda e: e.tensor_scalar(out, in0, s1, s2, op0, **kw), rd, wr)

    def stt(self, out, in0, scalar, in1, op0, op1, eng="dve", accum_out=None):
        rd = [in0, in1] + ([scalar] if not isinstance(scalar, (int, float)) else [])
        wr = [out] + ([accum_out] if accum_out is not None else [])
        kw = {}
        if accum_out is not None:
            kw["accum_out"] = accum_out
        self.op(eng, lambda e: e.scalar_tensor_tensor(out, in0, scalar, in1, op0, op1, **kw), rd, wr)

    def cp(self, out, in_, eng="dve"):
        if eng == "act":
            self.op("act", lambda e: e.copy(out, in_), [in_], [out])
        else:
            self.op(eng, lambda e: e.tensor_copy(out, in_), [in_], [out])

    def memset(self, out, val, eng="pool"):
        self.op(eng, lambda e: e.memset(out, val), [], [out])

    def red(self, out, in_, op=None, eng="dve"):
        self.op(eng, lambda e: e.tensor_reduce(out, in_, AX.X, op if op is not None else ALU.add), [in_], [out])

    def recip(self, out, in_):
        self.op("dve", lambda e: e.reciprocal(out, in_), [in_], [out])

    def finish(self):
        for slot, n in self.dma_cnt.items():
            k = ("D", slot)
            v = 16 * n
            if self.waited["sp"].get(k, 0) < v:
                self.waited["sp"][k] = v
                self.streams["sp"].append(({k: v}, None, None, 0))
        nc = self.nc
        keys = [("E", e) for e in CENGS] + [("D", s) for s in self.dma_cnt]
        sems = {}
        for i, k in enumerate(keys):
            sems[k] = self.es.enter_context(nc.semaphore("s%d" % i))
        emap = {"pe": "tensor", "act": "scalar", "dve": "vector", "pool": "gpsimd", "sp": "sync"}

        def replay(name, e):
            for waits, fn, key, inc in self.streams[name]:
                for k, v in waits.items():
                    e.wait_ge(sems[k], v)
                if fn is not None:
                    fn(e).then_inc(sems[key], inc)

        with nc.Block() as block:
            for name in ENGS:
                if not self.streams[name]:
                    continue
                getattr(block, emap[name])(lambda e, name=name: replay(name, e))
        self.es.close()


def consts_np():
    j = np.arange(128)
    c = {}
    c["ident_f"] = np.eye(128, dtype=np.float32)
    c["U_f"] = (j[:, None] <= j[None, :]).astype(np.float32)
    c["Un16_f"] = ((j[:, None] <= j[None, :]).astype(np.float32) / -16.0).astype(np.float32)
    c["negmask_f"] = np.where(j[:, None] <= j[None, :], 0.0, -30000.0).astype(np.float32)
    c["nsmask_f"] = np.where(j[:, None] < j[None, :], -1.0, 0.0).astype(np.float32)
    c["imask_f"] = (j[:, None] <= j[None, :]).astype(np.float32)
    c["ones_f"] = np.ones((128, 128), np.float32)
    c["e16_f"] = np.tile(np.eye(16, dtype=np.float32).reshape(1, 256), (128, 1))
    return c


class K:
    pass


def build(stage=99):
    nc = bass.Bass("TRN2", target_bir_lowering=False)
    P = Prog(nc)
    k = K()

    def din(name, shape):
        return nc.dram_tensor(name, list(shape), F32, kind="ExternalInput").ap()

    def dout(name, shape):
        return nc.dram_tensor(name, list(shape), F32, kind="ExternalOutput").ap()

    def bcast(ap1):
        return ap1.partition_broadcast(128).rearrange("p o f -> p (o f)")

    xp = din("xp", [2048, 1024]); xs = din("xs", [16, 1024])
    w0 = din("w0", [57, 128, 1024]); w1 = din("w1", [41, 128, 1024])
    wo0 = din("wo0", [16, 128, 1024]); wo1 = din("wo1", [16, 128, 1024])
    e_pre_g = din("e_pre_g", [1, 1024]); e_post_g = din("e_post_g", [1, 1024])
    o_pre_g = din("o_pre_g", [1, 1024]); o_post_g = din("o_post_g", [1, 1024])
    e_cw = din("e_cw", [128, 96]); e_cb = din("e_cb", [128, 24])
    s_cw = din("s_cw", [128, 96]); s_cb = din("s_cb", [128, 24])
    g_alog = din("g_alog", [1, 8]); g_dtb = din("g_dtb", [1, 8]); g_ng = din("g_ng", [128, 1])
    l_wlr = din("l_wlr", [17, 512]); l_ng = din("l_ng", [128, 2])
    s_alog = din("s_alog", [1, 32]); s_dtb = din("s_dtb", [1, 32]); s_d = din("s_d", [1, 32])
    s_ng = din("s_ng", [1, 2048])
    s_dcol = din("s_dcol", [128, 16]); s_ngcol = din("s_ngcol", [128, 16])
    st_gconv = din("st_gconv", [48, 3072]); st_gdn = din("st_gdn", [16, 8, 128, 128])
    st_gla = din("st_gla", [16, 4, 128, 256])
    st_sconv = din("st_sconv", [48, 3072]); st_ssd = din("st_ssd", [16, 32 * 64, 128])
    cn = consts_np()
    cd = {n: din("c_" + n, list(a.shape)) for n, a in cn.items()}
    h1scr = nc.dram_tensor("h1scr", [NT, 1024], F32, kind="Internal").ap()

    o_yp = dout("o_yp", [2048, 1024]); o_ys = dout("o_ys", [16, 1024])
    o_gconv_p = dout("o_gconv_p", [3, 3072]); o_gdn_p = dout("o_gdn_p", [8, 128, 128])
    o_gla_p = dout("o_gla_p", [4, 128, 256])
    o_sconv_p = dout("o_sconv_p", [3, 3072]); o_ssd_p = dout("o_ssd_p", [32 * 64, 128])
    o_gconv_s = dout("o_gconv_s", [48, 3072]); o_gdn_s = dout("o_gdn_s", [16, 8, 128, 128])
    o_gla_s = dout("o_gla_s", [16, 4, 128, 256])
    o_sconv_s = dout("o_sconv_s", [48, 3072]); o_ssd_s = dout("o_ssd_s", [16, 32 * 64, 128])

    ident_f = P.sb("ident_f", [128, 128], F32); U_f = P.sb("U_f", [128, 128], F32)
    Un16 = P.sb("Un16_f", [128, 128], F32)
    negmask = P.sb("negmask_f", [128, 128], F32); nsmask = P.sb("nsmask_f", [128, 128], F32)
    imask = P.sb("imask_f", [128, 128], F32)
    ones_f = P.sb("ones_f", [128, 128], F32)
    ident_b = P.sb("ident_b", [128, 128], BF16); ones_b = P.sb("ones_b", [128, 128], BF16)
    E16 = P.sb("e16_f", [128, 16, 16], F32)
    P.dma([(E16[:].rearrange("p a b -> p (a b)"), cd["e16_f"][:])], "c_e16")
    for t, n in [(ident_f, "ident_f"), (U_f, "U_f"), (Un16, "Un16_f"), (negmask, "negmask_f"), (nsmask, "nsmask_f"),
                 (imask, "imask_f"), (ones_f, "ones_f")]:
        P.dma([(t[:], cd[n][:])], "c_" + n)
    P.cp(ident_b[:], ident_f[:], eng="pool"); P.cp(ones_b[:], ones_f[:], eng="pool")

    cw = P.sb("cw", [128, 96], F32); cb = P.sb("cb", [128, 24], F32)
    P.dma([(cw[:], e_cw[:]), (cb[:], e_cb[:])], "c_cw")

    pA = [P.ps("pA0", [128, 512]), P.ps("pA1", [128, 512])]
    pT = P.ps("pT", [128, 1024], BF16)
    pS = P.ps("pS", [128, 512])
    pC = [P.ps("pC%d" % i, [128, 512]) for i in range(4)]

    uT = P.sb("uT", [128, 8, NT], BF16)
    oT = P.sb("oT", [128, 16, NT], BF16)
    ub = [P.sb("ub0", [128, 1024], BF16), P.sb("ub1", [128, 1024], BF16)]
    ss = P.sb("ss", [128, 4], F32)
    wst = [P.sb("wst0", [128, 1024], F32), P.sb("wst1", [128, 1024], F32)]
    wbf = [P.sb("wbf0", [128, 8, 128], BF16), P.sb("wbf1", [128, 8, 128], BF16)]
    k.ui = 0

    def rstd_of(s_, scale):
        P.actf(s_, s_, AF.Ln, bias=EPS, scale=scale)
        P.actf(s_, s_, AF.Exp, scale=-0.5)

    def norm_to_uT(x_, n, col0, gbc):
        i = k.ui % 2; k.ui += 1
        u_ = ub[i]
        s_ = ss[0:n, i:i + 1]
        P.memset(s_, 0.0, eng="dve")
        P.actf(u_[0:n, :], x_[0:n, :], AF.Square, accum_out=s_)
        rstd_of(s_, 1.0 / 1024)
        P.stt(u_[0:n, :], x_[0:n, :], s_, gbc[0:n, :], ALU.mult, ALU.mult)
        for kc in range(8):
            P.tr(pT[:, kc * 128:kc * 128 + n], u_[0:n, kc * 128:(kc + 1) * 128], ident_b[0:n, 0:n])
        P.cp(uT[:, :, col0:col0 + n], pT[:].rearrange("p (k t) -> p k t", k=8)[:, :, 0:n], eng="act")

    ROWT = [(xp[i * 128:(i + 1) * 128, :], 128, i * 128) for i in range(16)] + [(xs[:, :], 16, 2048)]
    P.push()
    xt = [P.sb("xt0", [128, 1024], F32), P.sb("xt1", [128, 1024], F32)]
    gbc0 = P.sb("gbc0", [128, 1024], F32)
    P.dma([(gbc0[:], bcast(e_pre_g[0:1, :]))], "c_gbc")
    for i, (src, n, col0) in enumerate(ROWT):
        P.dma([(xt[i % 2][0:n, :], src)], "xt%d" % (i % 2))
        norm_to_uT(xt[i % 2], n, col0, gbc0)
    P.pop()

    if stage <= -3:
        P.finish(); return nc, cn
    k.wi = 0

    def load_w(wsrc, blk, dst=None):
        i = k.wi % 2; k.wi += 1
        P.dma([(wst[i][:], wsrc[blk])], "wst%d" % i)
        if dst is None:
            dst = wbf[i][:]
        P.cp(dst, wst[i][:].rearrange("p (k c) -> p k c", k=8), eng="pool")
        return wbf[i]

    class WQ:
        def __init__(self, items):
            self.items = list(items); self.nd = 0; self.nc_ = 0; self.dq = []; self.cq = []

        def _dma(self):
            if self.nd < len(self.items):
                wsrc, blk, dst = self.items[self.nd]; self.nd += 1
                bi = k.wi % 2; k.wi += 1
                P.dma([(wst[bi][:], wsrc[blk])], "wst%d" % bi)
                self.dq.append((bi, dst))

        def _cast(self):
            if self.dq:
                bi, dst = self.dq.pop(0)
                d_ = wbf[bi][:] if dst is None else dst
                P.cp(d_, wst[bi][:].rearrange("p (k c) -> p k c", k=8), eng="pool")
                self.cq.append(wbf[bi])

        def get(self):
            if not self.cq:
                self._dma(); self._cast()
            cur = self.cq.pop(0)
            if not self.dq:
                self._dma()
            self._cast()
            self._dma()
            return cur

    k.pi = 0

    def proj_fm(wb, evac, M=128, c0=0):
        for (t0, n) in GRPS:
            pa = pA[k.pi % 2]; k.pi += 1
            for kc in range(8):
                P.mm(pa[0:M, 0:n], wb[:, kc, c0:c0 + M], uT[:, kc, t0:t0 + n], start=(kc == 0), stop=(kc == 7))
            evac(pa[0:M, 0:n], t0, n)

    def load_hist(hist_, src):
        for pc in range(3):
            hb = wst[k.wi % 2]
            P.dma([(hb[0:48, :], src[:, pc * 1024:(pc + 1) * 1024])], "wst%d" % (k.wi % 2))
            k.wi += 1
            for c8 in range(8):
                ct = pc * 8 + c8
                P.tr(pS[:, 256:304], hb[0:48, c8 * 128:(c8 + 1) * 128], ident_f[0:48, 0:48])
                P.cp(hist_[:, ct, :], pS[:, 256:304])

    hist = P.sb("hist", [128, 24, 48], F32)
    Xc = P.sb("Xc", [128, 3 + 2048], F32)
    Xs = P.sb("Xs", [128, 16, 4], F32)
    Yc = P.sb("Yc", [128, NT], F32)
    nst = P.sb("nst", [128, 51], F32)
    nso = [P.sb("nso0", [51, 128], F32), P.sb("nso1", [51, 128], F32)]
    k.ni = 0
    P.memset(Xc[:, 0:3], 0.0)

    def conv_tile(wq, ct, ocp, ocs):
        wb = wq.get()

        def ev(pa, t0, n):
            if t0 < 2048:
                P.cp(Xc[:, 3 + t0:3 + t0 + n], pa, eng="act")
            else:
                P.cp(Xs[:, :, 3], pa, eng="act")
        proj_fm(wb, ev)
        P.cp(Xs[:, :, 0:3], hist[:, ct, :].rearrange("p (s r) -> p s r", r=3), eng="pool")
        for (xin, yout) in [(lambda i: Xc[:, i:i + 2048], Yc[:, 0:2048]), (lambda i: Xs[:, :, i], Yc[:, 2048:2064])]:
            P.ts(yout, xin(0), cw[:, ct * 4:ct * 4 + 1], ALU.mult, cb[:, ct:ct + 1], ALU.add)
            for i in (1, 2, 3):
                P.stt(yout, xin(i), cw[:, ct * 4 + i:ct * 4 + i + 1], yout, ALU.mult, ALU.add)
        P.actf(Yc[:], Yc[:], AF.Silu)
        P.cp(nst[:, 0:3], Xc[:, 2048:2051], eng="pool")
        P.cp(nst[:, 3:51].rearrange("p (s r) -> p s r", r=3), Xs[:, :, 1:4], eng="pool")
        P.tr(pS[0:51, 384:512], nst[:], ident_f[:])
        no = nso[k.ni % 2]
        P.cp(no[:], pS[0:51, 384:512])
        P.dma([(ocp[:, ct * 128:(ct + 1) * 128], no[0:3, :]), (ocs[:, ct * 128:(ct + 1) * 128], no[3:51, :])], "nso%d" % (k.ni % 2))
        k.ni += 1

    load_hist(hist, st_gconv)

    P.push()
    lrT = P.sb("lrT", [17, NT], BF16)
    P.push()
    wsm = P.sb("wsm", [128, 8, 128], BF16)
    load_w(w0, 56, wsm[:])
    alog_bc = P.sb("alog_bc", [128, 8], F32); dtb_bc = P.sb("dtb_bc", [128, 8], F32)
    negA = P.sb("negA", [128, 8], F32); ng_col = P.sb("ng_col", [128, 1], F32)
    P.dma([(ng_col[:], g_ng[:]), (alog_bc[:], bcast(g_alog[0:1, :])), (dtb_bc[:], bcast(g_dtb[0:1, :]))], "c_misc")
    P.actf(negA[:], alog_bc[:], AF.Exp)
    P.ts(negA[:], negA[:], -1.0, ALU.mult)
    SC = P.sb("SC", [128, 16, 64], F32)
    SCs = P.sb("SCs", [16, 32], F32)
    GL = P.sb("GL", [128, 16, 16], F32)
    tmp8 = P.sb("tmp8", [128, 16], F32)
    for kc in range(8):
        P.mm(pS[0:16, 0:32], uT[:, kc, 2048:2064], wsm[:, kc, 0:32], start=(kc == 0), stop=(kc == 7))
    P.actf(SCs[:, 0:8], pS[0:16, 0:8], AF.Sigmoid)
    P.tt(tmp8[0:16, 0:8], pS[0:16, 8:16], dtb_bc[0:16, :], ALU.add)
    P.actf(tmp8[0:16, 0:8], tmp8[0:16, 0:8], AF.Exp)
    P.actf(tmp8[0:16, 0:8], tmp8[0:16, 0:8], AF.Ln, bias=1.0)
    P.tt(SCs[:, 8:16], tmp8[0:16, 0:8], negA[0:16, :], ALU.mult)
    P.actf(SCs[:, 16:24], SCs[:, 8:16], AF.Exp)
    P.ts(SCs[:, 24:32], SCs[:, 16:24], -1.0, ALU.mult)
    for g, (c0, n) in enumerate(SEGS[:16]):
        ps = pS[0:n, 0:32]
        for kc in range(8):
            P.mm(ps, uT[:, kc, c0:c0 + n], wsm[:, kc, 0:32], start=(kc == 0), stop=(kc == 7))
        sc = SC[0:n, g, :]
        P.actf(sc[:, 0:8], ps[:, 0:8], AF.Sigmoid)
        P.tt(tmp8[0:n, 0:8], ps[:, 8:16], dtb_bc[0:n, :], ALU.add)
        P.actf(tmp8[0:n, 0:8], tmp8[0:n, 0:8], AF.Exp)
        P.actf(tmp8[0:n, 0:8], tmp8[0:n, 0:8], AF.Ln, bias=1.0)
        P.tt(sc[:, 8:16], tmp8[0:n, 0:8], negA[0:n, :], ALU.mult)
        P.mm(pS[0:n, 32:40], U_f[0:n, 0:n], sc[:, 8:16])
        P.mm(pS[:, 40:48], ones_f[0:n, :], sc[:, 8:16])
        P.cp(sc[:, 16:24], pS[0:n, 32:40])
        P.ts(sc[:, 24:32], pS[0:n, 32:40], -1.0, ALU.mult)
        P.actf(sc[:, 32:40], pS[0:n, 32:40], AF.Exp)
        P.ts(sc[:, 40:48], sc[:, 32:40], -1.0, ALU.mult)
        P.cp(GL[:, g, 8:16], pS[:, 40:48])
        P.actf(GL[:, g, 0:8], pS[:, 40:48], AF.Exp)
        P.tt(tmp8[0:n, 8:16], GL[0:n, g, 8:16], sc[:, 16:24], ALU.subtract)
        P.actf(sc[:, 48:56], tmp8[0:n, 8:16], AF.Exp)
    P.memset(lrT[:], 1.0)
    for (t0, n) in GRPS:
        for kc in range(8):
            P.mm(pS[0:16, 0:n], wsm[:, kc, 16:32], uT[:, kc, t0:t0 + n], start=(kc == 0), stop=(kc == 7))
        P.cp(lrT[0:16, t0:t0 + n], pS[0:16, 0:n], eng="act")

    sq = P.sb("sq", [128, NT], BF16)
    rn = P.sb("rn", [128, 512], F32)

    def l2norm_to(dsts, scale):
        P.tt(sq[:], Yc[:], Yc[:], ALU.mult, eng="pool")
        for (t0, n) in GRPS:
            P.mm(pS[:, 0:n], ones_b[:], sq[:, t0:t0 + n])
            P.actf(rn[:, 0:n], pS[:, 0:n], AF.Ln, bias=EPS)
            P.actf(rn[:, 0:n], rn[:, 0:n], AF.Exp, scale=-0.5)
            for d_ in dsts:
                P.stt(d_[:, t0:t0 + n], Yc[:, t0:t0 + n], scale, rn[:, 0:n], ALU.mult, ALU.mult)

    qT = P.sb("qT", [128, NT], BF16); vT = P.sb("vT", [128, NT], BF16)
    kT32 = P.sb("kT32", [128, NT], F32)
    gsT = P.sb("gsT", [128, NT], BF16)
    S = [P.sb("S0", [128, 128], F32), P.sb("S1", [128, 128], F32)]
    Sb = [P.sb("Sb0", [128, 128], BF16), P.sb("Sb1", [128, 128], BF16)]
    vtok = [P.sb("vtk_%d" % i, [128, 128], BF16) for i in range(3)]; kdtok = [P.sb("kdtk_%d" % i, [128, 128], F32) for i in range(3)]
    decTs = [P.sb("decT%d" % i, [128, 128], F32) for i in range(2)]; pTm = [P.sb("pTmm_%d" % i, [128, 128], F32) for i in range(3)]
    Mis = [P.sb("Mi%d" % i, [128, 128], F32) for i in range(2)]; kcbs = [P.sb("kcb%d" % i, [128, 128], BF16) for i in range(2)]
    Abs = [[P.sb("Ab%d_%d" % (t, i), [128, 128], F32) for i in range(2)] for t in range(2)]
    ATbs = [[P.sb("ATb%d_%d" % (t, i), [128, 128], F32) for i in range(2)] for t in range(2)]
    P32 = [P.sb("P32_%d" % i, [128, 128], F32) for i in range(3)]
    R32 = P.sb("R32", [128, 128], F32); u32 = P.sb("u32", [128, 128], F32)
    o2s = P.sb("o2s", [128, 128], F32); osb = P.sb("osb", [128, 128], F32); onb = P.sb("onb", [128, 128], BF16)
    oss = P.sb("oss", [128, 1], F32)
    k.si = 0

    def interleave(gens):
        gens = [g_ for g_ in gens if g_ is not None]
        while gens:
            for g_ in list(gens):
                try:
                    next(g_)
                except StopIteration:
                    gens.remove(g_)

    def gdn_prep(h, g, c0, n, pb, tb):
        sc = SC[0:n, g, :]
        Ab = Abs[tb]; ATb = ATbs[tb]; decT = decTs[tb]; Mi = Mis[tb]; kcb = kcbs[tb]
        if tb == 0:
            r_kk = pC[0][:, 0:128]; r_qk = pC[0][:, 128:256]; r_G = pC[0][:, 256:384]
            r_s0 = pC[1][:, 0:128]; r_s1 = pC[1][:, 128:256]; r_P = pA[0][:, 0:128]
            r_kt = pA[1][:, 0:128]; r_vt = pT[:, 0:128]
        else:
            r_kk = pC[1][:, 256:384]; r_qk = pC[1][:, 384:512]; r_G = pA[0][:, 128:256]
            r_s0 = pA[1][:, 128:256]; r_s1 = pA[1][:, 256:384]; r_P = pA[0][:, 256:384]
            r_kt = pA[1][:, 384:512]; r_vt = pT[:, 128:256]
        qc_ = qT[:, c0:c0 + n]; k32 = kT32[:, c0:c0 + n]
        kc_ = kcb[:, 0:n]
        P.cp(kc_, k32, eng="pool")
        P.tr(r_vt[0:n, :], vT[:, c0:c0 + n], ident_b[:])
        P.cp(vtok[pb][0:n, :], r_vt[0:n, :], eng="act")
        P.tr(r_kt[0:n, :], k32, ident_f[:])
        P.actf(kdtok[pb][0:n, :], r_kt[0:n, :], AF.Copy, scale=sc[:, 48 + h:49 + h])
        yield
        P.mm(r_qk[0:n, 0:n], kc_, qc_)
        P.mm(r_G[0:n, 0:n], sc[:, 8 + h:9 + h].to_broadcast([n, n]), U_f[0:n, 0:n], start=True, stop=False)
        P.mm(r_G[0:n, 0:n], ident_f[0:n, 0:n], negmask[0:n, 0:n], start=False, stop=True)
        if n > 1:
            P.mm(r_kk, kc_, kc_)
        yield
        P.actf(decT[0:n, 0:n], r_G[0:n, 0:n], AF.Exp, bias=sc[:, 24 + h:25 + h])
        P.tt(pTm[pb][0:n, 0:n], r_qk[0:n, 0:n], decT[0:n, 0:n], ALU.mult)
        yield
        if n > 1:
            Pf = P32[pb]
            P.stt(Mi[:], r_kk, sc[:, h:h + 1], decT[:], ALU.mult, ALU.mult)
            P.tt(Ab[0][:], Mi[:], nsmask[:], ALU.mult, eng="pool")
            yield
            P.tr(r_s0, Ab[0][:], ident_f[:])
            P.cp(ATb[0][:], r_s0, eng="act")
            P.tt(Pf[:], Ab[0][:], ident_f[:], ALU.add, eng="pool")
            yield
            cur = 0
            for it in range(6):
                nx = 1 - cur
                P.mm(r_s0, Ab[cur][:], ATb[cur][:])
                if it < 5:
                    P.mm(r_s1, ATb[cur][:], Ab[cur][:])
                yield
                P.cp(ATb[nx][:], r_s0, eng="act")
                if it < 5:
                    P.cp(Ab[nx][:], r_s1, eng="dve")
                yield
                P.mm(r_P, ATb[nx][:], Pf[:])
                yield
                P.tt(Pf[:], r_P, Pf[:], ALU.add)
                yield
                cur = nx

    def gdn_seq(h, g, c0, n, first, last, s_src, s_dst, pb):
        sc = SC[0:n, g, :]
        if first:
            k.si += 1
        Sf = S[k.si % 2]; Sbb = Sb[k.si % 2]
        if first:
            if s_src is None:
                P.memset(Sf[:], 0.0)
            else:
                P.dma([(Sf[:], s_src)], "Sld%d" % (k.si % 2))
            P.cp(Sbb[:], Sf[:], eng="pool")
            yield
        qc_ = qT[:, c0:c0 + n]; k32 = kT32[:, c0:c0 + n]
        Pfin = P32[pb] if n > 1 else ident_f
        P.mm(pC[3][0:n, 0:128], k32, Sf[:])
        yield
        P.stt(R32[0:n, :], pC[3][0:n, 0:128], sc[:, 40 + h:41 + h], vtok[pb][0:n, :], ALU.mult, ALU.add)
        yield
        P.mm(pC[2][0:n, 0:128], Pfin[0:n, 0:n], R32[0:n, :])
        yield
        P.actf(u32[0:n, :], pC[2][0:n, 0:128], AF.Copy, scale=sc[:, h:h + 1])
        yield
        P.mm(pC[3][:, 128:256], kdtok[pb][0:n, :], u32[0:n, :])
        P.mm(pC[2][0:n, 256:384], qc_, Sbb[:])
        P.mm(pC[2][0:n, 384:512], pTm[pb][0:n, 0:n], u32[0:n, :])
        yield
        P.stt(Sf[:], Sf[:], GL[:, g, h:h + 1], pC[3][:, 128:256], ALU.mult, ALU.add)
        yield
        if not last:
            P.cp(Sbb[:], Sf[:], eng="pool")
        else:
            P.dma([(s_dst, Sf[:])], "Sst%d" % (k.si % 2))
        P.cp(o2s[0:n, :], pC[2][0:n, 384:512], eng="act")
        yield
        P.stt(osb[0:n, :], pC[2][0:n, 256:384], sc[:, 32 + h:33 + h], o2s[0:n, :], ALU.mult, ALU.add)
        P.memset(oss[0:n, :], 0.0, eng="dve")
        yield
        P.actf(onb[0:n, :], osb[0:n, :], AF.Square, accum_out=oss[0:n, :])
        rstd_of(oss[0:n, :], 1.0 / 128)
        P.actf(osb[0:n, :], osb[0:n, :], AF.Copy, scale=oss[0:n, :])
        yield
        P.tr(pS[:, 0:n], osb[0:n, :], ident_f[0:n, 0:n])
        yield
        P.tt(oT[:, h, c0:c0 + n], pS[:, 0:n], gsT[:, c0:c0 + n], ALU.mult)
        yield

    def gdn_head_segs(h):
        segl = []
        for g, (c0, n) in enumerate(SEGS[:16]):
            segl.append((g, c0, n, g == 0, g == 15, None, o_gdn_p[h]))
        preps = [gdn_prep(h, sg[0], sg[1], sg[2], i % 3, i % 2) for i, sg in enumerate(segl)]
        done = [False] * len(segl)

        def step(i):
            if i < len(segl) and not done[i]:
                try:
                    next(preps[i])
                except StopIteration:
                    done[i] = True

        while not done[0]:
            step(0)
        for i, (g, c0, n, fi, la, src, dst) in enumerate(segl):
            sq_ = gdn_seq(h, g, c0, n, fi, la, src, dst, i % 3)
            sq_done = False
            while not (sq_done and (i + 1 >= len(segl) or done[i + 1])):
                if not sq_done:
                    try:
                        next(sq_)
                    except StopIteration:
                        sq_done = True
                step(i + 1)
                step(i + 2)

    Sall = P.sb("Sall", [128, 16, 128], F32)
    Km = P.sb("Km", [128, 16, 16], F32); Qm = P.sb("Qm", [128, 16, 16], F32); q32c = P.sb("q32c", [128, 16], F32)
    vts = P.sb("vts", [16, 128], BF16); Rs = P.sb("Rs", [16, 128], F32); us = P.sb("us", [16, 128], F32)
    dg = P.sb("dg", [16, 16], F32); egbc = P.sb("egbc", [128, 16, 1], F32)
    osbs = P.sb("osbs", [16, 128], F32); onbs = P.sb("onbs", [16, 128], F32); osss = P.sb("osss", [16, 1], F32)

    def gdn_sample(h):
        kc32 = kT32[:, 2048:2064]
        P.cp(q32c[:], qT[:, 2048:2064], eng="pool")
        P.tt(Km[:], kc32.unsqueeze(2).to_broadcast([128, 16, 16]), E16[:], ALU.mult, eng="pool")
        P.tt(Qm[:], q32c[:].unsqueeze(2).to_broadcast([128, 16, 16]), E16[:], ALU.mult, eng="pool")
        P.tr(pT[0:16, 0:128], vT[:, 2048:2064], ident_b[:])
        P.cp(vts[:], pT[0:16, 0:128], eng="act")
        P.ts(dg[:], ident_f[0:16, 0:16], SCs[:, 16 + h:17 + h], ALU.mult)
        P.mm(pC[1][:, 0:16], ones_f[0:16, :], dg[:])
        P.cp(egbc[:].rearrange("p s o -> p (s o)"), pC[1][:, 0:16], eng="act")
        for s in range(16):
            P.mm(pC[0][0:16, 0:128], Km[:, s, :], Sall[:, s, :], start=(s == 0), stop=(s == 15))
        P.stt(Rs[:], pC[0][0:16, 0:128], SCs[:, 24 + h:25 + h], vts[:], ALU.mult, ALU.add)
        P.ts(us[:], Rs[:], SCs[:, h:h + 1], ALU.mult)
        P.tt(Sall[:], Sall[:], egbc[:].to_broadcast([128, 16, 128]), ALU.mult)
        for s in range(16):
            bk = pC[2] if (s // 4) % 2 == 0 else pC[3]
            r0 = (s % 4) * 128
            P.mm(bk[:, r0:r0 + 128], ident_f[0:16, s:s + 1].to_broadcast([16, 128]), us[:])
            P.stt(Sall[:, s, :], bk[:, r0:r0 + 128], kc32[:, s:s + 1], Sall[:, s, :], ALU.mult, ALU.add)
        for s in range(16):
            P.mm(pC[0][0:16, 128:256], Qm[:, s, :], Sall[:, s, :], start=(s == 0), stop=(s == 15))
        P.dma([(o_gdn_s[:, h].rearrange("s d e -> d s e"), Sall[:])], "Sall")
        P.cp(osbs[:], pC[0][0:16, 128:256], eng="act")
        P.memset(osss[:], 0.0, eng="dve")
        P.actf(onbs[:], osbs[:], AF.Square, accum_out=osss[:])
        rstd_of(osss[:], 1.0 / 128)
        P.actf(onbs[:], osbs[:], AF.Copy, scale=osss[:])
        P.tr(pS[:, 0:16], onbs[:], ident_f[0:16, 0:16])
        P.tt(oT[:, h, 2048:2064], pS[:, 0:16], gsT[:, 2048:2064], ALU.mult)

    wq0 = WQ([(w0, b_, None) for h in range(8) for b_ in (h, 8 + h, 16 + h, 24 + h)])
    for h in range(8 if stage >= 1 else 0):
        conv_tile(wq0, h, o_gconv_p, o_gconv_s); l2norm_to([qT], 128 ** -0.5)
        conv_tile(wq0, 8 + h, o_gconv_p, o_gconv_s); l2norm_to([kT32], 1.0)
        conv_tile(wq0, 16 + h, o_gconv_p, o_gconv_s); P.cp(vT[:], Yc[:], eng="pool")
        wb = wq0.get()
        proj_fm(wb, lambda pa, t0, n: P.actf(Yc[:, t0:t0 + n], pa, AF.Silu))
        P.ts(gsT[:], Yc[:], ng_col[:], ALU.mult)
        P.dma([(Sall[:], st_gdn[:, h].rearrange("s d e -> d s e"))], "Sall")
        gdn_head_segs(h)
        gdn_sample(h)
    P.pop()
    if stage <= 1:
        P.pop()
        print('sbuf_remaining', nc.sbuf_bytes_remaining)
        print('counts', P.cnt, 'waits', P.nwaits, 'dma slots', len(P.dma_cnt))
        P.finish(); return nc, cn

    P.push()
    wlr = P.sb("wlr", [17, 512], BF16)
    P.dma([(wst[k.wi % 2][0:17, 0:512], l_wlr[:])], "wst%d" % (k.wi % 2))
    P.cp(wlr[:], wst[k.wi % 2][0:17, 0:512], eng="pool")
    k.wi += 1
    lng = P.sb("lng", [128, 2], F32)
    P.dma([(lng[:], l_ng[:])], "c_misc")
    lqT = P.sb("lqT", [128, NT], BF16); lkT = P.sb("lkT", [128, NT], BF16)
    gs2 = P.sb("gs2", [128, 2, NT], BF16)
    wv = P.sb("wv", [128, 8, 256], BF16)
    S2 = [P.sb("S2a", [128, 256], F32), P.sb("S2b", [128, 256], F32)]
    S2h = [P.sb("S2ha", [128, 256], BF16), P.sb("S2hb", [128, 256], BF16)]
    sp32 = P.sb("sp32", [128, 128], F32)
    ebt = P.sb("ebt", [128, 128], F32); enbt = P.sb("enbt", [128, 128], F32); ekdt = P.sb("ekdt", [128, 128], F32)
    bl = P.sb("bl", [128, 2], F32)
    qeb = P.sb("qeb", [128, 128], BF16); keb = P.sb("keb", [128, 128], BF16); kdTb = P.sb("kdTb", [128, 128], BF16)
    kdtok2 = P.sb("kdtok2", [128, 128], BF16); pTm2 = P.sb("pTm2", [128, 128], BF16)
    vtok2 = P.sb("vtok2", [128, 256], BF16)
    osb2 = P.sb("osb2", [128, 256], F32); onb2 = P.sb("onb2", [128, 256], BF16); oss2 = P.sb("oss2", [128, 1], F32)
    k.s2 = 0

    def gla_seg(h, g, c0, n, first, last, s_src, s_dst):
        if first:
            k.s2 += 1
        Sf = S2[k.s2 % 2]; Sbb = S2h[k.s2 % 2]
        if first:
            if s_src is None:
                P.memset(Sf[:], 0.0)
            else:
                P.dma([(Sf[:], s_src)], "S2ld%d" % (k.s2 % 2))
            P.cp(Sbb[:], Sf[:], eng="pool")
        P.mm(pS[0:n, 0:128], lrT[0:17, c0:c0 + n], wlr[0:17, h * 128:(h + 1) * 128])
        P.actf(sp32[0:n, :], pS[0:n, 0:128], AF.Exp, scale=-1.0)
        P.actf(sp32[0:n, :], sp32[0:n, :], AF.Ln, bias=1.0)
        P.mm(pC[0][:, 0:n], sp32[0:n, :], Un16[0:n, 0:n])
        P.actf(ebt[:, 0:n], pC[0][:, 0:n], AF.Exp)
        P.actf(enbt[:, 0:n], pC[0][:, 0:n], AF.Exp, scale=-1.0)
        P.cp(bl[:, 0:1], pC[0][:, n - 1:n])
        P.actf(ekdt[:, 0:n], pC[0][:, 0:n], AF.Exp, scale=-1.0, bias=bl[:, 0:1])
        P.actf(bl[:, 1:2], bl[:, 0:1], AF.Exp)
        P.stt(qeb[:, 0:n], lqT[:, c0:c0 + n], 128 ** -0.5, ebt[:, 0:n], ALU.mult, ALU.mult)
        P.tt(keb[:, 0:n], lkT[:, c0:c0 + n], enbt[:, 0:n], ALU.mult)
        P.tt(kdTb[:, 0:n], lkT[:, c0:c0 + n], ekdt[:, 0:n], ALU.mult, eng="pool")
        P.tr(pT[0:n, 0:128], kdTb[:, 0:n], ident_b[:])
        P.cp(kdtok2[0:n, :], pT[0:n, 0:128], eng="act")
        P.mm(pC[1][0:n, 0:n], keb[:, 0:n], qeb[:, 0:n])
        P.tt(pTm2[0:n, 0:n], pC[1][0:n, 0:n], imask[0:n, 0:n], ALU.mult)
        for kc in range(8):
            P.mm(pA[0][0:n, 0:256], uT[:, kc, c0:c0 + n], wv[:, kc, :], start=(kc == 0), stop=(kc == 7))
        P.cp(vtok2[0:n, :], pA[0][0:n, 0:256], eng="act")
        P.mm(pC[2][0:n, 0:256], qeb[:, 0:n], Sbb[:], start=True, stop=False)
        P.mm(pC[2][0:n, 0:256], pTm2[0:n, 0:n], vtok2[0:n, :], start=False, stop=True)
        P.mm(pC[3][:, 0:256], kdtok2[0:n, :], vtok2[0:n, :])
        P.stt(Sf[:], Sf[:], bl[:, 1:2], pC[3][:, 0:256], ALU.mult, ALU.add)
        if not last:
            P.cp(Sbb[:], Sf[:], eng="pool")
        else:
            P.dma([(s_dst, Sf[:])], "S2st%d" % (k.s2 % 2))
        P.cp(osb2[0:n, :], pC[2][0:n, 0:256], eng="act")
        P.memset(oss2[0:n, :], 0.0, eng="dve")
        P.actf(onb2[0:n, :], osb2[0:n, :], AF.Square, accum_out=oss2[0:n, :])
        rstd_of(oss2[0:n, :], 1.0 / 256)
        P.actf(onb2[0:n, :], osb2[0:n, :], AF.Copy, scale=oss2[0:n, :])
        for j in range(2):
            P.tr(pT[:, 256 + j * 128:256 + j * 128 + n], onb2[0:n, j * 128:(j + 1) * 128], ident_b[0:n, 0:n])
        for j in range(2):
            P.tt(oT[:, 8 + 2 * h + j, c0:c0 + n], pT[:, 256 + j * 128:256 + j * 128 + n], gs2[:, j, c0:c0 + n], ALU.mult)

    Sg = P.sb("Sg", [128, 8, 256], F32)
    Qm2 = P.sb("Qm2", [128, 16, 16], F32); q32g = P.sb("q32g", [128, 16], F32); k32g = P.sb("k32g", [128, 16], F32)
    aTs = P.sb("aTs", [128, 16, 1], F32); sps = P.sb("sps", [16, 128], F32)
    vts2 = P.sb("vts2", [16, 256], BF16)
    osg = P.sb("osg", [16, 256], F32); ong = P.sb("ong", [16, 256], F32); ossg = P.sb("ossg", [16, 1], F32)

    def gla_sample(h):
        P.mm(pS[0:16, 0:128], lrT[0:17, 2048:2064], wlr[0:17, h * 128:(h + 1) * 128])
        P.actf(sps[:], pS[0:16, 0:128], AF.Exp, scale=-1.0)
        P.actf(sps[:], sps[:], AF.Ln, bias=1.0)
        P.actf(sps[:], sps[:], AF.Exp, scale=-1.0 / 16)
        P.tr(pS[:, 128:144], sps[:], ident_f[0:16, 0:16])
        P.cp(aTs[:].rearrange("p s o -> p (s o)"), pS[:, 128:144], eng="act")
        P.cp(k32g[:], lkT[:, 2048:2064], eng="pool")
        P.ts(q32g[:], lqT[:, 2048:2064], 128 ** -0.5, ALU.mult)
        P.tt(Qm2[:], q32g[:].unsqueeze(2).to_broadcast([128, 16, 16]), E16[:], ALU.mult, eng="pool")
        for kc in range(8):
            P.mm(pA[0][0:16, 0:256], uT[:, kc, 2048:2064], wv[:, kc, :], start=(kc == 0), stop=(kc == 7))
        P.cp(vts2[:], pA[0][0:16, 0:256], eng="act")
        for hf in range(2):
            P.dma([(Sg[:], st_gla[hf * 8:(hf + 1) * 8, h].rearrange("s d e -> d s e"))], "Sg")
            for s8 in range(8):
                s = hf * 8 + s8
                bk = pC[2] if s % 2 == 0 else pC[3]
                P.mm(bk[:, 0:256], ident_b[0:16, s:s + 1].to_broadcast([16, 128]), vts2[:])
                P.tt(Sg[:, s8, :], Sg[:, s8, :], aTs[:, s, :].to_broadcast([128, 256]), ALU.mult, eng="pool")
                P.stt(Sg[:, s8, :], bk[:, 0:256], k32g[:, s:s + 1], Sg[:, s8, :], ALU.mult, ALU.add)
            for s8 in range(8):
                s = hf * 8 + s8
                P.mm(pC[0][0:16, 0:256], Qm2[:, s, :], Sg[:, s8, :], start=(s == 0), stop=(s == 15), )
            P.dma([(o_gla_s[hf * 8:(hf + 1) * 8, h].rearrange("s d e -> d s e"), Sg[:])], "Sg")
        P.cp(osg[:], pC[0][0:16, 0:256], eng="act")
        P.memset(ossg[:], 0.0, eng="dve")
        P.actf(ong[:], osg[:], AF.Square, accum_out=ossg[:])
        rstd_of(ossg[:], 1.0 / 256)
        P.actf(ong[:], osg[:], AF.Copy, scale=ossg[:])
        for j in range(2):
            P.tr(pS[:, 256 + j * 16:256 + (j + 1) * 16], ong[:, j * 128:(j + 1) * 128], ident_f[0:16, 0:16])
        for j in range(2):
            P.tt(oT[:, 8 + 2 * h + j, 2048:2064], pS[:, 256 + j * 16:256 + (j + 1) * 16], gs2[:, j, 2048:2064], ALU.mult)

    gl_items = []
    for h in range(4):
        gl_items += [(w0, 32 + h, None), (w0, 36 + h, None)]
        for j in range(2):
            gl_items += [(w0, 48 + 2 * h + j, None), (w0, 40 + 2 * h + j, wv[:, :, j * 128:(j + 1) * 128])]
    wqg = WQ(gl_items)
    for h in range(4):
        proj_fm(wqg.get(), lambda pa, t0, n: P.cp(lqT[:, t0:t0 + n], pa, eng="act"))
        proj_fm(wqg.get(), lambda pa, t0, n: P.cp(lkT[:, t0:t0 + n], pa, eng="act"))
        for j in range(2):
            proj_fm(wqg.get(), lambda pa, t0, n: P.actf(Yc[:, t0:t0 + n], pa, AF.Silu))
            P.ts(gs2[:, j, :], Yc[:], lng[:, j:j + 1], ALU.mult)
            wqg.get()
        for g, (c0, n) in enumerate(SEGS[:16]):
            gla_seg(h, g, c0, n, g == 0, g == 15, None, o_gla_p[h])
        gla_sample(h)
    P.pop()
    P.pop()
    if stage <= 2:
        print('sbuf_remaining', nc.sbuf_bytes_remaining)
        print('counts', P.cnt, 'waits', P.nwaits, 'dma slots', len(P.dma_cnt))
        P.finish(); return nc, cn

    def out_proj(wo_src, post_g_src, res_src, dst_fn, next_pre_g):
        P.push()
        tag = "L%d" % k.ui
        wo = P.sb("wo" + tag, [128, 16, 1024], BF16)
        gpo = P.sb("gpo" + tag, [128, 1024], F32)
        h1t = [P.sb("h1a" + tag, [128, 1024], F32), P.sb("h1b" + tag, [128, 1024], F32)]
        xt = [P.sb("xta" + tag, [128, 1024], F32), P.sb("xtb" + tag, [128, 1024], F32)]
        ss2 = P.sb("ss2" + tag, [128, 4], F32)
        junk2 = [P.sb("jka" + tag, [128, 512], BF16), P.sb("jkb" + tag, [128, 512], BF16)]
        P.dma([(gpo[:], bcast(post_g_src[0:1, :]))], "c_gpo")
        if next_pre_g is not None:
            gbc = P.sb("gbc" + tag, [128, 1024], F32)
            P.dma([(gbc[:], bcast(next_pre_g[0:1, :]))], "c_gbc")
        for f in range(16):
            i = k.wi % 2; k.wi += 1
            P.dma([(wst[i][:], wo_src[f])], "wst%d" % i)
            P.cp(wo[:, f, :], wst[i][:], eng="pool")
        rows = [(i * 128, 128, i * 128) for i in range(16)] + [(2048, 16, 2048)]
        for i, (r0, n, col0) in enumerate(rows):
            ht = h1t[i % 2]; xr = xt[i % 2]
            P.dma([(xr[0:n, :], res_src(r0, n))], "xr%d" % (i % 2))
            ssx = ss2[0:n, 2 * (i % 2):2 * (i % 2) + 2]
            pb2 = (pA[0], pA[1]) if i % 2 == 0 else (pC[0], pC[1])
            P.memset(ssx, 0.0, eng="dve")
            for hf in range(2):
                for f in range(16):
                    P.mm(pb2[hf][0:n, :], oT[:, f, col0:col0 + n], wo[:, f, hf * 512:(hf + 1) * 512], start=(f == 0), stop=(f == 15))
                P.actf(junk2[i % 2][0:n, :], pb2[hf][0:n, :], AF.Square, accum_out=ssx[:, hf:hf + 1])
            P.tt(ssx[:, 0:1], ssx[:, 0:1], ssx[:, 1:2], ALU.add)
            rstd_of(ssx[:, 0:1], 1.0 / 1024)
            for hf in range(2):
                sl = slice(hf * 512, (hf + 1) * 512)
                P.stt(ht[0:n, sl], pb2[hf][0:n, :], ssx[:, 0:1], gpo[0:n, sl], ALU.mult, ALU.mult)
            P.tt(ht[0:n, :], ht[0:n, :], xr[0:n, :], ALU.add, eng="pool")
            dst_fn(ht, r0, n, i)
            if next_pre_g is not None:
                norm_to_uT(ht, n, col0, gbc)
        P.pop()

    def l0_dst(ht, r0, n, i):
        P.dma([(h1scr[r0:r0 + n, :], ht[0:n, :])], "h1st%d" % (i % 2))

    out_proj(wo0, e_post_g, lambda r0, n: (xp[r0:r0 + n, :] if r0 < 2048 else xs[:, :]), l0_dst, o_pre_g)
    if stage <= 3:
        print('sbuf_remaining', nc.sbuf_bytes_remaining)
        print('counts', P.cnt, 'waits', P.nwaits, 'dma slots', len(P.dma_cnt))
        P.finish(); return nc, cn

    P.dma([(cw[:], s_cw[:]), (cb[:], s_cb[:])], "c_cw")
    load_hist(hist, st_sconv)
    P.push()
    wsm1 = P.sb("wsm1", [128, 8, 128], BF16)
    load_w(w1, 40, wsm1[:])
    sAl = P.sb("sAl", [128, 32], F32); sdtb = P.sb("sdtb", [128, 32], F32); negA1 = P.sb("negA1", [128, 32], F32)
    sD = P.sb("sD", [128, 32, 1], F32)
    sng = P.sb("sng", [128, 512], F32)
    P.dma([(sAl[:], bcast(s_alog[0:1, :])), (sdtb[:], bcast(s_dtb[0:1, :])),
           (sD[:].rearrange("p h o -> p (h o)"), bcast(s_d[0:1, :]))], "c_misc")
    P.actf(negA1[:], sAl[:], AF.Exp)
    P.ts(negA1[:], negA1[:], -1.0, ALU.mult)
    BcT = P.sb("BcT", [128, NT], BF16); CcT = P.sb("CcT", [128, NT], BF16)
    xT = P.sb("xT", [128, 4, NT], BF16)
    wz = P.sb("wz", [128, 8, 512], BF16)
    hT = [P.sb("hTa", [128, 8, 64], F32), P.sb("hTb", [128, 8, 64], F32)]
    hTh = [P.sb("hTha", [128, 512], BF16), P.sb("hThb", [128, 512], BF16)]
    hin = [P.sb("hin0", [128, 4, 128], F32), P.sb("hin1", [128, 4, 128], F32)]
    hld = hin[0]
    sc1s = [P.sb("sc1_%d" % i, [128, 48, 1], F32) for i in range(2)]
    lls = [P.sb("ll_%d" % i, [128, 16, 1], F32) for i in range(2)]
    t8s = [P.sb("t8_%d" % i, [128, 8], F32) for i in range(2)]
    xdt = P.sb("xdt", [128, 8, 64], BF16); xdec = P.sb("xdec", [128, 8, 64], BF16); xtk = P.sb("xtk", [128, 8, 64], BF16)
    Btok = P.sb("Btok", [128, 128], BF16)
    cbT = P.sb("cbT", [128, 128], F32)
    dec1 = [P.sb("dec1_%d" % i, [128, 128], BF16) for i in range(8)]
    mT = [P.sb("mT_%d" % i, [128, 128], BF16) for i in range(8)]
    yis = P.sb("yis", [128, 8, 64], F32); yv = P.sb("yv", [128, 8, 64], F32); zs = P.sb("zs", [128, 512], F32)
    ynb = P.sb("ynb", [128, 512], BF16); oss3 = P.sb("oss3", [128, 1], F32)
    k.s3 = 0

    def bc3(ap_n81, n):
        return ap_n81.to_broadcast([n, 8, 64])

    def ssd_scal(g, c0, n, pb):
        h0 = 8 * g
        sc = sc1s[pb][0:n]; ll = lls[pb]; t8 = t8s[pb]
        for kc in range(8):
            P.mm(pC[0][0:n, 256:264], uT[:, kc, c0:c0 + n], wsm1[:, kc, h0:h0 + 8], start=(kc == 0), stop=(kc == 7))
        yield
        P.tt(t8[0:n, :], pC[0][0:n, 256:264], sdtb[0:n, h0:h0 + 8], ALU.add)
        yield
        P.actf(t8[0:n, :], t8[0:n, :], AF.Exp)
        P.actf(sc[:, 0:8, 0], t8[0:n, :], AF.Ln, bias=1.0)
        yield
        P.tt(sc[:, 8:16, 0], sc[:, 0:8, 0], negA1[0:n, h0:h0 + 8], ALU.mult)
        yield
        P.mm(pC[0][0:n, 288:296], U_f[0:n, 0:n], sc[:, 8:16, 0])
        P.mm(pC[0][:, 296:304], ones_f[0:n, :], sc[:, 8:16, 0])
        yield
        P.cp(sc[:, 16:24, 0], pC[0][0:n, 288:296])
        P.ts(sc[:, 24:32, 0], pC[0][0:n, 288:296], -1.0, ALU.mult)
        P.cp(ll[:, 0:8, 0], pC[0][:, 296:304])
        yield
        P.actf(sc[:, 32:40, 0], pC[0][0:n, 288:296], AF.Exp)
        P.actf(ll[:, 8:16, 0], pC[0][:, 296:304], AF.Exp)
        yield
        P.tt(t8[0:n, :], ll[0:n, 0:8, 0], sc[:, 16:24, 0], ALU.subtract)
        yield
        P.actf(sc[:, 40:48, 0], t8[0:n, :], AF.Exp)
        yield

    def ssd_seg(g, gi, c0, n, first, last, s_src, s_dst, pb):
        if first:
            k.s3 += 1
        hf = hT[k.s3 % 2]; hb = hTh[k.s3 % 2]
        hf2 = hf[:].rearrange("p h q -> p (h q)")
        if first:
            if s_src is None:
                P.memset(hf[:], 0.0)
            else:
                P.dma([(hld[:], s_src.rearrange("(j p) n -> p j n", p=128))], "hld")
                for j in range(4):
                    P.tr(pC[0][:, j * 128:(j + 1) * 128], hld[:, j, :], ident_f[:])
                P.cp(hf2, pC[0][:, :], eng="act")
            P.cp(hb[:], hf2, eng="pool")
        h0 = 8 * g
        sc = sc1s[pb][0:n]; ll = lls[pb]
        yield
        for j in range(4):
            P.tr(pT[0:n, j * 128:(j + 1) * 128], xT[:, j, c0:c0 + n], ident_b[:])
        P.tr(pT[0:n, 512:640], BcT[:, c0:c0 + n], ident_b[:])
        ptx = pT[0:n, 0:512].rearrange("p (h q) -> p h q", h=8)
        P.cp(xtk[0:n], ptx, eng="act")
        P.cp(Btok[0:n, :], pT[0:n, 512:640], eng="act")
        yield
        P.tt(xdt[0:n], xtk[0:n], bc3(sc[:, 0:8, :], n), ALU.mult)
        P.tt(xdec[0:n], xdt[0:n], bc3(sc[:, 40:48, :], n), ALU.mult, eng="pool")
        yield
        P.mm(pC[0][0:n, 0:n], BcT[:, c0:c0 + n], CcT[:, c0:c0 + n])
        P.cp(cbT[0:n, 0:n], pC[0][0:n, 0:n], eng="act")
        P.mm(pC[3][0:n, :], CcT[:, c0:c0 + n], hb[:])
        P.cp(yis[0:n].rearrange("p h q -> p (h q)"), pC[3][0:n, :], eng="act")
        for kc in range(8):
            P.mm(pA[0][0:n, :], uT[:, kc, c0:c0 + n], wz[:, kc, :], start=(kc == 0), stop=(kc == 7))
        P.actf(zs[0:n, :], pA[0][0:n, :], AF.Silu)
        yield
        P.mm(pA[1][:, :], Btok[0:n, :], xdec[0:n].rearrange("p h q -> p (h q)"))
        P.tt(hf[:], hf[:], ll[:, 8:16, :].to_broadcast([128, 8, 64]), ALU.mult)
        P.tt(hf2, hf2, pA[1][:, :], ALU.add)
        if not last:
            P.cp(hb[:], hf2, eng="pool")
        else:
            for j in range(4):
                P.tr(pC[0][:, j * 128:(j + 1) * 128], hf2[:, j * 128:(j + 1) * 128], ident_f[:])
            P.cp(hld[:].rearrange("p j n -> p (j n)"), pC[0][:, :], eng="act")
            P.dma([(s_dst.rearrange("(j p) n -> p j n", p=128), hld[:])], "hst")
        yield
        if n == 1:
            P.cp(mT[0][0:1, 0:1], cbT[0:1, 0:1], eng="pool")
            P.mm(pC[2][0:1, :], mT[0][0:1, 0:1], xdt[0:1].rearrange("p h q -> p (h q)"))
        else:
            for hh in range(8):
                gb = pC[1] if hh < 4 else pS
                r0 = (hh % 4) * 128
                P.mm(gb[:, r0:r0 + 128], sc[:, 8 + hh, :].to_broadcast([128, 128]), U_f[:], start=True, stop=False)
                P.mm(gb[:, r0:r0 + 128], ident_f[:], negmask[:], start=False, stop=True)
            for hh in range(8):
                gb = pC[1] if hh < 4 else pS
                r0 = (hh % 4) * 128
                P.actf(dec1[hh][:], gb[:, r0:r0 + 128], AF.Exp, bias=sc[:, 24 + hh, :])
            for hh in range(8):
                P.tt(mT[hh][:], cbT[:], dec1[hh][:], ALU.mult, eng=("pool" if hh % 2 else "dve"))
            for hh in range(8):
                P.mm(pC[2][:, hh * 64:(hh + 1) * 64], mT[hh][:], xdt[:, hh, :])
        yield
        P.tt(yis[0:n], yis[0:n], bc3(sc[:, 32:40, :], n), ALU.mult)
        P.tt(yv[0:n].rearrange("p h q -> p (h q)"), yis[0:n].rearrange("p h q -> p (h q)"), pC[2][0:n, :], ALU.add)
        P.tt(yis[0:n], xtk[0:n], sD[0:n, h0:h0 + 8, :].to_broadcast([n, 8, 64]), ALU.mult, eng="pool")
        P.tt(yv[0:n], yv[0:n], yis[0:n], ALU.add)
        yv2 = yv[0:n].rearrange("p h q -> p (h q)")
        P.tt(yv2, yv2, zs[0:n, :], ALU.mult)
        P.memset(oss3[0:n, :], 0.0, eng="dve")
        P.actf(ynb[0:n, :], yv2, AF.Square, accum_out=oss3[0:n, :])
        rstd_of(oss3[0:n, :], 1.0 / 512)
        P.stt(ynb[0:n, :], yv2, oss3[0:n, :], sng[0:n, :], ALU.mult, ALU.mult)
        for j in range(4):
            P.tr(pT[:, j * 128:j * 128 + n], ynb[0:n, j * 128:(j + 1) * 128], ident_b[0:n, 0:n])
        P.cp(oT[:, 4 * g:4 * g + 4, c0:c0 + n], pT[:, 0:512].rearrange("p (j t) -> p j t", j=4)[:, :, 0:n], eng="act")

    dcol = P.sb("dcol", [128, 16], F32); ngcol = P.sb("ngcol", [128, 16], F32)
    P.dma([(dcol[:], s_dcol[:]), (ngcol[:], s_ngcol[:])], "c_cols")
    sdt = P.sb("sdt", [16, 32, 1], F32); sea = P.sb("sea", [16, 32, 1], F32)
    for kc in range(8):
        P.mm(pS[0:16, 0:32], uT[:, kc, 2048:2064], wsm1[:, kc, 0:32], start=(kc == 0), stop=(kc == 7))
    sdt2 = sdt[:].rearrange("p h o -> p (h o)"); sea2 = sea[:].rearrange("p h o -> p (h o)")
    P.tt(sdt2, pS[0:16, 0:32], sdtb[0:16, :], ALU.add)
    P.actf(sdt2, sdt2, AF.Exp)
    P.actf(sdt2, sdt2, AF.Ln, bias=1.0)
    P.tt(sea2, sdt2, negA1[0:16, :], ALU.mult)
    P.actf(sea2, sea2, AF.Exp)
    exE = P.sb("exE", [16, 8, 64], F32)
    DTc = P.sb("DTc", [128, 4, 16], F32); EAc = P.sb("EAc", [128, 4, 16], F32); DTX = P.sb("DTX", [128, 4, 16], F32)
    BCt = P.sb("BCt", [16, 256], BF16)
    t1s = yis[:].rearrange("p h q -> p (h q)").rearrange("p (j n) -> p j n", j=4)
    ycol = P.sb("ycol", [128, 4, 16], F32); yz = P.sb("yz", [128, 4, 16], F32); zsT = P.sb("zsT", [128, 4, 16], F32)
    rs16 = P.sb("rs16", [128, 16], F32)

    def ssd_sample(g):
        h0 = 8 * g
        for w, src_ in enumerate((sdt, sea)):
            P.cp(exE[:], src_[:, h0:h0 + 8, :].to_broadcast([16, 8, 64]), eng="pool")
            ex2 = exE[:].rearrange("p h q -> p (h q)")
            for j in range(4):
                P.tr(pS[:, (w * 4 + j) * 16:(w * 4 + j + 1) * 16], ex2[:, j * 128:(j + 1) * 128], ident_f[0:16, 0:16])
        P.cp(DTc[:].rearrange("p j s -> p (j s)"), pS[:, 0:64], eng="act")
        P.cp(EAc[:].rearrange("p j s -> p (j s)"), pS[:, 64:128], eng="act")
        P.tt(DTX[:], xT[:, :, 2048:2064], DTc[:], ALU.mult)
        P.tr(pT[0:16, 0:128], BcT[:, 2048:2064], ident_b[:])
        P.tr(pT[0:16, 128:256], CcT[:, 2048:2064], ident_b[:])
        P.cp(BCt[:], pT[0:16, 0:256], eng="act")
        for j in range(4):
            for kc in range(8):
                P.mm(pA[0][:, j * 16:(j + 1) * 16], wz[:, kc, j * 128:(j + 1) * 128], uT[:, kc, 2048:2064], start=(kc == 0), stop=(kc == 7))
        P.actf(zsT[:].rearrange("p j s -> p (j s)"), pA[0][:, 0:64], AF.Silu)
        for s in range(16):
            hi = hin[s % 2]; ho = hi
            pb = pC[s % 4]
            P.dma([(hi[:], st_ssd[s, g * 512:(g + 1) * 512, :].rearrange("(j p) n -> p j n", p=128))], "hin%d" % (s % 2))
            P.mm(pb[:, 0:256], ident_b[0:16, s:s + 1].to_broadcast([16, 128]), BCt[:])
            P.tt(t1s, pb[:, 0:128].unsqueeze(1).to_broadcast([128, 4, 128]), DTX[:, :, s:s + 1].to_broadcast([128, 4, 128]), ALU.mult)
            P.tt(hi[:], hi[:], EAc[:, :, s:s + 1].to_broadcast([128, 4, 128]), ALU.mult, eng="pool")
            P.tt(ho[:], hi[:], t1s, ALU.add, eng="pool")
            P.dma([(o_ssd_s[s, g * 512:(g + 1) * 512, :].rearrange("(j p) n -> p j n", p=128), ho[:])], "hin%d" % (s % 2))
            P.tt(t1s, ho[:], pb[:, 128:256].unsqueeze(1).to_broadcast([128, 4, 128]), ALU.mult)
            P.red(ycol[:, :, s], t1s)
        for j in range(4):
            P.stt(ycol[:, j, :], xT[:, j, 2048:2064], dcol[:, 4 * g + j:4 * g + j + 1], ycol[:, j, :], ALU.mult, ALU.add)
        P.tt(yz[:], ycol[:], zsT[:], ALU.mult)
        P.tt(ycol[:], yz[:], yz[:], ALU.mult)
        for j in range(4):
            P.mm(pS[:, 256:272], ones_f[:], ycol[:, j, :], start=(j == 0), stop=(j == 3))
        P.actf(rs16[:], pS[:, 256:272], AF.Ln, bias=EPS, scale=1.0 / 512)
        P.actf(rs16[:], rs16[:], AF.Exp, scale=-0.5)
        for j in range(4):
            P.stt(oT[:, 4 * g + j, 2048:2064], yz[:, j, :], ngcol[:, 4 * g + j:4 * g + j + 1], rs16[:], ALU.mult, ALU.mult)

    s_items = []
    for g in range(4):
        s_items += [(w1, 32 + g, None), (w1, 36 + g, None)]
        for j in range(4):
            s_items += [(w1, 16 + 4 * g + j, None), (w1, 4 * g + j, wz[:, :, j * 128:(j + 1) * 128])]
    wq1 = WQ(s_items)
    for g in range(4):
        P.dma([(sng[:], bcast(s_ng[0:1, g * 512:(g + 1) * 512]))], "c_sng")
        conv_tile(wq1, 16 + g, o_sconv_p, o_sconv_s); P.cp(BcT[:], Yc[:], eng="pool")
        conv_tile(wq1, 20 + g, o_sconv_p, o_sconv_s); P.cp(CcT[:], Yc[:], eng="pool")
        for j in range(4):
            conv_tile(wq1, 4 * g + j, o_sconv_p, o_sconv_s); P.cp(xT[:, j, :], Yc[:], eng="pool")
            wq1.get()
        interleave([ssd_scal(g, 0, 128, 0)])
        for gi, (c0, n) in enumerate(SEGS[:16]):
            nxt = ssd_scal(g, SEGS[gi + 1][0], 128, (gi + 1) % 2) if gi < 15 else None
            interleave([ssd_seg(g, gi, c0, n, gi == 0, gi == 15, None, o_ssd_p[g * 512:(g + 1) * 512, :], gi % 2), nxt])
        ssd_sample(g)
    P.pop()
    if stage <= 4:
        print('sbuf_remaining', nc.sbuf_bytes_remaining)
        print('counts', P.cnt, 'waits', P.nwaits, 'dma slots', len(P.dma_cnt))
        P.finish(); return nc, cn

    def l1_dst(ht, r0, n, i):
        if r0 < 2048:
            P.dma([(o_yp[r0:r0 + n, :], ht[0:n, :])], "yst%d" % (i % 2))
        else:
            P.dma([(o_ys[:, :], ht[0:n, :])], "yst%d" % (i % 2))

    out_proj(wo1, o_post_g, lambda r0, n: h1scr[r0:r0 + n, :], l1_dst, None)
    print('sbuf_remaining', nc.sbuf_bytes_remaining)
    print('counts', P.cnt, 'waits', P.nwaits, 'dma slots', len(P.dma_cnt))
    P.finish()
    return nc, cn

def _blocks(Wc):
    nb = Wc.shape[1] // 128
    return Wc.reshape(8, 128, nb, 128).transpose(2, 1, 0, 3).reshape(nb, 128, 1024)


def shared_inputs(inp):
    f = np.float32
    m = {}
    W = inp["e_w_in"][0]
    cols = list(range(0, 4096)) + list(range(4112, 7184))
    sm = np.zeros((1024, 128), f)
    sm[:, 0:16] = W[:, 4096:4112]
    sm[:, 16:32] = W[:, 7184:7200]
    m["w0"] = np.ascontiguousarray(np.concatenate([_blocks(W[:, cols]), _blocks(sm)], 0))
    W1 = inp["o_w_in"][0]
    sm1 = np.zeros((1024, 128), f)
    sm1[:, 0:32] = W1[:, 5120:5152]
    m["w1"] = np.ascontiguousarray(np.concatenate([_blocks(W1[:, 0:5120]), _blocks(sm1)], 0))
    m["wo0"] = np.ascontiguousarray(inp["e_w_out"][0].reshape(16, 128, 1024))
    m["wo1"] = np.ascontiguousarray(inp["o_w_out"][0].reshape(16, 128, 1024))
    for nme in ("e_pre_g", "e_post_g", "o_pre_g", "o_post_g"):
        m[nme] = np.ascontiguousarray(inp[nme].reshape(1, 1024))
    m["e_cw"] = np.ascontiguousarray(inp["e_conv_w"][0].T.reshape(24, 128, 4).transpose(1, 0, 2).reshape(128, 96))
    m["e_cb"] = np.ascontiguousarray(inp["e_conv_b"][0].reshape(24, 128).T)
    m["s_cw"] = np.ascontiguousarray(inp["ssd_conv_w"][0].T.reshape(24, 128, 4).transpose(1, 0, 2).reshape(128, 96))
    m["s_cb"] = np.ascontiguousarray(inp["ssd_conv_b"][0].reshape(24, 128).T)
    m["g_alog"] = np.ascontiguousarray(inp["gdn_a_log"].reshape(1, 8))
    m["g_dtb"] = np.ascontiguousarray(inp["gdn_dt_bias"].reshape(1, 8))
    m["g_ng"] = np.ascontiguousarray(inp["gdn_norm_g"].reshape(128, 1))
    m["l_wlr"] = np.ascontiguousarray(np.concatenate([inp["gla_w_lr"][0], inp["gla_b_lr"][0][None]], 0))
    m["l_ng"] = np.ascontiguousarray(inp["gla_norm_g"].reshape(2, 128).T)
    m["s_alog"] = np.ascontiguousarray(inp["ssd_a_log"].reshape(1, 32))
    m["s_dtb"] = np.ascontiguousarray(inp["ssd_dt_bias"].reshape(1, 32))
    m["s_d"] = np.ascontiguousarray(inp["ssd_d"].reshape(1, 32))
    m["s_ng"] = np.ascontiguousarray(inp["ssd_norm_g"].reshape(1, 2048))
    m["s_dcol"] = np.ascontiguousarray(np.repeat(inp["ssd_d"].reshape(32), 64).reshape(16, 128).T)
    m["s_ngcol"] = np.ascontiguousarray(inp["ssd_norm_g"].reshape(16, 128).T)
    for n_, a in consts_np().items():
        m["c_" + n_] = a
    return m


def make_in_maps(inp):
    sh = shared_inputs(inp)
    in_maps = []
    for c in range(8):
        m = dict(sh)
        sl = slice(c * 16, (c + 1) * 16)
        m["xp"] = np.ascontiguousarray(inp["x_prompt"][c])
        m["xs"] = np.ascontiguousarray(inp["x_sample"][sl, 0])
        m["st_gconv"] = np.ascontiguousarray(inp["state_gdn_conv"][0, sl].reshape(48, 3072))
        m["st_gdn"] = np.ascontiguousarray(inp["state_gdn"][0, sl])
        m["st_gla"] = np.ascontiguousarray(inp["state_gla"][0, sl])
        m["st_sconv"] = np.ascontiguousarray(inp["state_ssd_conv"][0, sl].reshape(48, 3072))
        m["st_ssd"] = np.ascontiguousarray(inp["state_ssd"][0, sl].reshape(16, 2048, 128))
        in_maps.append(m)
    return in_maps


def gather(R):
    f = np.float32
    st = lambda k_: np.stack([np.asarray(R[c][k_], f) for c in range(8)])
    cat = lambda k_: np.concatenate([np.asarray(R[c][k_], f) for c in range(8)])
    return (st("o_yp"), cat("o_ys").reshape(128, 1, 1024),
            st("o_gconv_p")[None], st("o_gdn_p")[None], st("o_gla_p")[None],
            st("o_sconv_p")[None], st("o_ssd_p").reshape(1, 8, 32, 64, 128),
            cat("o_gconv_s").reshape(1, 128, 3, 3072), cat("o_gdn_s")[None], cat("o_gla_s")[None],
            cat("o_sconv_s").reshape(1, 128, 3, 3072), cat("o_ssd_s").reshape(1, 128, 32, 64, 128))


def kernel(**inp):
    inp = {k_: np.asarray(v) for k_, v in inp.items()}
    nc, cn = build()
    res = run_bass_kernel_spmd(nc, make_in_maps(inp), core_ids=list(range(8)))
    return gather(res.results)
```

```python
import numpy as np
from contextlib import ExitStack
import concourse.bass as bass
import concourse.mybir as mybir
from concourse.bass_utils import run_bass_kernel_spmd

F32 = mybir.dt.float32
BF16 = mybir.dt.bfloat16
AF = mybir.ActivationFunctionType
ALU = mybir.AluOpType
AX = mybir.AxisListType
CENGS = ["pe", "act", "dve", "pool"]
ENGS = CENGS + ["sp"]
EPS = 1e-6
NT = 2064
GRPS = [(0, 512), (512, 512), (1024, 512), (1536, 512), (2048, 16)]
SEGS = [(128 * i, 128) for i in range(16)] + [(2048 + s, 1) for s in range(16)]
class Prog:
    def __init__(self, nc):
        self.nc = nc
        self.es = ExitStack()
        self.streams = {e: [] for e in ENGS}
        self.cnt = {e: 0 for e in CENGS}
        self.waited = {e: {} for e in ENGS}
        self.recs = {}
        self.fsz = {}
        self.dma_cnt = {}
        self.nwaits = 0
        self.psum_names = set()
        self.scopes = []

    def sb(self, name, shape, dt):
        st = self.scopes[-1][0] if self.scopes else self.es
        t = st.enter_context(self.nc.sbuf_tensor(name, list(shape), dt))
        self.fsz[name] = int(np.prod(shape[1:]))
        if self.scopes:
            self.scopes[-1][1].append(name)
        return t

    def push(self):
        self.scopes.append((ExitStack(), []))

    def pop(self):
        st, names = self.scopes.pop()
        for e in ENGS:
            waits = {}
            for o in CENGS:
                if o != e and self.cnt[o] > 0 and self.waited[e].get(("E", o), 0) < self.cnt[o]:
                    waits[("E", o)] = self.cnt[o]
            for slot, n in self.dma_cnt.items():
                if self.waited[e].get(("D", slot), 0) < 16 * n:
                    waits[("D", slot)] = 16 * n
            for k_, v in waits.items():
                self.waited[e][k_] = v
            if waits:
                self.streams[e].append((waits, None, None, 0))
        for nm in names:
            self.recs.pop(nm, None)
        st.close()

    def ps(self, name, shape, dt=F32):
        t = self.es.enter_context(self.nc.psum_tensor(name, list(shape), dt))
        self.fsz[name] = int(np.prod(shape[1:]))
        self.psum_names.add(name)
        return t

    def box(self, a):
        name = a.tensor.name
        ap = a.ap
        off = a.offset
        if name in self.fsz:
            F = self.fsz[name]
            p0 = off // F
            f0 = off % F
            ext = sum((c - 1) * abs(s) for s, c in ap[1:])
            return (name, p0, p0 + ap[0][1], f0, f0 + ext + 1)
        ext = sum((c - 1) * abs(s) for s, c in ap)
        return (name, 0, 1, off, off + ext + 1)

    def _deps(self, eng, reads, writes):
        waits = {}
        boxes = []
        for a in reads:
            b = self.box(a)
            if b[0] in self.psum_names:
                boxes.append(((b[0], 0, 128, 0, self.fsz[b[0]]), True))
            else:
                boxes.append((b, False))
        for a in writes:
            b = self.box(a)
            if b[0] in self.psum_names:
                b = (b[0], 0, 128, 0, self.fsz[b[0]])
            boxes.append((b, True))
        for (name, p0, p1, f0, f1), isw in boxes:
            for rec in self.recs.get(name, ()):
                rk, rv, risw, rp0, rp1, rf0, rf1, reng = rec
                if rp0 < p1 and p0 < rp1 and rf0 < f1 and f0 < rf1 and (isw or risw):
                    if reng == eng:
                        if eng == "pe":
                            continue
                        if not (risw and not isw):
                            continue
                    if waits.get(rk, 0) < rv:
                        waits[rk] = rv
        out = {}
        w = self.waited[eng]
        for k, v in waits.items():
            if w.get(k, 0) < v:
                w[k] = v
                out[k] = v
        self.nwaits += len(out)
        return out, boxes

    def _record(self, boxes, semkey, val, eng):
        for (name, p0, p1, f0, f1), isw in boxes:
            lst = self.recs.setdefault(name, [])
            if isw:
                lst[:] = [r for r in lst if not (p0 <= r[3] and r[4] <= p1 and f0 <= r[5] and r[6] <= f1)]
                lst.append((semkey, val, True, p0, p1, f0, f1, eng))
            else:
                for i, r in enumerate(lst):
                    if (not r[2]) and r[7] == eng and r[0] == semkey and r[3:7] == (p0, p1, f0, f1):
                        lst[i] = (semkey, val, False, p0, p1, f0, f1, eng)
                        break
                else:
                    lst.append((semkey, val, False, p0, p1, f0, f1, eng))

    def op(self, eng, fn, reads, writes):
        waits, boxes = self._deps(eng, reads, writes)
        self.cnt[eng] += 1
        val = self.cnt[eng]
        key = ("E", eng)
        self.streams[eng].append((waits, fn, key, 1))
        self._record(boxes, key, val, eng)

    def dma(self, pairs, slot, queue="sp"):
        key = ("D", slot)
        n0 = self.dma_cnt.get(slot, 0)
        final = 16 * (n0 + len(pairs))
        self.dma_cnt[slot] = n0 + len(pairs)
        for o, i in pairs:
            waits, boxes = self._deps(queue, [i], [o])
            self.streams[queue].append(
                (waits, (lambda e, o=o, i=i: e.dma_start(out=o, in_=i)), key, 16))
            self._record(boxes, key, final, "dma")

    def mm(self, out, lhsT, rhs, start=True, stop=True):
        self.op("pe", lambda e: e.matmul(out, lhsT, rhs, start=start, stop=stop), [lhsT, rhs], [out])

    def tr(self, out, in_, ident):
        self.op("pe", lambda e: e.transpose(out, in_, ident), [in_, ident], [out])

    def actf(self, out, in_, func, bias=None, scale=None, eng="act", accum_out=None):
        kw = {}
        rd = [in_]
        wr = [out]
        if bias is not None:
            kw["bias"] = bias
            if not isinstance(bias, (int, float)):
                rd.append(bias)
        if scale is not None:
            kw["scale"] = scale
            if not isinstance(scale, (int, float)):
                rd.append(scale)
        if accum_out is not None:
            kw["accum_out"] = accum_out
            wr.append(accum_out)
        self.op("act", lambda e: e.activation(out, in_, func, **kw), rd, wr)

    def tt(self, out, in0, in1, op, eng="dve"):
        self.op(eng, lambda e: e.tensor_tensor(out, in0, in1, op), [in0, in1], [out])

    def ts(self, out, in0, s1, op0, s2=None, op1=None, eng="dve", accum_out=None):
        rd = [in0] + [s for s in (s1, s2) if s is not None and not isinstance(s, (int, float))]
        wr = [out] + ([accum_out] if accum_out is not None else [])
        kw = {}
        if op1 is not None:
            kw["op1"] = op1
        if accum_out is not None:
            kw["accum_out"] = accum_out
        self.op(eng, lambda e: e.tensor_scalar(out, in0, s1, s2, op0, **kw), rd, wr)

    def stt(self, out, in0, scalar, in1, op0, op1, eng="dve", accum_out=None):
        rd = [in0, in1] + ([scalar] if not isinstance(scalar, (int, float)) else [])
        wr = [out] + ([accum_out] if accum_out is not None else [])
        kw = {}
        if accum_out is not None:
            kw["accum_out"] = accum_out
        self.op(eng, lambda e: e.scalar_tensor_tensor(out, in0, scalar, in1, op0, op1, **kw), rd, wr)

    def cp(self, out, in_, eng="dve"):
        if eng == "act":
            self.op("act", lambda e: e.copy(out, in_), [in_], [out])
        else:
            self.op(eng, lambda e: e.tensor_copy(out, in_), [in_], [out])

    def memset(self, out, val, eng="pool"):
        self.op(eng, lambda e: e.memset(out, val), [], [out])

    def red(self, out, in_, op=None, eng="dve"):
        self.op(eng, lambda e: e.tensor_reduce(out, in_, AX.X, op if op is not None else ALU.add), [in_], [out])

    def recip(self, out, in_):
        self.op("dve", lambda e: e.reciprocal(out, in_), [in_], [out])

    def finish(self):
        for slot, n in self.dma_cnt.items():
            k = ("D", slot)
            v = 16 * n
            if self.waited["sp"].get(k, 0) < v:
                self.waited["sp"][k] = v
                self.streams["sp"].append(({k: v}, None, None, 0))
        nc = self.nc
        keys = [("E", e) for e in CENGS] + [("D", s) for s in self.dma_cnt]
        sems = {}
        for i, k in enumerate(keys):
            sems[k] = self.es.enter_context(nc.semaphore("s%d" % i))
        emap = {"pe": "tensor", "act": "scalar", "dve": "vector", "pool": "gpsimd", "sp": "sync"}

        def replay(name, e):
            for waits, fn, key, inc in self.streams[name]:
                for k, v in waits.items():
                    e.wait_ge(sems[k], v)
                if fn is not None:
                    fn(e).then_inc(sems[key], inc)

        with nc.Block() as block:
            for name in ENGS:
                if not self.streams[name]:
                    continue
                getattr(block, emap[name])(lambda e, name=name: replay(name, e))
        self.es.close()


def consts_np():
    j = np.arange(128)
    c = {}
    c["ident_f"] = np.eye(128, dtype=np.float32)
    c["U_f"] = (j[:, None] <= j[None, :]).astype(np.float32)
    c["Un16_f"] = ((j[:, None] <= j[None, :]).astype(np.float32) / -16.0).astype(np.float32)
    c["negmask_f"] = np.where(j[:, None] <= j[None, :], 0.0, -30000.0).astype(np.float32)
    c["nsmask_f"] = np.where(j[:, None] < j[None, :], -1.0, 0.0).astype(np.float32)
    c["imask_f"] = (j[:, None] <= j[None, :]).astype(np.float32)
    c["ones_f"] = np.ones((128, 128), np.float32)
    c["e16_f"] = np.tile(np.eye(16, dtype=np.float32).reshape(1, 256), (128, 1))
    return c


class K:
    pass


def build(stage=99):
    nc = bass.Bass("TRN2", target_bir_lowering=False)
    P = Prog(nc)
    k = K()

    def din(name, shape):
        return nc.dram_tensor(name, list(shape), F32, kind="ExternalInput").ap()

    def dout(name, shape):
        return nc.dram_tensor(name, list(shape), F32, kind="ExternalOutput").ap()

    def bcast(ap1):
        return ap1.partition_broadcast(128).rearrange("p o f -> p (o f)")

    xp = din("xp", [2048, 1024]); xs = din("xs", [16, 1024])
    w0 = din("w0", [57, 128, 1024]); w1 = din("w1", [41, 128, 1024])
    wo0 = din("wo0", [16, 128, 1024]); wo1 = din("wo1", [16, 128, 1024])
    e_pre_g = din("e_pre_g", [1, 1024]); e_post_g = din("e_post_g", [1, 1024])
    o_pre_g = din("o_pre_g", [1, 1024]); o_post_g = din("o_post_g", [1, 1024])
    e_cw = din("e_cw", [128, 96]); e_cb = din("e_cb", [128, 24])
    s_cw = din("s_cw", [128, 96]); s_cb = din("s_cb", [128, 24])
    g_alog = din("g_alog", [1, 8]); g_dtb = din("g_dtb", [1, 8]); g_ng = din("g_ng", [128, 1])
    l_wlr = din("l_wlr", [17, 512]); l_ng = din("l_ng", [128, 2])
    s_alog = din("s_alog", [1, 32]); s_dtb = din("s_dtb", [1, 32]); s_d = din("s_d", [1, 32])
    s_ng = din("s_ng", [1, 2048])
    s_dcol = din("s_dcol", [128, 16]); s_ngcol = din("s_ngcol", [128, 16])
    st_gconv = din("st_gconv", [48, 3072]); st_gdn = din("st_gdn", [16, 8, 128, 128])
    st_gla = din("st_gla", [16, 4, 128, 256])
    st_sconv = din("st_sconv", [48, 3072]); st_ssd = din("st_ssd", [16, 32 * 64, 128])
    cn = consts_np()
    cd = {n: din("c_" + n, list(a.shape)) for n, a in cn.items()}
    h1scr = nc.dram_tensor("h1scr", [NT, 1024], F32, kind="Internal").ap()

    o_yp = dout("o_yp", [2048, 1024]); o_ys = dout("o_ys", [16, 1024])
    o_gconv_p = dout("o_gconv_p", [3, 3072]); o_gdn_p = dout("o_gdn_p", [8, 128, 128])
    o_gla_p = dout("o_gla_p", [4, 128, 256])
    o_sconv_p = dout("o_sconv_p", [3, 3072]); o_ssd_p = dout("o_ssd_p", [32 * 64, 128])
    o_gconv_s = dout("o_gconv_s", [48, 3072]); o_gdn_s = dout("o_gdn_s", [16, 8, 128, 128])
    o_gla_s = dout("o_gla_s", [16, 4, 128, 256])
    o_sconv_s = dout("o_sconv_s", [48, 3072]); o_ssd_s = dout("o_ssd_s", [16, 32 * 64, 128])

    ident_f = P.sb("ident_f", [128, 128], F32); U_f = P.sb("U_f", [128, 128], F32)
    Un16 = P.sb("Un16_f", [128, 128], F32)
    negmask = P.sb("negmask_f", [128, 128], F32); nsmask = P.sb("nsmask_f", [128, 128], F32)
    imask = P.sb("imask_f", [128, 128], F32)
    ones_f = P.sb("ones_f", [128, 128], F32)
    ident_b = P.sb("ident_b", [128, 128], BF16); ones_b = P.sb("ones_b", [128, 128], BF16)
    E16 = P.sb("e16_f", [128, 16, 16], F32)
    P.dma([(E16[:].rearrange("p a b -> p (a b)"), cd["e16_f"][:])], "c_e16")
    for t, n in [(ident_f, "ident_f"), (U_f, "U_f"), (Un16, "Un16_f"), (negmask, "negmask_f"), (nsmask, "nsmask_f"),
                 (imask, "imask_f"), (ones_f, "ones_f")]:
        P.dma([(t[:], cd[n][:])], "c_" + n)
    P.cp(ident_b[:], ident_f[:], eng="pool"); P.cp(ones_b[:], ones_f[:], eng="pool")

    cw = P.sb("cw", [128, 96], F32); cb = P.sb("cb", [128, 24], F32)
    P.dma([(cw[:], e_cw[:]), (cb[:], e_cb[:])], "c_cw")

    pA = [P.ps("pA0", [128, 512]), P.ps("pA1", [128, 512])]
    pT = P.ps("pT", [128, 1024], BF16)
    pS = P.ps("pS", [128, 512])
    pC = [P.ps("pC%d" % i, [128, 512]) for i in range(4)]

    uT = P.sb("uT", [128, 8, NT], BF16)
    oT = P.sb("oT", [128, 16, NT], BF16)
    ub = [P.sb("ub0", [128, 1024], BF16), P.sb("ub1", [128, 1024], BF16)]
    ss = P.sb("ss", [128, 4], F32)
    wst = [P.sb("wst0", [128, 1024], F32), P.sb("wst1", [128, 1024], F32)]
    wbf = [P.sb("wbf0", [128, 8, 128], BF16), P.sb("wbf1", [128, 8, 128], BF16)]
    k.ui = 0

    def rstd_of(s_, scale):
        P.actf(s_, s_, AF.Ln, bias=EPS, scale=scale)
        P.actf(s_, s_, AF.Exp, scale=-0.5)

    def norm_to_uT(x_, n, col0, gbc):
        i = k.ui % 2; k.ui += 1
        u_ = ub[i]
        s_ = ss[0:n, i:i + 1]
        P.memset(s_, 0.0, eng="dve")
        P.actf(u_[0:n, :], x_[0:n, :], AF.Square, accum_out=s_)
        rstd_of(s_, 1.0 / 1024)
        P.stt(u_[0:n, :], x_[0:n, :], s_, gbc[0:n, :], ALU.mult, ALU.mult)
        for kc in range(8):
            P.tr(pT[:, kc * 128:kc * 128 + n], u_[0:n, kc * 128:(kc + 1) * 128], ident_b[0:n, 0:n])
        P.cp(uT[:, :, col0:col0 + n], pT[:].rearrange("p (k t) -> p k t", k=8)[:, :, 0:n], eng="act")

    ROWT = [(xp[i * 128:(i + 1) * 128, :], 128, i * 128) for i in range(16)] + [(xs[:, :], 16, 2048)]
    P.push()
    xt = [P.sb("xt0", [128, 1024], F32), P.sb("xt1", [128, 1024], F32)]
    gbc0 = P.sb("gbc0", [128, 1024], F32)
    P.dma([(gbc0[:], bcast(e_pre_g[0:1, :]))], "c_gbc")
    for i, (src, n, col0) in enumerate(ROWT):
        P.dma([(xt[i % 2][0:n, :], src)], "xt%d" % (i % 2))
        norm_to_uT(xt[i % 2], n, col0, gbc0)
    P.pop()

    if stage <= -3:
        P.finish(); return nc, cn
    k.wi = 0

    def load_w(wsrc, blk, dst=None):
        i = k.wi % 2; k.wi += 1
        P.dma([(wst[i][:], wsrc[blk])], "wst%d" % i)
        if dst is None:
            dst = wbf[i][:]
        P.cp(dst, wst[i][:].rearrange("p (k c) -> p k c", k=8), eng="pool")
        return wbf[i]

    class WQ:
        def __init__(self, items):
            self.items = list(items); self.nd = 0; self.nc_ = 0; self.dq = []; self.cq = []

        def _dma(self):
            if self.nd < len(self.items):
                wsrc, blk, dst = self.items[self.nd]; self.nd += 1
                bi = k.wi % 2; k.wi += 1
                P.dma([(wst[bi][:], wsrc[blk])], "wst%d" % bi)
                self.dq.append((bi, dst))

        def _cast(self):
            if self.dq:
                bi, dst = self.dq.pop(0)
                d_ = wbf[bi][:] if dst is None else dst
                P.cp(d_, wst[bi][:].rearrange("p (k c) -> p k c", k=8), eng="pool")
                self.cq.append(wbf[bi])

        def get(self):
            if not self.cq:
                self._dma(); self._cast()
            cur = self.cq.pop(0)
            if not self.dq:
                self._dma()
            self._cast()
            self._dma()
            return cur

    k.pi = 0

    def proj_fm(wb, evac, M=128, c0=0):
        for (t0, n) in GRPS:
            pa = pA[k.pi % 2]; k.pi += 1
            for kc in range(8):
                P.mm(pa[0:M, 0:n], wb[:, kc, c0:c0 + M], uT[:, kc, t0:t0 + n], start=(kc == 0), stop=(kc == 7))
            evac(pa[0:M, 0:n], t0, n)

    def load_hist(hist_, src):
        for pc in range(3):
            hb = wst[k.wi % 2]
            P.dma([(hb[0:48, :], src[:, pc * 1024:(pc + 1) * 1024])], "wst%d" % (k.wi % 2))
            k.wi += 1
            for c8 in range(8):
                ct = pc * 8 + c8
                P.tr(pS[:, 256:304], hb[0:48, c8 * 128:(c8 + 1) * 128], ident_f[0:48, 0:48])
                P.cp(hist_[:, ct, :], pS[:, 256:304])

    hist = P.sb("hist", [128, 24, 48], F32)
    Xc = P.sb("Xc", [128, 3 + 2048], F32)
    Xs = P.sb("Xs", [128, 16, 4], F32)
    Yc = P.sb("Yc", [128, NT], F32)
    nst = P.sb("nst", [128, 51], F32)
    nso = [P.sb("nso0", [51, 128], F32), P.sb("nso1", [51, 128], F32)]
    k.ni = 0
    P.memset(Xc[:, 0:3], 0.0)

    def conv_tile(wq, ct, ocp, ocs):
        wb = wq.get()

        def ev(pa, t0, n):
            if t0 < 2048:
                P.cp(Xc[:, 3 + t0:3 + t0 + n], pa, eng="act")
            else:
                P.cp(Xs[:, :, 3], pa, eng="act")
        proj_fm(wb, ev)
        P.cp(Xs[:, :, 0:3], hist[:, ct, :].rearrange("p (s r) -> p s r", r=3), eng="pool")
        for (xin, yout) in [(lambda i: Xc[:, i:i + 2048], Yc[:, 0:2048]), (lambda i: Xs[:, :, i], Yc[:, 2048:2064])]:
            P.ts(yout, xin(0), cw[:, ct * 4:ct * 4 + 1], ALU.mult, cb[:, ct:ct + 1], ALU.add)
            for i in (1, 2, 3):
                P.stt(yout, xin(i), cw[:, ct * 4 + i:ct * 4 + i + 1], yout, ALU.mult, ALU.add)
        P.actf(Yc[:], Yc[:], AF.Silu)
        P.cp(nst[:, 0:3], Xc[:, 2048:2051], eng="pool")
        P.cp(nst[:, 3:51].rearrange("p (s r) -> p s r", r=3), Xs[:, :, 1:4], eng="pool")
        P.tr(pS[0:51, 384:512], nst[:], ident_f[:])
        no = nso[k.ni % 2]
        P.cp(no[:], pS[0:51, 384:512])
        P.dma([(ocp[:, ct * 128:(ct + 1) * 128], no[0:3, :]), (ocs[:, ct * 128:(ct + 1) * 128], no[3:51, :])], "nso%d" % (k.ni % 2))
        k.ni += 1

    load_hist(hist, st_gconv)

    P.push()
    lrT = P.sb("lrT", [17, NT], BF16)
    P.push()
    wsm = P.sb("wsm", [128, 8, 128], BF16)
    load_w(w0, 56, wsm[:])
    alog_bc = P.sb("alog_bc", [128, 8], F32); dtb_bc = P.sb("dtb_bc", [128, 8], F32)
    negA = P.sb("negA", [128, 8], F32); ng_col = P.sb("ng_col", [128, 1], F32)
    P.dma([(ng_col[:], g_ng[:]), (alog_bc[:], bcast(g_alog[0:1, :])), (dtb_bc[:], bcast(g_dtb[0:1, :]))], "c_misc")
    P.actf(negA[:], alog_bc[:], AF.Exp)
    P.ts(negA[:], negA[:], -1.0, ALU.mult)
    SC = P.sb("SC", [128, 16, 64], F32)
    SCs = P.sb("SCs", [16, 32], F32)
    GL = P.sb("GL", [128, 16, 16], F32)
    tmp8 = P.sb("tmp8", [128, 16], F32)
    for kc in range(8):
        P.mm(pS[0:16, 0:32], uT[:, kc, 2048:2064], wsm[:, kc, 0:32], start=(kc == 0), stop=(kc == 7))
    P.actf(SCs[:, 0:8], pS[0:16, 0:8], AF.Sigmoid)
    P.tt(tmp8[0:16, 0:8], pS[0:16, 8:16], dtb_bc[0:16, :], ALU.add)
    P.actf(tmp8[0:16, 0:8], tmp8[0:16, 0:8], AF.Exp)
    P.actf(tmp8[0:16, 0:8], tmp8[0:16, 0:8], AF.Ln, bias=1.0)
    P.tt(SCs[:, 8:16], tmp8[0:16, 0:8], negA[0:16, :], ALU.mult)
    P.actf(SCs[:, 16:24], SCs[:, 8:16], AF.Exp)
    P.ts(SCs[:, 24:32], SCs[:, 16:24], -1.0, ALU.mult)
    for g, (c0, n) in enumerate(SEGS[:16]):
        ps = pS[0:n, 0:32]
        for kc in range(8):
            P.mm(ps, uT[:, kc, c0:c0 + n], wsm[:, kc, 0:32], start=(kc == 0), stop=(kc == 7))
        sc = SC[0:n, g, :]
        P.actf(sc[:, 0:8], ps[:, 0:8], AF.Sigmoid)
        P.tt(tmp8[0:n, 0:8], ps[:, 8:16], dtb_bc[0:n, :], ALU.add)
        P.actf(tmp8[0:n, 0:8], tmp8[0:n, 0:8], AF.Exp)
        P.actf(tmp8[0:n, 0:8], tmp8[0:n, 0:8], AF.Ln, bias=1.0)
        P.tt(sc[:, 8:16], tmp8[0:n, 0:8], negA[0:n, :], ALU.mult)
        P.mm(pS[0:n, 32:40], U_f[0:n, 0:n], sc[:, 8:16])
        P.mm(pS[:, 40:48], ones_f[0:n, :], sc[:, 8:16])
        P.cp(sc[:, 16:24], pS[0:n, 32:40])
        P.ts(sc[:, 24:32], pS[0:n, 32:40], -1.0, ALU.mult)
        P.actf(sc[:, 32:40], pS[0:n, 32:40], AF.Exp)
        P.ts(sc[:, 40:48], sc[:, 32:40], -1.0, ALU.mult)
        P.cp(GL[:, g, 8:16], pS[:, 40:48])
        P.actf(GL[:, g, 0:8], pS[:, 40:48], AF.Exp)
        P.tt(tmp8[0:n, 8:16], GL[0:n, g, 8:16], sc[:, 16:24], ALU.subtract)
        P.actf(sc[:, 48:56], tmp8[0:n, 8:16], AF.Exp)
    P.memset(lrT[:], 1.0)
    for (t0, n) in GRPS:
        for kc in range(8):
            P.mm(pS[0:16, 0:n], wsm[:, kc, 16:32], uT[:, kc, t0:t0 + n], start=(kc == 0), stop=(kc == 7))
        P.cp(lrT[0:16, t0:t0 + n], pS[0:16, 0:n], eng="act")

    sq = P.sb("sq", [128, NT], BF16)
    rn = P.sb("rn", [128, 512], F32)

    def l2norm_to(dsts, scale):
        P.tt(sq[:], Yc[:], Yc[:], ALU.mult, eng="pool")
        for (t0, n) in GRPS:
            P.mm(pS[:, 0:n], ones_b[:], sq[:, t0:t0 + n])
            P.actf(rn[:, 0:n], pS[:, 0:n], AF.Ln, bias=EPS)
            P.actf(rn[:, 0:n], rn[:, 0:n], AF.Exp, scale=-0.5)
            for d_ in dsts:
                P.stt(d_[:, t0:t0 + n], Yc[:, t0:t0 + n], scale, rn[:, 0:n], ALU.mult, ALU.mult)

    qT = P.sb("qT", [128, NT], BF16); vT = P.sb("vT", [128, NT], BF16)
    kT32 = P.sb("kT32", [128, NT], F32)
    gsT = P.sb("gsT", [128, NT], BF16)
    S = [P.sb("S0", [128, 128], F32), P.sb("S1", [128, 128], F32)]
    Sb = [P.sb("Sb0", [128, 128], BF16), P.sb("Sb1", [128, 128], BF16)]
    vtok = [P.sb("vtk_%d" % i, [128, 128], BF16) for i in range(3)]; kdtok = [P.sb("kdtk_%d" % i, [128, 128], F32) for i in range(3)]
    decTs = [P.sb("decT%d" % i, [128, 128], F32) for i in range(2)]; pTm = [P.sb("pTmm_%d" % i, [128, 128], F32) for i in range(3)]
    Mis = [P.sb("Mi%d" % i, [128, 128], F32) for i in range(2)]; kcbs = [P.sb("kcb%d" % i, [128, 128], BF16) for i in range(2)]
    Abs = [[P.sb("Ab%d_%d" % (t, i), [128, 128], F32) for i in range(2)] for t in range(2)]
    ATbs = [[P.sb("ATb%d_%d" % (t, i), [128, 128], F32) for i in range(2)] for t in range(2)]
    P32 = [P.sb("P32_%d" % i, [128, 128], F32) for i in range(3)]
    R32 = P.sb("R32", [128, 128], F32); u32 = P.sb("u32", [128, 128], F32)
    o2s = P.sb("o2s", [128, 128], F32); osb = P.sb("osb", [128, 128], F32); onb = P.sb("onb", [128, 128], BF16)
    oss = P.sb("oss", [128, 1], F32)
    k.si = 0

    def interleave(gens):
        gens = [g_ for g_ in gens if g_ is not None]
        while gens:
            for g_ in list(gens):
                try:
                    next(g_)
                except StopIteration:
                    gens.remove(g_)

    def gdn_prep(h, g, c0, n, pb, tb):
        sc = SC[0:n, g, :]
        Ab = Abs[tb]; ATb = ATbs[tb]; decT = decTs[tb]; Mi = Mis[tb]; kcb = kcbs[tb]
        if tb == 0:
            r_kk = pC[0][:, 0:128]; r_qk = pC[0][:, 128:256]; r_G = pC[0][:, 256:384]
            r_s0 = pC[1][:, 0:128]; r_s1 = pC[1][:, 128:256]; r_P = pA[0][:, 0:128]
            r_kt = pA[1][:, 0:128]; r_vt = pT[:, 0:128]
        else:
            r_kk = pC[1][:, 256:384]; r_qk = pC[1][:, 384:512]; r_G = pA[0][:, 128:256]
            r_s0 = pA[1][:, 128:256]; r_s1 = pA[1][:, 256:384]; r_P = pA[0][:, 256:384]
            r_kt = pA[1][:, 384:512]; r_vt = pT[:, 128:256]
        qc_ = qT[:, c0:c0 + n]; k32 = kT32[:, c0:c0 + n]
        kc_ = kcb[:, 0:n]
        P.cp(kc_, k32, eng="pool")
        P.tr(r_vt[0:n, :], vT[:, c0:c0 + n], ident_b[:])
        P.cp(vtok[pb][0:n, :], r_vt[0:n, :], eng="act")
        P.tr(r_kt[0:n, :], k32, ident_f[:])
        P.actf(kdtok[pb][0:n, :], r_kt[0:n, :], AF.Copy, scale=sc[:, 48 + h:49 + h])
        yield
        P.mm(r_qk[0:n, 0:n], kc_, qc_)
        P.mm(r_G[0:n, 0:n], sc[:, 8 + h:9 + h].to_broadcast([n, n]), U_f[0:n, 0:n], start=True, stop=False)
        P.mm(r_G[0:n, 0:n], ident_f[0:n, 0:n], negmask[0:n, 0:n], start=False, stop=True)
        if n > 1:
            P.mm(r_kk, kc_, kc_)
        yield
        P.actf(decT[0:n, 0:n], r_G[0:n, 0:n], AF.Exp, bias=sc[:, 24 + h:25 + h])
        P.tt(pTm[pb][0:n, 0:n], r_qk[0:n, 0:n], decT[0:n, 0:n], ALU.mult)
        yield
        if n > 1:
            Pf = P32[pb]
            P.stt(Mi[:], r_kk, sc[:, h:h + 1], decT[:], ALU.mult, ALU.mult)
            P.tt(Ab[0][:], Mi[:], nsmask[:], ALU.mult, eng="pool")
            yield
            P.tr(r_s0, Ab[0][:], ident_f[:])
            P.cp(ATb[0][:], r_s0, eng="act")
            P.tt(Pf[:], Ab[0][:], ident_f[:], ALU.add, eng="pool")
            yield
            cur = 0
            for it in range(6):
                nx = 1 - cur
                P.mm(r_s0, Ab[cur][:], ATb[cur][:])
                if it < 5:
                    P.mm(r_s1, ATb[cur][:], Ab[cur][:])
                yield
                P.cp(ATb[nx][:], r_s0, eng="act")
                if it < 5:
                    P.cp(Ab[nx][:], r_s1, eng="dve")
                yield
                P.mm(r_P, ATb[nx][:], Pf[:])
                yield
                P.tt(Pf[:], r_P, Pf[:], ALU.add)
                yield
                cur = nx

    def gdn_seq(h, g, c0, n, first, last, s_src, s_dst, pb):
        sc = SC[0:n, g, :]
        if first:
            k.si += 1
        Sf = S[k.si % 2]; Sbb = Sb[k.si % 2]
        if first:
            if s_src is None:
                P.memset(Sf[:], 0.0)
            else:
                P.dma([(Sf[:], s_src)], "Sld%d" % (k.si % 2))
            P.cp(Sbb[:], Sf[:], eng="pool")
            yield
        qc_ = qT[:, c0:c0 + n]; k32 = kT32[:, c0:c0 + n]
        Pfin = P32[pb] if n > 1 else ident_f
        P.mm(pC[3][0:n, 0:128], k32, Sf[:])
        yield
        P.stt(R32[0:n, :], pC[3][0:n, 0:128], sc[:, 40 + h:41 + h], vtok[pb][0:n, :], ALU.mult, ALU.add)
        yield
        P.mm(pC[2][0:n, 0:128], Pfin[0:n, 0:n], R32[0:n, :])
        yield
        P.actf(u32[0:n, :], pC[2][0:n, 0:128], AF.Copy, scale=sc[:, h:h + 1])
        yield
        P.mm(pC[3][:, 128:256], kdtok[pb][0:n, :], u32[0:n, :])
        P.mm(pC[2][0:n, 256:384], qc_, Sbb[:])
        P.mm(pC[2][0:n, 384:512], pTm[pb][0:n, 0:n], u32[0:n, :])
        yield
        P.stt(Sf[:], Sf[:], GL[:, g, h:h + 1], pC[3][:, 128:256], ALU.mult, ALU.add)
        yield
        if not last:
            P.cp(Sbb[:], Sf[:], eng="pool")
        else:
            P.dma([(s_dst, Sf[:])], "Sst%d" % (k.si % 2))
        P.cp(o2s[0:n, :], pC[2][0:n, 384:512], eng="act")
        yield
        P.stt(osb[0:n, :], pC[2][0:n, 256:384], sc[:, 32 + h:33 + h], o2s[0:n, :], ALU.mult, ALU.add)
        P.memset(oss[0:n, :], 0.0, eng="dve")
        yield
        P.actf(onb[0:n, :], osb[0:n, :], AF.Square, accum_out=oss[0:n, :])
        rstd_of(oss[0:n, :], 1.0 / 128)
        P.actf(osb[0:n, :], osb[0:n, :], AF.Copy, scale=oss[0:n, :])
        yield
        P.tr(pS[:, 0:n], osb[0:n, :], ident_f[0:n, 0:n])
        yield
        P.tt(oT[:, h, c0:c0 + n], pS[:, 0:n], gsT[:, c0:c0 + n], ALU.mult)
        yield

    def gdn_head_segs(h):
        segl = []
        for g, (c0, n) in enumerate(SEGS[:16]):
            segl.append((g, c0, n, g == 0, g == 15, None, o_gdn_p[h]))
        preps = [gdn_prep(h, sg[0], sg[1], sg[2], i % 3, i % 2) for i, sg in enumerate(segl)]
        done = [False] * len(segl)

        def step(i):
            if i < len(segl) and not done[i]:
                try:
                    next(preps[i])
                except StopIteration:
                    done[i] = True

        while not done[0]:
            step(0)
        for i, (g, c0, n, fi, la, src, dst) in enumerate(segl):
            sq_ = gdn_seq(h, g, c0, n, fi, la, src, dst, i % 3)
            sq_done = False
            while not (sq_done and (i + 1 >= len(segl) or done[i + 1])):
                if not sq_done:
                    try:
                        next(sq_)
                    except StopIteration:
                        sq_done = True
                step(i + 1)
                step(i + 2)

    Sall = P.sb("Sall", [128, 16, 128], F32)
    Km = P.sb("Km", [128, 16, 16], F32); Qm = P.sb("Qm", [128, 16, 16], F32); q32c = P.sb("q32c", [128, 16], F32)
    vts = P.sb("vts", [16, 128], BF16); Rs = P.sb("Rs", [16, 128], F32); us = P.sb("us", [16, 128], F32)
    dg = P.sb("dg", [16, 16], F32); egbc = P.sb("egbc", [128, 16, 1], F32)
    osbs = P.sb("osbs", [16, 128], F32); onbs = P.sb("onbs", [16, 128], F32); osss = P.sb("osss", [16, 1], F32)

    def gdn_sample(h):
        kc32 = kT32[:, 2048:2064]
        P.cp(q32c[:], qT[:, 2048:2064], eng="pool")
        P.tt(Km[:], kc32.unsqueeze(2).to_broadcast([128, 16, 16]), E16[:], ALU.mult, eng="pool")
        P.tt(Qm[:], q32c[:].unsqueeze(2).to_broadcast([128, 16, 16]), E16[:], ALU.mult, eng="pool")
        P.tr(pT[0:16, 0:128], vT[:, 2048:2064], ident_b[:])
        P.cp(vts[:], pT[0:16, 0:128], eng="act")
        P.ts(dg[:], ident_f[0:16, 0:16], SCs[:, 16 + h:17 + h], ALU.mult)
        P.mm(pC[1][:, 0:16], ones_f[0:16, :], dg[:])
        P.cp(egbc[:].rearrange("p s o -> p (s o)"), pC[1][:, 0:16], eng="act")
        for s in range(16):
            P.mm(pC[0][0:16, 0:128], Km[:, s, :], Sall[:, s, :], start=(s == 0), stop=(s == 15))
        P.stt(Rs[:], pC[0][0:16, 0:128], SCs[:, 24 + h:25 + h], vts[:], ALU.mult, ALU.add)
        P.ts(us[:], Rs[:], SCs[:, h:h + 1], ALU.mult)
        P.tt(Sall[:], Sall[:], egbc[:].to_broadcast([128, 16, 128]), ALU.mult)
        for s in range(16):
            bk = pC[2] if (s // 4) % 2 == 0 else pC[3]
            r0 = (s % 4) * 128
            P.mm(bk[:, r0:r0 + 128], ident_f[0:16, s:s + 1].to_broadcast([16, 128]), us[:])
            P.stt(Sall[:, s, :], bk[:, r0:r0 + 128], kc32[:, s:s + 1], Sall[:, s, :], ALU.mult, ALU.add)
        for s in range(16):
            P.mm(pC[0][0:16, 128:256], Qm[:, s, :], Sall[:, s, :], start=(s == 0), stop=(s == 15))
        P.dma([(o_gdn_s[:, h].rearrange("s d e -> d s e"), Sall[:])], "Sall")
        P.cp(osbs[:], pC[0][0:16, 128:256], eng="act")
        P.memset(osss[:], 0.0, eng="dve")
        P.actf(onbs[:], osbs[:], AF.Square, accum_out=osss[:])
        rstd_of(osss[:], 1.0 / 128)
        P.actf(onbs[:], osbs[:], AF.Copy, scale=osss[:])
        P.tr(pS[:, 0:16], onbs[:], ident_f[0:16, 0:16])
        P.tt(oT[:, h, 2048:2064], pS[:, 0:16], gsT[:, 2048:2064], ALU.mult)

    wq0 = WQ([(w0, b_, None) for h in range(8) for b_ in (h, 8 + h, 16 + h, 24 + h)])
    for h in range(8 if stage >= 1 else 0):
        conv_tile(wq0, h, o_gconv_p, o_gconv_s); l2norm_to([qT], 128 ** -0.5)
        conv_tile(wq0, 8 + h, o_gconv_p, o_gconv_s); l2norm_to([kT32], 1.0)
        conv_tile(wq0, 16 + h, o_gconv_p, o_gconv_s); P.cp(vT[:], Yc[:], eng="pool")
        wb = wq0.get()
        proj_fm(wb, lambda pa, t0, n: P.actf(Yc[:, t0:t0 + n], pa, AF.Silu))
        P.ts(gsT[:], Yc[:], ng_col[:], ALU.mult)
        P.dma([(Sall[:], st_gdn[:, h].rearrange("s d e -> d s e"))], "Sall")
        gdn_head_segs(h)
        gdn_sample(h)
    P.pop()
    if stage <= 1:
        P.pop()
        print('sbuf_remaining', nc.sbuf_bytes_remaining)
        print('counts', P.cnt, 'waits', P.nwaits, 'dma slots', len(P.dma_cnt))
        P.finish(); return nc, cn

    P.push()
    wlr = P.sb("wlr", [17, 512], BF16)
    P.dma([(wst[k.wi % 2][0:17, 0:512], l_wlr[:])], "wst%d" % (k.wi % 2))
    P.cp(wlr[:], wst[k.wi % 2][0:17, 0:512], eng="pool")
    k.wi += 1
    lng = P.sb("lng", [128, 2], F32)
    P.dma([(lng[:], l_ng[:])], "c_misc")
    lqT = P.sb("lqT", [128, NT], BF16); lkT = P.sb("lkT", [128, NT], BF16)
    gs2 = P.sb("gs2", [128, 2, NT], BF16)
    wv = P.sb("wv", [128, 8, 256], BF16)
    S2 = [P.sb("S2a", [128, 256], F32), P.sb("S2b", [128, 256], F32)]
    S2h = [P.sb("S2ha", [128, 256], BF16), P.sb("S2hb", [128, 256], BF16)]
    sp32 = P.sb("sp32", [128, 128], F32)
    ebt = P.sb("ebt", [128, 128], F32); enbt = P.sb("enbt", [128, 128], F32); ekdt = P.sb("ekdt", [128, 128], F32)
    bl = P.sb("bl", [128, 2], F32)
    qeb = P.sb("qeb", [128, 128], BF16); keb = P.sb("keb", [128, 128], BF16); kdTb = P.sb("kdTb", [128, 128], BF16)
    kdtok2 = P.sb("kdtok2", [128, 128], BF16); pTm2 = P.sb("pTm2", [128, 128], BF16)
    vtok2 = P.sb("vtok2", [128, 256], BF16)
    osb2 = P.sb("osb2", [128, 256], F32); onb2 = P.sb("onb2", [128, 256], BF16); oss2 = P.sb("oss2", [128, 1], F32)
    k.s2 = 0

    def gla_seg(h, g, c0, n, first, last, s_src, s_dst):
        if first:
            k.s2 += 1
        Sf = S2[k.s2 % 2]; Sbb = S2h[k.s2 % 2]
        if first:
            if s_src is None:
                P.memset(Sf[:], 0.0)
            else:
                P.dma([(Sf[:], s_src)], "S2ld%d" % (k.s2 % 2))
            P.cp(Sbb[:], Sf[:], eng="pool")
        P.mm(pS[0:n, 0:128], lrT[0:17, c0:c0 + n], wlr[0:17, h * 128:(h + 1) * 128])
        P.actf(sp32[0:n, :], pS[0:n, 0:128], AF.Exp, scale=-1.0)
        P.actf(sp32[0:n, :], sp32[0:n, :], AF.Ln, bias=1.0)
        P.mm(pC[0][:, 0:n], sp32[0:n, :], Un16[0:n, 0:n])
        P.actf(ebt[:, 0:n], pC[0][:, 0:n], AF.Exp)
        P.actf(enbt[:, 0:n], pC[0][:, 0:n], AF.Exp, scale=-1.0)
        P.cp(bl[:, 0:1], pC[0][:, n - 1:n])
        P.actf(ekdt[:, 0:n], pC[0][:, 0:n], AF.Exp, scale=-1.0, bias=bl[:, 0:1])
        P.actf(bl[:, 1:2], bl[:, 0:1], AF.Exp)
        P.stt(qeb[:, 0:n], lqT[:, c0:c0 + n], 128 ** -0.5, ebt[:, 0:n], ALU.mult, ALU.mult)
        P.tt(keb[:, 0:n], lkT[:, c0:c0 + n], enbt[:, 0:n], ALU.mult)
        P.tt(kdTb[:, 0:n], lkT[:, c0:c0 + n], ekdt[:, 0:n], ALU.mult, eng="pool")
        P.tr(pT[0:n, 0:128], kdTb[:, 0:n], ident_b[:])
        P.cp(kdtok2[0:n, :], pT[0:n, 0:128], eng="act")
        P.mm(pC[1][0:n, 0:n], keb[:, 0:n], qeb[:, 0:n])
        P.tt(pTm2[0:n, 0:n], pC[1][0:n, 0:n], imask[0:n, 0:n], ALU.mult)
        for kc in range(8):
            P.mm(pA[0][0:n, 0:256], uT[:, kc, c0:c0 + n], wv[:, kc, :], start=(kc == 0), stop=(kc == 7))
        P.cp(vtok2[0:n, :], pA[0][0:n, 0:256], eng="act")
        P.mm(pC[2][0:n, 0:256], qeb[:, 0:n], Sbb[:], start=True, stop=False)
        P.mm(pC[2][0:n, 0:256], pTm2[0:n, 0:n], vtok2[0:n, :], start=False, stop=True)
        P.mm(pC[3][:, 0:256], kdtok2[0:n, :], vtok2[0:n, :])
        P.stt(Sf[:], Sf[:], bl[:, 1:2], pC[3][:, 0:256], ALU.mult, ALU.add)
        if not last:
            P.cp(Sbb[:], Sf[:], eng="pool")
        else:
            P.dma([(s_dst, Sf[:])], "S2st%d" % (k.s2 % 2))
        P.cp(osb2[0:n, :], pC[2][0:n, 0:256], eng="act")
        P.memset(oss2[0:n, :], 0.0, eng="dve")
        P.actf(onb2[0:n, :], osb2[0:n, :], AF.Square, accum_out=oss2[0:n, :])
        rstd_of(oss2[0:n, :], 1.0 / 256)
        P.actf(onb2[0:n, :], osb2[0:n, :], AF.Copy, scale=oss2[0:n, :])
        for j in range(2):
            P.tr(pT[:, 256 + j * 128:256 + j * 128 + n], onb2[0:n, j * 128:(j + 1) * 128], ident_b[0:n, 0:n])
        for j in range(2):
            P.tt(oT[:, 8 + 2 * h + j, c0:c0 + n], pT[:, 256 + j * 128:256 + j * 128 + n], gs2[:, j, c0:c0 + n], ALU.mult)

    Sgs = [P.sb("Sg0", [128, 8, 256], F32), P.sb("Sg1", [128, 8, 256], F32)]
    Qm2 = P.sb("Qm2", [128, 16, 16], F32); q32g = P.sb("q32g", [128, 16], F32); k32g = P.sb("k32g", [128, 16], F32)
    aTs = P.sb("aTs", [128, 16, 1], F32); sps = P.sb("sps", [16, 128], F32)
    vts2 = P.sb("vts2", [16, 256], BF16)
    osg = P.sb("osg", [16, 256], F32); ong = P.sb("ong", [16, 256], F32); ossg = P.sb("ossg", [16, 1], F32)

    def gla_sample(h):
        P.mm(pS[0:16, 0:128], lrT[0:17, 2048:2064], wlr[0:17, h * 128:(h + 1) * 128])
        P.actf(sps[:], pS[0:16, 0:128], AF.Exp, scale=-1.0)
        P.actf(sps[:], sps[:], AF.Ln, bias=1.0)
        P.actf(sps[:], sps[:], AF.Exp, scale=-1.0 / 16)
        P.tr(pS[:, 128:144], sps[:], ident_f[0:16, 0:16])
        P.cp(aTs[:].rearrange("p s o -> p (s o)"), pS[:, 128:144], eng="act")
        P.cp(k32g[:], lkT[:, 2048:2064], eng="pool")
        P.ts(q32g[:], lqT[:, 2048:2064], 128 ** -0.5, ALU.mult)
        P.tt(Qm2[:], q32g[:].unsqueeze(2).to_broadcast([128, 16, 16]), E16[:], ALU.mult, eng="pool")
        for kc in range(8):
            P.mm(pA[0][0:16, 0:256], uT[:, kc, 2048:2064], wv[:, kc, :], start=(kc == 0), stop=(kc == 7))
        P.cp(vts2[:], pA[0][0:16, 0:256], eng="act")
        for hf in range(2):
            Sg = Sgs[hf]
            for s8 in range(8):
                s = hf * 8 + s8
                bk = pC[2] if s % 2 == 0 else pC[3]
                P.mm(bk[:, 0:256], ident_b[0:16, s:s + 1].to_broadcast([16, 128]), vts2[:])
                P.tt(Sg[:, s8, :], Sg[:, s8, :], aTs[:, s, :].to_broadcast([128, 256]), ALU.mult, eng="pool")
                P.stt(Sg[:, s8, :], bk[:, 0:256], k32g[:, s:s + 1], Sg[:, s8, :], ALU.mult, ALU.add)
            for s8 in range(8):
                s = hf * 8 + s8
                P.mm(pC[0][0:16, 0:256], Qm2[:, s, :], Sg[:, s8, :], start=(s == 0), stop=(s == 15), )
            P.dma([(o_gla_s[hf * 8:(hf + 1) * 8, h].rearrange("s d e -> d s e"), Sg[:])], "Sg%d" % hf)
        P.cp(osg[:], pC[0][0:16, 0:256], eng="act")
        P.memset(ossg[:], 0.0, eng="dve")
        P.actf(ong[:], osg[:], AF.Square, accum_out=ossg[:])
        rstd_of(ossg[:], 1.0 / 256)
        P.actf(ong[:], osg[:], AF.Copy, scale=ossg[:])
        for j in range(2):
            P.tr(pS[:, 256 + j * 16:256 + (j + 1) * 16], ong[:, j * 128:(j + 1) * 128], ident_f[0:16, 0:16])
        for j in range(2):
            P.tt(oT[:, 8 + 2 * h + j, 2048:2064], pS[:, 256 + j * 16:256 + (j + 1) * 16], gs2[:, j, 2048:2064], ALU.mult)

    gl_items = []
    for h in range(4):
        gl_items += [(w0, 32 + h, None), (w0, 36 + h, None)]
        for j in range(2):
            gl_items += [(w0, 48 + 2 * h + j, None), (w0, 40 + 2 * h + j, wv[:, :, j * 128:(j + 1) * 128])]
    wqg = WQ(gl_items)
    for h in range(4):
        proj_fm(wqg.get(), lambda pa, t0, n: P.cp(lqT[:, t0:t0 + n], pa, eng="act"))
        proj_fm(wqg.get(), lambda pa, t0, n: P.cp(lkT[:, t0:t0 + n], pa, eng="act"))
        for j in range(2):
            proj_fm(wqg.get(), lambda pa, t0, n: P.actf(Yc[:, t0:t0 + n], pa, AF.Silu))
            P.ts(gs2[:, j, :], Yc[:], lng[:, j:j + 1], ALU.mult)
            wqg.get()
        for hf in range(2):
            P.dma([(Sgs[hf][:], st_gla[hf * 8:(hf + 1) * 8, h].rearrange("s d e -> d s e"))], "Sg%d" % hf)
        for g, (c0, n) in enumerate(SEGS[:16]):
            gla_seg(h, g, c0, n, g == 0, g == 15, None, o_gla_p[h])
        gla_sample(h)
    P.pop()
    P.pop()
    if stage <= 2:
        print('sbuf_remaining', nc.sbuf_bytes_remaining)
        print('counts', P.cnt, 'waits', P.nwaits, 'dma slots', len(P.dma_cnt))
        P.finish(); return nc, cn

    def out_proj(wo_src, post_g_src, res_src, dst_fn, next_pre_g):
        P.push()
        tag = "L%d" % k.ui
        wo = P.sb("wo" + tag, [128, 16, 1024], BF16)
        gpo = P.sb("gpo" + tag, [128, 1024], F32)
        h1t = [P.sb("h1a" + tag, [128, 1024], F32), P.sb("h1b" + tag, [128, 1024], F32)]
        xt = [P.sb("xta" + tag, [128, 1024], F32), P.sb("xtb" + tag, [128, 1024], F32)]
        ss2 = P.sb("ss2" + tag, [128, 4], F32)
        junk2 = [P.sb("jka" + tag, [128, 512], BF16), P.sb("jkb" + tag, [128, 512], BF16)]
        P.dma([(gpo[:], bcast(post_g_src[0:1, :]))], "c_gpo")
        if next_pre_g is not None:
            gbc = P.sb("gbc" + tag, [128, 1024], F32)
            P.dma([(gbc[:], bcast(next_pre_g[0:1, :]))], "c_gbc")
        for f in range(16):
            i = k.wi % 2; k.wi += 1
            P.dma([(wst[i][:], wo_src[f])], "wst%d" % i)
            P.cp(wo[:, f, :], wst[i][:], eng="pool")
        rows = [(i * 128, 128, i * 128) for i in range(16)] + [(2048, 16, 2048)]
        for i, (r0, n, col0) in enumerate(rows):
            ht = h1t[i % 2]; xr = xt[i % 2]
            P.dma([(xr[0:n, :], res_src(r0, n))], "xr%d" % (i % 2))
            ssx = ss2[0:n, 2 * (i % 2):2 * (i % 2) + 2]
            pb2 = (pA[0], pA[1]) if i % 2 == 0 else (pC[0], pC[1])
            P.memset(ssx, 0.0, eng="dve")
            for hf in range(2):
                for f in range(16):
                    P.mm(pb2[hf][0:n, :], oT[:, f, col0:col0 + n], wo[:, f, hf * 512:(hf + 1) * 512], start=(f == 0), stop=(f == 15))
                P.actf(junk2[i % 2][0:n, :], pb2[hf][0:n, :], AF.Square, accum_out=ssx[:, hf:hf + 1])
            P.tt(ssx[:, 0:1], ssx[:, 0:1], ssx[:, 1:2], ALU.add)
            rstd_of(ssx[:, 0:1], 1.0 / 1024)
            for hf in range(2):
                sl = slice(hf * 512, (hf + 1) * 512)
                P.stt(ht[0:n, sl], pb2[hf][0:n, :], ssx[:, 0:1], gpo[0:n, sl], ALU.mult, ALU.mult)
            P.tt(ht[0:n, :], ht[0:n, :], xr[0:n, :], ALU.add, eng="pool")
            dst_fn(ht, r0, n, i)
            if next_pre_g is not None:
                norm_to_uT(ht, n, col0, gbc)
        P.pop()

    def l0_dst(ht, r0, n, i):
        P.dma([(h1scr[r0:r0 + n, :], ht[0:n, :])], "h1st%d" % (i % 2))

    out_proj(wo0, e_post_g, lambda r0, n: (xp[r0:r0 + n, :] if r0 < 2048 else xs[:, :]), l0_dst, o_pre_g)
    if stage <= 3:
        print('sbuf_remaining', nc.sbuf_bytes_remaining)
        print('counts', P.cnt, 'waits', P.nwaits, 'dma slots', len(P.dma_cnt))
        P.finish(); return nc, cn

    P.dma([(cw[:], s_cw[:]), (cb[:], s_cb[:])], "c_cw")
    load_hist(hist, st_sconv)
    P.push()
    wsm1 = P.sb("wsm1", [128, 8, 128], BF16)
    load_w(w1, 40, wsm1[:])
    sAl = P.sb("sAl", [128, 32], F32); sdtb = P.sb("sdtb", [128, 32], F32); negA1 = P.sb("negA1", [128, 32], F32)
    sD = P.sb("sD", [128, 32, 1], F32)
    sng = P.sb("sng", [128, 512], F32)
    P.dma([(sAl[:], bcast(s_alog[0:1, :])), (sdtb[:], bcast(s_dtb[0:1, :])),
           (sD[:].rearrange("p h o -> p (h o)"), bcast(s_d[0:1, :]))], "c_misc")
    P.actf(negA1[:], sAl[:], AF.Exp)
    P.ts(negA1[:], negA1[:], -1.0, ALU.mult)
    BcT = P.sb("BcT", [128, NT], BF16); CcT = P.sb("CcT", [128, NT], BF16)
    xT = P.sb("xT", [128, 4, NT], BF16)
    wz = P.sb("wz", [128, 8, 512], BF16)
    hT = [P.sb("hTa", [128, 8, 64], F32), P.sb("hTb", [128, 8, 64], F32)]
    hTh = [P.sb("hTha", [128, 512], BF16), P.sb("hThb", [128, 512], BF16)]
    hin = [P.sb("hin0", [128, 4, 128], F32), P.sb("hin1", [128, 4, 128], F32)]
    hld = hin[0]
    sc1s = [P.sb("sc1_%d" % i, [128, 48, 1], F32) for i in range(2)]
    lls = [P.sb("ll_%d" % i, [128, 16, 1], F32) for i in range(2)]
    t8s = [P.sb("t8_%d" % i, [128, 8], F32) for i in range(2)]
    xdt = P.sb("xdt", [128, 8, 64], BF16); xdec = P.sb("xdec", [128, 8, 64], BF16); xtk = P.sb("xtk", [128, 8, 64], BF16)
    Btok = P.sb("Btok", [128, 128], BF16)
    cbT = P.sb("cbT", [128, 128], F32)
    dec1 = [P.sb("dec1_%d" % i, [128, 128], BF16) for i in range(8)]
    mT = [P.sb("mT_%d" % i, [128, 128], BF16) for i in range(8)]
    yis = P.sb("yis", [128, 8, 64], F32); yv = P.sb("yv", [128, 8, 64], F32); zs = P.sb("zs", [128, 512], F32)
    ynb = P.sb("ynb", [128, 512], BF16); oss3 = P.sb("oss3", [128, 1], F32)
    k.s3 = 0

    def bc3(ap_n81, n):
        return ap_n81.to_broadcast([n, 8, 64])

    def ssd_scal(g, c0, n, pb):
        h0 = 8 * g
        sc = sc1s[pb][0:n]; ll = lls[pb]; t8 = t8s[pb]
        for kc in range(8):
            P.mm(pC[0][0:n, 256:264], uT[:, kc, c0:c0 + n], wsm1[:, kc, h0:h0 + 8], start=(kc == 0), stop=(kc == 7))
        yield
        P.tt(t8[0:n, :], pC[0][0:n, 256:264], sdtb[0:n, h0:h0 + 8], ALU.add)
        yield
        P.actf(t8[0:n, :], t8[0:n, :], AF.Exp)
        P.actf(sc[:, 0:8, 0], t8[0:n, :], AF.Ln, bias=1.0)
        yield
        P.tt(sc[:, 8:16, 0], sc[:, 0:8, 0], negA1[0:n, h0:h0 + 8], ALU.mult)
        yield
        P.mm(pC[0][0:n, 288:296], U_f[0:n, 0:n], sc[:, 8:16, 0])
        P.mm(pC[0][:, 296:304], ones_f[0:n, :], sc[:, 8:16, 0])
        yield
        P.cp(sc[:, 16:24, 0], pC[0][0:n, 288:296])
        P.ts(sc[:, 24:32, 0], pC[0][0:n, 288:296], -1.0, ALU.mult)
        P.cp(ll[:, 0:8, 0], pC[0][:, 296:304])
        yield
        P.actf(sc[:, 32:40, 0], pC[0][0:n, 288:296], AF.Exp)
        P.actf(ll[:, 8:16, 0], pC[0][:, 296:304], AF.Exp)
        yield
        P.tt(t8[0:n, :], ll[0:n, 0:8, 0], sc[:, 16:24, 0], ALU.subtract)
        yield
        P.actf(sc[:, 40:48, 0], t8[0:n, :], AF.Exp)
        yield

    def ssd_seg(g, gi, c0, n, first, last, s_src, s_dst, pb):
        if first:
            k.s3 += 1
        hf = hT[k.s3 % 2]; hb = hTh[k.s3 % 2]
        hf2 = hf[:].rearrange("p h q -> p (h q)")
        if first:
            if s_src is None:
                P.memset(hf[:], 0.0)
            else:
                P.dma([(hld[:], s_src.rearrange("(j p) n -> p j n", p=128))], "hld")
                for j in range(4):
                    P.tr(pC[0][:, j * 128:(j + 1) * 128], hld[:, j, :], ident_f[:])
                P.cp(hf2, pC[0][:, :], eng="act")
            P.cp(hb[:], hf2, eng="pool")
        h0 = 8 * g
        sc = sc1s[pb][0:n]; ll = lls[pb]
        yield
        for j in range(4):
            P.tr(pT[0:n, j * 128:(j + 1) * 128], xT[:, j, c0:c0 + n], ident_b[:])
        P.tr(pT[0:n, 512:640], BcT[:, c0:c0 + n], ident_b[:])
        ptx = pT[0:n, 0:512].rearrange("p (h q) -> p h q", h=8)
        P.cp(xtk[0:n], ptx, eng="act")
        P.cp(Btok[0:n, :], pT[0:n, 512:640], eng="act")
        yield
        P.tt(xdt[0:n], xtk[0:n], bc3(sc[:, 0:8, :], n), ALU.mult)
        P.tt(xdec[0:n], xdt[0:n], bc3(sc[:, 40:48, :], n), ALU.mult, eng="pool")
        yield
        P.mm(pC[0][0:n, 0:n], BcT[:, c0:c0 + n], CcT[:, c0:c0 + n])
        P.cp(cbT[0:n, 0:n], pC[0][0:n, 0:n], eng="act")
        P.mm(pC[3][0:n, :], CcT[:, c0:c0 + n], hb[:])
        P.cp(yis[0:n].rearrange("p h q -> p (h q)"), pC[3][0:n, :], eng="act")
        for kc in range(8):
            P.mm(pA[0][0:n, :], uT[:, kc, c0:c0 + n], wz[:, kc, :], start=(kc == 0), stop=(kc == 7))
        P.actf(zs[0:n, :], pA[0][0:n, :], AF.Silu)
        yield
        P.mm(pA[1][:, :], Btok[0:n, :], xdec[0:n].rearrange("p h q -> p (h q)"))
        P.tt(hf[:], hf[:], ll[:, 8:16, :].to_broadcast([128, 8, 64]), ALU.mult)
        P.tt(hf2, hf2, pA[1][:, :], ALU.add)
        if not last:
            P.cp(hb[:], hf2, eng="pool")
        else:
            for j in range(4):
                P.tr(pC[0][:, j * 128:(j + 1) * 128], hf2[:, j * 128:(j + 1) * 128], ident_f[:])
            P.cp(hld[:].rearrange("p j n -> p (j n)"), pC[0][:, :], eng="act")
            P.dma([(s_dst.rearrange("(j p) n -> p j n", p=128), hld[:])], "hst")
        yield
        if n == 1:
            P.cp(mT[0][0:1, 0:1], cbT[0:1, 0:1], eng="pool")
            P.mm(pC[2][0:1, :], mT[0][0:1, 0:1], xdt[0:1].rearrange("p h q -> p (h q)"))
        else:
            for hh in range(8):
                gb = pC[1] if hh < 4 else pS
                r0 = (hh % 4) * 128
                P.mm(gb[:, r0:r0 + 128], sc[:, 8 + hh, :].to_broadcast([128, 128]), U_f[:], start=True, stop=False)
                P.mm(gb[:, r0:r0 + 128], ident_f[:], negmask[:], start=False, stop=True)
            for hh in range(8):
                gb = pC[1] if hh < 4 else pS
                r0 = (hh % 4) * 128
                P.actf(dec1[hh][:], gb[:, r0:r0 + 128], AF.Exp, bias=sc[:, 24 + hh, :])
            for hh in range(8):
                P.tt(mT[hh][:], cbT[:], dec1[hh][:], ALU.mult, eng=("pool" if hh % 2 else "dve"))
            for hh in range(8):
                P.mm(pC[2][:, hh * 64:(hh + 1) * 64], mT[hh][:], xdt[:, hh, :])
        yield
        P.tt(yis[0:n], yis[0:n], bc3(sc[:, 32:40, :], n), ALU.mult)
        P.tt(yv[0:n].rearrange("p h q -> p (h q)"), yis[0:n].rearrange("p h q -> p (h q)"), pC[2][0:n, :], ALU.add)
        P.tt(yis[0:n], xtk[0:n], sD[0:n, h0:h0 + 8, :].to_broadcast([n, 8, 64]), ALU.mult, eng="pool")
        P.tt(yv[0:n], yv[0:n], yis[0:n], ALU.add)
        yv2 = yv[0:n].rearrange("p h q -> p (h q)")
        P.tt(yv2, yv2, zs[0:n, :], ALU.mult)
        P.memset(oss3[0:n, :], 0.0, eng="dve")
        P.actf(ynb[0:n, :], yv2, AF.Square, accum_out=oss3[0:n, :])
        rstd_of(oss3[0:n, :], 1.0 / 512)
        P.stt(ynb[0:n, :], yv2, oss3[0:n, :], sng[0:n, :], ALU.mult, ALU.mult)
        for j in range(4):
            P.tr(pT[:, j * 128:j * 128 + n], ynb[0:n, j * 128:(j + 1) * 128], ident_b[0:n, 0:n])
        P.cp(oT[:, 4 * g:4 * g + 4, c0:c0 + n], pT[:, 0:512].rearrange("p (j t) -> p j t", j=4)[:, :, 0:n], eng="act")

    dcol = P.sb("dcol", [128, 16], F32); ngcol = P.sb("ngcol", [128, 16], F32)
    P.dma([(dcol[:], s_dcol[:]), (ngcol[:], s_ngcol[:])], "c_cols")
    sdt = P.sb("sdt", [16, 32, 1], F32); sea = P.sb("sea", [16, 32, 1], F32)
    for kc in range(8):
        P.mm(pS[0:16, 0:32], uT[:, kc, 2048:2064], wsm1[:, kc, 0:32], start=(kc == 0), stop=(kc == 7))
    sdt2 = sdt[:].rearrange("p h o -> p (h o)"); sea2 = sea[:].rearrange("p h o -> p (h o)")
    P.tt(sdt2, pS[0:16, 0:32], sdtb[0:16, :], ALU.add)
    P.actf(sdt2, sdt2, AF.Exp)
    P.actf(sdt2, sdt2, AF.Ln, bias=1.0)
    P.tt(sea2, sdt2, negA1[0:16, :], ALU.mult)
    P.actf(sea2, sea2, AF.Exp)
    exE = P.sb("exE", [16, 8, 64], F32)
    DTc = P.sb("DTc", [128, 4, 16], F32); EAc = P.sb("EAc", [128, 4, 16], F32); DTX = P.sb("DTX", [128, 4, 16], F32)
    BCt = P.sb("BCt", [16, 256], BF16)
    t1s = yis[:].rearrange("p h q -> p (h q)").rearrange("p (j n) -> p j n", j=4)
    ycol = P.sb("ycol", [128, 4, 16], F32); yz = P.sb("yz", [128, 4, 16], F32); zsT = P.sb("zsT", [128, 4, 16], F32)
    rs16 = P.sb("rs16", [128, 16], F32)

    def ssd_sample(g):
        h0 = 8 * g
        for w, src_ in enumerate((sdt, sea)):
            P.cp(exE[:], src_[:, h0:h0 + 8, :].to_broadcast([16, 8, 64]), eng="pool")
            ex2 = exE[:].rearrange("p h q -> p (h q)")
            for j in range(4):
                P.tr(pS[:, (w * 4 + j) * 16:(w * 4 + j + 1) * 16], ex2[:, j * 128:(j + 1) * 128], ident_f[0:16, 0:16])
        P.cp(DTc[:].rearrange("p j s -> p (j s)"), pS[:, 0:64], eng="act")
        P.cp(EAc[:].rearrange("p j s -> p (j s)"), pS[:, 64:128], eng="act")
        P.tt(DTX[:], xT[:, :, 2048:2064], DTc[:], ALU.mult)
        P.tr(pT[0:16, 0:128], BcT[:, 2048:2064], ident_b[:])
        P.tr(pT[0:16, 128:256], CcT[:, 2048:2064], ident_b[:])
        P.cp(BCt[:], pT[0:16, 0:256], eng="act")
        for j in range(4):
            for kc in range(8):
                P.mm(pA[0][:, j * 16:(j + 1) * 16], wz[:, kc, j * 128:(j + 1) * 128], uT[:, kc, 2048:2064], start=(kc == 0), stop=(kc == 7))
        P.actf(zsT[:].rearrange("p j s -> p (j s)"), pA[0][:, 0:64], AF.Silu)
        for s in range(16):
            hi = hin[s % 2]; ho = hi
            pb = pC[s % 4]
            P.dma([(hi[:], st_ssd[s, g * 512:(g + 1) * 512, :].rearrange("(j p) n -> p j n", p=128))], "hin%d" % (s % 2))
            P.mm(pb[:, 0:256], ident_b[0:16, s:s + 1].to_broadcast([16, 128]), BCt[:])
            P.tt(t1s, pb[:, 0:128].unsqueeze(1).to_broadcast([128, 4, 128]), DTX[:, :, s:s + 1].to_broadcast([128, 4, 128]), ALU.mult)
            P.tt(hi[:], hi[:], EAc[:, :, s:s + 1].to_broadcast([128, 4, 128]), ALU.mult, eng="pool")
            P.tt(ho[:], hi[:], t1s, ALU.add, eng="pool")
            P.dma([(o_ssd_s[s, g * 512:(g + 1) * 512, :].rearrange("(j p) n -> p j n", p=128), ho[:])], "hin%d" % (s % 2))
            P.tt(t1s, ho[:], pb[:, 128:256].unsqueeze(1).to_broadcast([128, 4, 128]), ALU.mult)
            P.red(ycol[:, :, s], t1s)
        for j in range(4):
            P.stt(ycol[:, j, :], xT[:, j, 2048:2064], dcol[:, 4 * g + j:4 * g + j + 1], ycol[:, j, :], ALU.mult, ALU.add)
        P.tt(yz[:], ycol[:], zsT[:], ALU.mult)
        P.tt(ycol[:], yz[:], yz[:], ALU.mult)
        for j in range(4):
            P.mm(pS[:, 256:272], ones_f[:], ycol[:, j, :], start=(j == 0), stop=(j == 3))
        P.actf(rs16[:], pS[:, 256:272], AF.Ln, bias=EPS, scale=1.0 / 512)
        P.actf(rs16[:], rs16[:], AF.Exp, scale=-0.5)
        for j in range(4):
            P.stt(oT[:, 4 * g + j, 2048:2064], yz[:, j, :], ngcol[:, 4 * g + j:4 * g + j + 1], rs16[:], ALU.mult, ALU.mult)

    s_items = []
    for g in range(4):
        s_items += [(w1, 32 + g, None), (w1, 36 + g, None)]
        for j in range(4):
            s_items += [(w1, 16 + 4 * g + j, None), (w1, 4 * g + j, wz[:, :, j * 128:(j + 1) * 128])]
    wq1 = WQ(s_items)
    for g in range(4):
        P.dma([(sng[:], bcast(s_ng[0:1, g * 512:(g + 1) * 512]))], "c_sng")
        conv_tile(wq1, 16 + g, o_sconv_p, o_sconv_s); P.cp(BcT[:], Yc[:], eng="pool")
        conv_tile(wq1, 20 + g, o_sconv_p, o_sconv_s); P.cp(CcT[:], Yc[:], eng="pool")
        for j in range(4):
            conv_tile(wq1, 4 * g + j, o_sconv_p, o_sconv_s); P.cp(xT[:, j, :], Yc[:], eng="pool")
            wq1.get()
        interleave([ssd_scal(g, 0, 128, 0)])
        for gi, (c0, n) in enumerate(SEGS[:16]):
            nxt = ssd_scal(g, SEGS[gi + 1][0], 128, (gi + 1) % 2) if gi < 15 else None
            interleave([ssd_seg(g, gi, c0, n, gi == 0, gi == 15, None, o_ssd_p[g * 512:(g + 1) * 512, :], gi % 2), nxt])
        ssd_sample(g)
    P.pop()
    if stage <= 4:
        print('sbuf_remaining', nc.sbuf_bytes_remaining)
        print('counts', P.cnt, 'waits', P.nwaits, 'dma slots', len(P.dma_cnt))
        P.finish(); return nc, cn

    def l1_dst(ht, r0, n, i):
        if r0 < 2048:
            P.dma([(o_yp[r0:r0 + n, :], ht[0:n, :])], "yst%d" % (i % 2))
        else:
            P.dma([(o_ys[:, :], ht[0:n, :])], "yst%d" % (i % 2))

    out_proj(wo1, o_post_g, lambda r0, n: h1scr[r0:r0 + n, :], l1_dst, None)
    print('sbuf_remaining', nc.sbuf_bytes_remaining)
    print('counts', P.cnt, 'waits', P.nwaits, 'dma slots', len(P.dma_cnt))
    P.finish()
    return nc, cn

def _blocks(Wc):
    nb = Wc.shape[1] // 128
    return Wc.reshape(8, 128, nb, 128).transpose(2, 1, 0, 3).reshape(nb, 128, 1024)


def shared_inputs(inp):
    f = np.float32
    m = {}
    W = inp["e_w_in"][0]
    cols = list(range(0, 4096)) + list(range(4112, 7184))
    sm = np.zeros((1024, 128), f)
    sm[:, 0:16] = W[:, 4096:4112]
    sm[:, 16:32] = W[:, 7184:7200]
    m["w0"] = np.ascontiguousarray(np.concatenate([_blocks(W[:, cols]), _blocks(sm)], 0))
    W1 = inp["o_w_in"][0]
    sm1 = np.zeros((1024, 128), f)
    sm1[:, 0:32] = W1[:, 5120:5152]
    m["w1"] = np.ascontiguousarray(np.concatenate([_blocks(W1[:, 0:5120]), _blocks(sm1)], 0))
    m["wo0"] = np.ascontiguousarray(inp["e_w_out"][0].reshape(16, 128, 1024))
    m["wo1"] = np.ascontiguousarray(inp["o_w_out"][0].reshape(16, 128, 1024))
    for nme in ("e_pre_g", "e_post_g", "o_pre_g", "o_post_g"):
        m[nme] = np.ascontiguousarray(inp[nme].reshape(1, 1024))
    m["e_cw"] = np.ascontiguousarray(inp["e_conv_w"][0].T.reshape(24, 128, 4).transpose(1, 0, 2).reshape(128, 96))
    m["e_cb"] = np.ascontiguousarray(inp["e_conv_b"][0].reshape(24, 128).T)
    m["s_cw"] = np.ascontiguousarray(inp["ssd_conv_w"][0].T.reshape(24, 128, 4).transpose(1, 0, 2).reshape(128, 96))
    m["s_cb"] = np.ascontiguousarray(inp["ssd_conv_b"][0].reshape(24, 128).T)
    m["g_alog"] = np.ascontiguousarray(inp["gdn_a_log"].reshape(1, 8))
    m["g_dtb"] = np.ascontiguousarray(inp["gdn_dt_bias"].reshape(1, 8))
    m["g_ng"] = np.ascontiguousarray(inp["gdn_norm_g"].reshape(128, 1))
    m["l_wlr"] = np.ascontiguousarray(np.concatenate([inp["gla_w_lr"][0], inp["gla_b_lr"][0][None]], 0))
    m["l_ng"] = np.ascontiguousarray(inp["gla_norm_g"].reshape(2, 128).T)
    m["s_alog"] = np.ascontiguousarray(inp["ssd_a_log"].reshape(1, 32))
    m["s_dtb"] = np.ascontiguousarray(inp["ssd_dt_bias"].reshape(1, 32))
    m["s_d"] = np.ascontiguousarray(inp["ssd_d"].reshape(1, 32))
    m["s_ng"] = np.ascontiguousarray(inp["ssd_norm_g"].reshape(1, 2048))
    m["s_dcol"] = np.ascontiguousarray(np.repeat(inp["ssd_d"].reshape(32), 64).reshape(16, 128).T)
    m["s_ngcol"] = np.ascontiguousarray(inp["ssd_norm_g"].reshape(16, 128).T)
    for n_, a in consts_np().items():
        m["c_" + n_] = a
    return m


def make_in_maps(inp):
    sh = shared_inputs(inp)
    in_maps = []
    for c in range(8):
        m = dict(sh)
        sl = slice(c * 16, (c + 1) * 16)
        m["xp"] = np.ascontiguousarray(inp["x_prompt"][c])
        m["xs"] = np.ascontiguousarray(inp["x_sample"][sl, 0])
        m["st_gconv"] = np.ascontiguousarray(inp["state_gdn_conv"][0, sl].reshape(48, 3072))
        m["st_gdn"] = np.ascontiguousarray(inp["state_gdn"][0, sl])
        m["st_gla"] = np.ascontiguousarray(inp["state_gla"][0, sl])
        m["st_sconv"] = np.ascontiguousarray(inp["state_ssd_conv"][0, sl].reshape(48, 3072))
        m["st_ssd"] = np.ascontiguousarray(inp["state_ssd"][0, sl].reshape(16, 2048, 128))
        in_maps.append(m)
    return in_maps


def gather(R):
    f = np.float32
    st = lambda k_: np.stack([np.asarray(R[c][k_], f) for c in range(8)])
    cat = lambda k_: np.concatenate([np.asarray(R[c][k_], f) for c in range(8)])
    return (st("o_yp"), cat("o_ys").reshape(128, 1, 1024),
            st("o_gconv_p")[None], st("o_gdn_p")[None], st("o_gla_p")[None],
            st("o_sconv_p")[None], st("o_ssd_p").reshape(1, 8, 32, 64, 128),
            cat("o_gconv_s").reshape(1, 128, 3, 3072), cat("o_gdn_s")[None], cat("o_gla_s")[None],
            cat("o_sconv_s").reshape(1, 128, 3, 3072), cat("o_ssd_s").reshape(1, 128, 32, 64, 128))


def kernel(**inp):
    inp = {k_: np.asarray(v) for k_, v in inp.items()}
    nc, cn = build()
    res = run_bass_kernel_spmd(nc, make_in_maps(inp), core_ids=list(range(8)))
    return gather(res.results)
```

```python
import numpy as np
from contextlib import ExitStack
import concourse.bass as bass
import concourse.mybir as mybir
from concourse.bass_utils import run_bass_kernel_spmd

F32 = mybir.dt.float32
BF16 = mybir.dt.bfloat16
AF = mybir.ActivationFunctionType
ALU = mybir.AluOpType
AX = mybir.AxisListType
CENGS = ["pe", "act", "dve", "pool"]
ENGS = CENGS + ["sp"]
EPS = 1e-6
NT = 2064
GRPS = [(0, 512), (512, 512), (1024, 512), (1536, 512), (2048, 16)]
SEGS = [(128 * i, 128) for i in range(16)] + [(2048 + s, 1) for s in range(16)]
class Prog:
    def __init__(self, nc):
        self.nc = nc
        self.es = ExitStack()
        self.streams = {e: [] for e in ENGS}
        self.cnt = {e: 0 for e in CENGS}
        self.waited = {e: {} for e in ENGS}
        self.recs = {}
        self.fsz = {}
        self.dma_cnt = {}
        self.nwaits = 0
        self.psum_names = set()
        self.scopes = []

    def sb(self, name, shape, dt):
        st = self.scopes[-1][0] if self.scopes else self.es
        t = st.enter_context(self.nc.sbuf_tensor(name, list(shape), dt))
        self.fsz[name] = int(np.prod(shape[1:]))
        if self.scopes:
            self.scopes[-1][1].append(name)
        return t

    def push(self):
        self.scopes.append((ExitStack(), []))

    def pop(self):
        st, names = self.scopes.pop()
        for e in ENGS:
            waits = {}
            for o in CENGS:
                if o != e and self.cnt[o] > 0 and self.waited[e].get(("E", o), 0) < self.cnt[o]:
                    waits[("E", o)] = self.cnt[o]
            for slot, n in self.dma_cnt.items():
                if self.waited[e].get(("D", slot), 0) < 16 * n:
                    waits[("D", slot)] = 16 * n
            for k_, v in waits.items():
                self.waited[e][k_] = v
            if waits:
                self.streams[e].append((waits, None, None, 0))
        for nm in names:
            self.recs.pop(nm, None)
        st.close()

    def ps(self, name, shape, dt=F32):
        t = self.es.enter_context(self.nc.psum_tensor(name, list(shape), dt))
        self.fsz[name] = int(np.prod(shape[1:]))
        self.psum_names.add(name)
        return t

    def box(self, a):
        name = a.tensor.name
        ap = a.ap
        off = a.offset
        if name in self.fsz:
            F = self.fsz[name]
            p0 = off // F
            f0 = off % F
            ext = sum((c - 1) * abs(s) for s, c in ap[1:])
            return (name, p0, p0 + ap[0][1], f0, f0 + ext + 1)
        ext = sum((c - 1) * abs(s) for s, c in ap)
        return (name, 0, 1, off, off + ext + 1)

    def _deps(self, eng, reads, writes):
        waits = {}
        boxes = []
        for a in reads:
            b = self.box(a)
            if b[0] in self.psum_names:
                boxes.append(((b[0], 0, 128, 0, self.fsz[b[0]]), True))
            else:
                boxes.append((b, False))
        for a in writes:
            b = self.box(a)
            if b[0] in self.psum_names:
                b = (b[0], 0, 128, 0, self.fsz[b[0]])
            boxes.append((b, True))
        for (name, p0, p1, f0, f1), isw in boxes:
            for rec in self.recs.get(name, ()):
                rk, rv, risw, rp0, rp1, rf0, rf1, reng = rec
                if rp0 < p1 and p0 < rp1 and rf0 < f1 and f0 < rf1 and (isw or risw):
                    if reng == eng:
                        if eng == "pe":
                            continue
                        if not (risw and not isw):
                            continue
                    if waits.get(rk, 0) < rv:
                        waits[rk] = rv
        out = {}
        w = self.waited[eng]
        for k, v in waits.items():
            if w.get(k, 0) < v:
                w[k] = v
                out[k] = v
        self.nwaits += len(out)
        return out, boxes

    def _record(self, boxes, semkey, val, eng):
        for (name, p0, p1, f0, f1), isw in boxes:
            lst = self.recs.setdefault(name, [])
            if isw:
                lst[:] = [r for r in lst if not (p0 <= r[3] and r[4] <= p1 and f0 <= r[5] and r[6] <= f1)]
                lst.append((semkey, val, True, p0, p1, f0, f1, eng))
            else:
                for i, r in enumerate(lst):
                    if (not r[2]) and r[7] == eng and r[0] == semkey and r[3:7] == (p0, p1, f0, f1):
                        lst[i] = (semkey, val, False, p0, p1, f0, f1, eng)
                        break
                else:
                    lst.append((semkey, val, False, p0, p1, f0, f1, eng))

    def op(self, eng, fn, reads, writes):
        waits, boxes = self._deps(eng, reads, writes)
        self.cnt[eng] += 1
        val = self.cnt[eng]
        key = ("E", eng)
        self.streams[eng].append((waits, fn, key, 1))
        self._record(boxes, key, val, eng)

    def dma(self, pairs, slot, queue="sp"):
        key = ("D", slot)
        n0 = self.dma_cnt.get(slot, 0)
        final = 16 * (n0 + len(pairs))
        self.dma_cnt[slot] = n0 + len(pairs)
        for o, i in pairs:
            waits, boxes = self._deps(queue, [i], [o])
            self.streams[queue].append(
                (waits, (lambda e, o=o, i=i: e.dma_start(out=o, in_=i)), key, 16))
            self._record(boxes, key, final, "dma")

    def mm(self, out, lhsT, rhs, start=True, stop=True):
        self.op("pe", lambda e: e.matmul(out, lhsT, rhs, start=start, stop=stop), [lhsT, rhs], [out])

    def tr(self, out, in_, ident):
        self.op("pe", lambda e: e.transpose(out, in_, ident), [in_, ident], [out])

    def actf(self, out, in_, func, bias=None, scale=None, eng="act", accum_out=None):
        kw = {}
        rd = [in_]
        wr = [out]
        if bias is not None:
            kw["bias"] = bias
            if not isinstance(bias, (int, float)):
                rd.append(bias)
        if scale is not None:
            kw["scale"] = scale
            if not isinstance(scale, (int, float)):
                rd.append(scale)
        if accum_out is not None:
            kw["accum_out"] = accum_out
            wr.append(accum_out)
        self.op("act", lambda e: e.activation(out, in_, func, **kw), rd, wr)

    def tt(self, out, in0, in1, op, eng="dve"):
        self.op(eng, lambda e: e.tensor_tensor(out, in0, in1, op), [in0, in1], [out])

    def ts(self, out, in0, s1, op0, s2=None, op1=None, eng="dve", accum_out=None):
        rd = [in0] + [s for s in (s1, s2) if s is not None and not isinstance(s, (int, float))]
        wr = [out] + ([accum_out] if accum_out is not None else [])
        kw = {}
        if op1 is not None:
            kw["op1"] = op1
        if accum_out is not None:
            kw["accum_out"] = accum_out
        self.op(eng, lambda e: e.tensor_scalar(out, in0, s1, s2, op0, **kw), rd, wr)

    def stt(self, out, in0, scalar, in1, op0, op1, eng="dve", accum_out=None):
        rd = [in0, in1] + ([scalar] if not isinstance(scalar, (int, float)) else [])
        wr = [out] + ([accum_out] if accum_out is not None else [])
        kw = {}
        if accum_out is not None:
            kw["accum_out"] = accum_out
        self.op(eng, lambda e: e.scalar_tensor_tensor(out, in0, scalar, in1, op0, op1, **kw), rd, wr)

    def cp(self, out, in_, eng="dve"):
        if eng == "act":
            self.op("act", lambda e: e.copy(out, in_), [in_], [out])
        else:
            self.op(eng, lambda e: e.tensor_copy(out, in_), [in_], [out])

    def memset(self, out, val, eng="pool"):
        self.op(eng, lambda e: e.memset(out, val), [], [out])

    def red(self, out, in_, op=None, eng="dve"):
        self.op(eng, lambda e: e.tensor_reduce(out, in_, AX.X, op if op is not None else ALU.add), [in_], [out])

    def recip(self, out, in_):
        self.op("dve", lambda e: e.reciprocal(out, in_), [in_], [out])

    def finish(self):
        for slot, n in self.dma_cnt.items():
            k = ("D", slot)
            v = 16 * n
            if self.waited["sp"].get(k, 0) < v:
                self.waited["sp"][k] = v
                self.streams["sp"].append(({k: v}, None, None, 0))
        nc = self.nc
        keys = [("E", e) for e in CENGS] + [("D", s) for s in self.dma_cnt]
        sems = {}
        for i, k in enumerate(keys):
            sems[k] = self.es.enter_context(nc.semaphore("s%d" % i))
        emap = {"pe": "tensor", "act": "scalar", "dve": "vector", "pool": "gpsimd", "sp": "sync"}

        def replay(name, e):
            for waits, fn, key, inc in self.streams[name]:
                for k, v in waits.items():
                    e.wait_ge(sems[k], v)
                if fn is not None:
                    fn(e).then_inc(sems[key], inc)

        with nc.Block() as block:
            for name in ENGS:
                if not self.streams[name]:
                    continue
                getattr(block, emap[name])(lambda e, name=name: replay(name, e))
        self.es.close()


def consts_np():
    j = np.arange(128)
    c = {}
    c["ident_f"] = np.eye(128, dtype=np.float32)
    c["U_f"] = (j[:, None] <= j[None, :]).astype(np.float32)
    c["Un16_f"] = ((j[:, None] <= j[None, :]).astype(np.float32) / -16.0).astype(np.float32)
    c["negmask_f"] = np.where(j[:, None] <= j[None, :], 0.0, -30000.0).astype(np.float32)
    c["nsmask_f"] = np.where(j[:, None] < j[None, :], -1.0, 0.0).astype(np.float32)
    c["imask_f"] = (j[:, None] <= j[None, :]).astype(np.float32)
    c["ones_f"] = np.ones((128, 128), np.float32)
    c["e16_f"] = np.tile(np.eye(16, dtype=np.float32).reshape(1, 256), (128, 1))
    return c


class K:
    pass


def build(stage=99):
    nc = bass.Bass("TRN2", target_bir_lowering=False)
    P = Prog(nc)
    k = K()

    def din(name, shape):
        return nc.dram_tensor(name, list(shape), F32, kind="ExternalInput").ap()

    def dout(name, shape):
        return nc.dram_tensor(name, list(shape), F32, kind="ExternalOutput").ap()

    def bcast(ap1):
        return ap1.partition_broadcast(128).rearrange("p o f -> p (o f)")

    xp = din("xp", [2048, 1024]); xs = din("xs", [16, 1024])
    w0 = din("w0", [57, 128, 1024]); w1 = din("w1", [41, 128, 1024])
    wo0 = din("wo0", [16, 128, 1024]); wo1 = din("wo1", [16, 128, 1024])
    e_pre_g = din("e_pre_g", [1, 1024]); e_post_g = din("e_post_g", [1, 1024])
    o_pre_g = din("o_pre_g", [1, 1024]); o_post_g = din("o_post_g", [1, 1024])
    e_cw = din("e_cw", [128, 96]); e_cb = din("e_cb", [128, 24])
    s_cw = din("s_cw", [128, 96]); s_cb = din("s_cb", [128, 24])
    g_alog = din("g_alog", [1, 8]); g_dtb = din("g_dtb", [1, 8]); g_ng = din("g_ng", [128, 1])
    l_wlr = din("l_wlr", [17, 512]); l_ng = din("l_ng", [128, 2])
    s_alog = din("s_alog", [1, 32]); s_dtb = din("s_dtb", [1, 32]); s_d = din("s_d", [1, 32])
    s_ng = din("s_ng", [1, 2048])
    s_dcol = din("s_dcol", [128, 16]); s_ngcol = din("s_ngcol", [128, 16])
    st_gconv = din("st_gconv", [48, 3072]); st_gdn = din("st_gdn", [16, 8, 128, 128])
    st_gla = din("st_gla", [16, 4, 128, 256])
    st_sconv = din("st_sconv", [48, 3072]); st_ssd = din("st_ssd", [16, 32 * 64, 128])
    cn = consts_np()
    cd = {n: din("c_" + n, list(a.shape)) for n, a in cn.items()}
    h1scr = nc.dram_tensor("h1scr", [NT, 1024], F32, kind="Internal").ap()

    o_yp = dout("o_yp", [2048, 1024]); o_ys = dout("o_ys", [16, 1024])
    o_gconv_p = dout("o_gconv_p", [3, 3072]); o_gdn_p = dout("o_gdn_p", [8, 128, 128])
    o_gla_p = dout("o_gla_p", [4, 128, 256])
    o_sconv_p = dout("o_sconv_p", [3, 3072]); o_ssd_p = dout("o_ssd_p", [32 * 64, 128])
    o_gconv_s = dout("o_gconv_s", [48, 3072]); o_gdn_s = dout("o_gdn_s", [16, 8, 128, 128])
    o_gla_s = dout("o_gla_s", [16, 4, 128, 256])
    o_sconv_s = dout("o_sconv_s", [48, 3072]); o_ssd_s = dout("o_ssd_s", [16, 32 * 64, 128])

    ident_f = P.sb("ident_f", [128, 128], F32); U_f = P.sb("U_f", [128, 128], F32)
    Un16 = P.sb("Un16_f", [128, 128], F32)
    negmask = P.sb("negmask_f", [128, 128], F32); nsmask = P.sb("nsmask_f", [128, 128], F32)
    imask = P.sb("imask_f", [128, 128], F32)
    ones_f = P.sb("ones_f", [128, 128], F32)
    ident_b = P.sb("ident_b", [128, 128], BF16); ones_b = P.sb("ones_b", [128, 128], BF16)
    E16 = P.sb("e16_f", [128, 16, 16], F32)
    P.dma([(E16[:].rearrange("p a b -> p (a b)"), cd["e16_f"][:])], "c_e16")
    for t, n in [(ident_f, "ident_f"), (U_f, "U_f"), (Un16, "Un16_f"), (negmask, "negmask_f"), (nsmask, "nsmask_f"),
                 (imask, "imask_f"), (ones_f, "ones_f")]:
        P.dma([(t[:], cd[n][:])], "c_" + n)
    P.cp(ident_b[:], ident_f[:], eng="pool"); P.cp(ones_b[:], ones_f[:], eng="pool")

    cw = P.sb("cw", [128, 96], F32); cb = P.sb("cb", [128, 24], F32)
    P.dma([(cw[:], e_cw[:]), (cb[:], e_cb[:])], "c_cw")

    pA = [P.ps("pA0", [128, 512]), P.ps("pA1", [128, 512])]
    pT = P.ps("pT", [128, 1024], BF16)
    pS = P.ps("pS", [128, 512])
    pC = [P.ps("pC%d" % i, [128, 512]) for i in range(4)]

    uT = P.sb("uT", [128, 8, NT], BF16)
    oT = P.sb("oT", [128, 16, NT], BF16)
    ub = [P.sb("ub0", [128, 1024], BF16), P.sb("ub1", [128, 1024], BF16)]
    ss = P.sb("ss", [128, 4], F32)
    wst = [P.sb("wst0", [128, 1024], F32), P.sb("wst1", [128, 1024], F32)]
    wbf = [P.sb("wbf0", [128, 8, 128], BF16), P.sb("wbf1", [128, 8, 128], BF16)]
    k.ui = 0

    def rstd_of(s_, scale):
        P.actf(s_, s_, AF.Ln, bias=EPS, scale=scale)
        P.actf(s_, s_, AF.Exp, scale=-0.5)

    def norm_to_uT(x_, n, col0, gbc):
        i = k.ui % 2; k.ui += 1
        u_ = ub[i]
        s_ = ss[0:n, i:i + 1]
        P.memset(s_, 0.0, eng="dve")
        P.actf(u_[0:n, :], x_[0:n, :], AF.Square, accum_out=s_)
        rstd_of(s_, 1.0 / 1024)
        P.stt(u_[0:n, :], x_[0:n, :], s_, gbc[0:n, :], ALU.mult, ALU.mult)
        for kc in range(8):
            P.tr(pT[:, kc * 128:kc * 128 + n], u_[0:n, kc * 128:(kc + 1) * 128], ident_b[0:n, 0:n])
        P.cp(uT[:, :, col0:col0 + n], pT[:].rearrange("p (k t) -> p k t", k=8)[:, :, 0:n], eng="act")

    ROWT = [(xp[i * 128:(i + 1) * 128, :], 128, i * 128) for i in range(16)] + [(xs[:, :], 16, 2048)]
    P.push()
    xt = [P.sb("xt0", [128, 1024], F32), P.sb("xt1", [128, 1024], F32)]
    gbc0 = P.sb("gbc0", [128, 1024], F32)
    P.dma([(gbc0[:], bcast(e_pre_g[0:1, :]))], "c_gbc")
    for i, (src, n, col0) in enumerate(ROWT):
        P.dma([(xt[i % 2][0:n, :], src)], "xt%d" % (i % 2))
        norm_to_uT(xt[i % 2], n, col0, gbc0)
    P.pop()

    if stage <= -3:
        P.finish(); return nc, cn
    k.wi = 0

    def load_w(wsrc, blk, dst=None):
        i = k.wi % 2; k.wi += 1
        P.dma([(wst[i][:], wsrc[blk])], "wst%d" % i)
        if dst is None:
            dst = wbf[i][:]
        P.cp(dst, wst[i][:].rearrange("p (k c) -> p k c", k=8), eng="pool")
        return wbf[i]

    class WQ:
        def __init__(self, items):
            self.items = list(items); self.nd = 0; self.nc_ = 0; self.dq = []; self.cq = []

        def _dma(self):
            if self.nd < len(self.items):
                wsrc, blk, dst = self.items[self.nd]; self.nd += 1
                bi = k.wi % 2; k.wi += 1
                P.dma([(wst[bi][:], wsrc[blk])], "wst%d" % bi)
                self.dq.append((bi, dst))

        def _cast(self):
            if self.dq:
                bi, dst = self.dq.pop(0)
                d_ = wbf[bi][:] if dst is None else dst
                P.cp(d_, wst[bi][:].rearrange("p (k c) -> p k c", k=8), eng="pool")
                self.cq.append(wbf[bi])

        def get(self):
            if not self.cq:
                self._dma(); self._cast()
            cur = self.cq.pop(0)
            if not self.dq:
                self._dma()
            self._cast()
            self._dma()
            return cur

    k.pi = 0

    def proj_fm(wb, evac, M=128, c0=0):
        for (t0, n) in GRPS:
            pa = pA[k.pi % 2]; k.pi += 1
            for kc in range(8):
                P.mm(pa[0:M, 0:n], wb[:, kc, c0:c0 + M], uT[:, kc, t0:t0 + n], start=(kc == 0), stop=(kc == 7))
            evac(pa[0:M, 0:n], t0, n)

    def load_hist(hist_, src):
        for pc in range(3):
            hb = wst[k.wi % 2]
            P.dma([(hb[0:48, :], src[:, pc * 1024:(pc + 1) * 1024])], "wst%d" % (k.wi % 2))
            k.wi += 1
            for c8 in range(8):
                ct = pc * 8 + c8
                P.tr(pS[:, 256:304], hb[0:48, c8 * 128:(c8 + 1) * 128], ident_f[0:48, 0:48])
                P.cp(hist_[:, ct, :], pS[:, 256:304])

    hist = P.sb("hist", [128, 24, 48], F32)
    Xc = P.sb("Xc", [128, 3 + 2048], F32)
    Xs = P.sb("Xs", [128, 16, 4], F32)
    Yc = P.sb("Yc", [128, NT], F32)
    nst = P.sb("nst", [128, 51], F32)
    nso = [P.sb("nso0", [51, 128], F32), P.sb("nso1", [51, 128], F32)]
    k.ni = 0
    P.memset(Xc[:, 0:3], 0.0)

    def conv_tile(wq, ct, ocp, ocs):
        wb = wq.get()

        def ev(pa, t0, n):
            if t0 < 2048:
                P.cp(Xc[:, 3 + t0:3 + t0 + n], pa, eng="act")
            else:
                P.cp(Xs[:, :, 3], pa, eng="act")
        proj_fm(wb, ev)
        P.cp(Xs[:, :, 0:3], hist[:, ct, :].rearrange("p (s r) -> p s r", r=3), eng="pool")
        for (xin, yout) in [(lambda i: Xc[:, i:i + 2048], Yc[:, 0:2048]), (lambda i: Xs[:, :, i], Yc[:, 2048:2064])]:
            P.ts(yout, xin(0), cw[:, ct * 4:ct * 4 + 1], ALU.mult, cb[:, ct:ct + 1], ALU.add)
            for i in (1, 2, 3):
                P.stt(yout, xin(i), cw[:, ct * 4 + i:ct * 4 + i + 1], yout, ALU.mult, ALU.add)
        P.actf(Yc[:], Yc[:], AF.Silu)
        P.cp(nst[:, 0:3], Xc[:, 2048:2051], eng="pool")
        P.cp(nst[:, 3:51].rearrange("p (s r) -> p s r", r=3), Xs[:, :, 1:4], eng="pool")
        P.tr(pS[0:51, 384:512], nst[:], ident_f[:])
        no = nso[k.ni % 2]
        P.cp(no[:], pS[0:51, 384:512])
        P.dma([(ocp[:, ct * 128:(ct + 1) * 128], no[0:3, :]), (ocs[:, ct * 128:(ct + 1) * 128], no[3:51, :])], "nso%d" % (k.ni % 2))
        k.ni += 1

    load_hist(hist, st_gconv)

    P.push()
    lrT = P.sb("lrT", [17, NT], BF16)
    P.push()
    wsm = P.sb("wsm", [128, 8, 128], BF16)
    load_w(w0, 56, wsm[:])
    alog_bc = P.sb("alog_bc", [128, 8], F32); dtb_bc = P.sb("dtb_bc", [128, 8], F32)
    negA = P.sb("negA", [128, 8], F32); ng_col = P.sb("ng_col", [128, 1], F32)
    P.dma([(ng_col[:], g_ng[:]), (alog_bc[:], bcast(g_alog[0:1, :])), (dtb_bc[:], bcast(g_dtb[0:1, :]))], "c_misc")
    P.actf(negA[:], alog_bc[:], AF.Exp)
    P.ts(negA[:], negA[:], -1.0, ALU.mult)
    SC = P.sb("SC", [128, 16, 64], F32)
    SCs = P.sb("SCs", [16, 32], F32)
    GL = P.sb("GL", [128, 16, 16], F32)
    tmp8 = P.sb("tmp8", [128, 16], F32)
    for kc in range(8):
        P.mm(pS[0:16, 0:32], uT[:, kc, 2048:2064], wsm[:, kc, 0:32], start=(kc == 0), stop=(kc == 7))
    P.actf(SCs[:, 0:8], pS[0:16, 0:8], AF.Sigmoid)
    P.tt(tmp8[0:16, 0:8], pS[0:16, 8:16], dtb_bc[0:16, :], ALU.add)
    P.actf(tmp8[0:16, 0:8], tmp8[0:16, 0:8], AF.Exp)
    P.actf(tmp8[0:16, 0:8], tmp8[0:16, 0:8], AF.Ln, bias=1.0)
    P.tt(SCs[:, 8:16], tmp8[0:16, 0:8], negA[0:16, :], ALU.mult)
    P.actf(SCs[:, 16:24], SCs[:, 8:16], AF.Exp)
    P.ts(SCs[:, 24:32], SCs[:, 16:24], -1.0, ALU.mult)
    for g, (c0, n) in enumerate(SEGS[:16]):
        ps = pS[0:n, 0:32]
        for kc in range(8):
            P.mm(ps, uT[:, kc, c0:c0 + n], wsm[:, kc, 0:32], start=(kc == 0), stop=(kc == 7))
        sc = SC[0:n, g, :]
        P.actf(sc[:, 0:8], ps[:, 0:8], AF.Sigmoid)
        P.tt(tmp8[0:n, 0:8], ps[:, 8:16], dtb_bc[0:n, :], ALU.add)
        P.actf(tmp8[0:n, 0:8], tmp8[0:n, 0:8], AF.Exp)
        P.actf(tmp8[0:n, 0:8], tmp8[0:n, 0:8], AF.Ln, bias=1.0)
        P.tt(sc[:, 8:16], tmp8[0:n, 0:8], negA[0:n, :], ALU.mult)
        P.mm(pS[0:n, 32:40], U_f[0:n, 0:n], sc[:, 8:16])
        P.mm(pS[:, 40:48], ones_f[0:n, :], sc[:, 8:16])
        P.cp(sc[:, 16:24], pS[0:n, 32:40])
        P.ts(sc[:, 24:32], pS[0:n, 32:40], -1.0, ALU.mult)
        P.actf(sc[:, 32:40], pS[0:n, 32:40], AF.Exp)
        P.ts(sc[:, 40:48], sc[:, 32:40], -1.0, ALU.mult)
        P.cp(GL[:, g, 8:16], pS[:, 40:48])
        P.actf(GL[:, g, 0:8], pS[:, 40:48], AF.Exp)
        P.tt(tmp8[0:n, 8:16], GL[0:n, g, 8:16], sc[:, 16:24], ALU.subtract)
        P.actf(sc[:, 48:56], tmp8[0:n, 8:16], AF.Exp)
    P.memset(lrT[:], 1.0)
    for (t0, n) in GRPS:
        for kc in range(8):
            P.mm(pS[0:16, 0:n], wsm[:, kc, 16:32], uT[:, kc, t0:t0 + n], start=(kc == 0), stop=(kc == 7))
        P.cp(lrT[0:16, t0:t0 + n], pS[0:16, 0:n], eng="act")

    sq = P.sb("sq", [128, NT], BF16)
    rn = P.sb("rn", [128, 512], F32)

    def l2norm_to(dsts, scale):
        P.tt(sq[:], Yc[:], Yc[:], ALU.mult, eng="pool")
        for (t0, n) in GRPS:
            P.mm(pS[:, 0:n], ones_b[:], sq[:, t0:t0 + n])
            P.actf(rn[:, 0:n], pS[:, 0:n], AF.Ln, bias=EPS)
            P.actf(rn[:, 0:n], rn[:, 0:n], AF.Exp, scale=-0.5)
            for d_ in dsts:
                P.stt(d_[:, t0:t0 + n], Yc[:, t0:t0 + n], scale, rn[:, 0:n], ALU.mult, ALU.mult)

    qT = P.sb("qT", [128, NT], BF16); vT = P.sb("vT", [128, NT], BF16)
    kT32 = P.sb("kT32", [128, NT], F32)
    gsT = P.sb("gsT", [128, NT], BF16)
    S = [P.sb("S0", [128, 128], F32), P.sb("S1", [128, 128], F32)]
    Sb = [P.sb("Sb0", [128, 128], BF16), P.sb("Sb1", [128, 128], BF16)]
    vtok = [P.sb("vtk_%d" % i, [128, 128], BF16) for i in range(3)]; kdtok = [P.sb("kdtk_%d" % i, [128, 128], F32) for i in range(3)]
    decTs = [P.sb("decT%d" % i, [128, 128], F32) for i in range(2)]; pTm = [P.sb("pTmm_%d" % i, [128, 128], F32) for i in range(3)]
    Mis = [P.sb("Mi%d" % i, [128, 128], F32) for i in range(2)]; kcbs = [P.sb("kcb%d" % i, [128, 128], BF16) for i in range(2)]
    Abs = [[P.sb("Ab%d_%d" % (t, i), [128, 128], F32) for i in range(2)] for t in range(2)]
    ATbs = [[P.sb("ATb%d_%d" % (t, i), [128, 128], F32) for i in range(2)] for t in range(2)]
    P32 = [P.sb("P32_%d" % i, [128, 128], F32) for i in range(3)]
    R32 = P.sb("R32", [128, 128], F32); u32 = P.sb("u32", [128, 128], F32)
    o2s = P.sb("o2s", [128, 128], F32); osb = P.sb("osb", [128, 128], F32); onb = P.sb("onb", [128, 128], BF16)
    oss = P.sb("oss", [128, 1], F32)
    k.si = 0

    def interleave(gens):
        gens = [g_ for g_ in gens if g_ is not None]
        while gens:
            for g_ in list(gens):
                try:
                    next(g_)
                except StopIteration:
                    gens.remove(g_)

    def gdn_prep(h, g, c0, n, pb, tb):
        sc = SC[0:n, g, :]
        Ab = Abs[tb]; ATb = ATbs[tb]; decT = decTs[tb]; Mi = Mis[tb]; kcb = kcbs[tb]
        if tb == 0:
            r_kk = pC[0][:, 0:128]; r_qk = pC[0][:, 128:256]; r_G = pC[0][:, 256:384]
            r_s0 = pC[1][:, 0:128]; r_s1 = pC[1][:, 128:256]; r_P = pA[0][:, 0:128]
            r_kt = pA[1][:, 0:128]; r_vt = pT[:, 0:128]
        else:
            r_kk = pC[1][:, 256:384]; r_qk = pC[1][:, 384:512]; r_G = pA[0][:, 128:256]
            r_s0 = pA[1][:, 128:256]; r_s1 = pA[1][:, 256:384]; r_P = pA[0][:, 256:384]
            r_kt = pA[1][:, 384:512]; r_vt = pT[:, 128:256]
        qc_ = qT[:, c0:c0 + n]; k32 = kT32[:, c0:c0 + n]
        kc_ = kcb[:, 0:n]
        P.cp(kc_, k32, eng="pool")
        P.tr(r_vt[0:n, :], vT[:, c0:c0 + n], ident_b[:])
        P.cp(vtok[pb][0:n, :], r_vt[0:n, :], eng="act")
        P.tr(r_kt[0:n, :], k32, ident_f[:])
        P.actf(kdtok[pb][0:n, :], r_kt[0:n, :], AF.Copy, scale=sc[:, 48 + h:49 + h])
        yield
        P.mm(r_qk[0:n, 0:n], kc_, qc_)
        P.mm(r_G[0:n, 0:n], sc[:, 8 + h:9 + h].to_broadcast([n, n]), U_f[0:n, 0:n], start=True, stop=False)
        P.mm(r_G[0:n, 0:n], ident_f[0:n, 0:n], negmask[0:n, 0:n], start=False, stop=True)
        if n > 1:
            P.mm(r_kk, kc_, kc_)
        yield
        P.actf(decT[0:n, 0:n], r_G[0:n, 0:n], AF.Exp, bias=sc[:, 24 + h:25 + h])
        P.tt(pTm[pb][0:n, 0:n], r_qk[0:n, 0:n], decT[0:n, 0:n], ALU.mult)
        yield
        if n > 1:
            Pf = P32[pb]
            P.stt(Mi[:], r_kk, sc[:, h:h + 1], decT[:], ALU.mult, ALU.mult)
            P.tt(Ab[0][:], Mi[:], nsmask[:], ALU.mult, eng="pool")
            yield
            P.tr(r_s0, Ab[0][:], ident_f[:])
            P.cp(ATb[0][:], r_s0, eng="act")
            P.tt(Pf[:], Ab[0][:], ident_f[:], ALU.add, eng="pool")
            yield
            cur = 0
            for it in range(6):
                nx = 1 - cur
                P.mm(r_s0, Ab[cur][:], ATb[cur][:])
                if it < 5:
                    P.mm(r_s1, ATb[cur][:], Ab[cur][:])
                yield
                P.cp(ATb[nx][:], r_s0, eng="act")
                if it < 5:
                    P.cp(Ab[nx][:], r_s1, eng="dve")
                yield
                P.mm(r_P, ATb[nx][:], Pf[:])
                yield
                P.tt(Pf[:], r_P, Pf[:], ALU.add)
                yield
                cur = nx

    def gdn_seq(h, g, c0, n, first, last, s_src, s_dst, pb):
        sc = SC[0:n, g, :]
        if first:
            k.si += 1
        Sf = S[k.si % 2]; Sbb = Sb[k.si % 2]
        if first:
            if s_src is None:
                P.memset(Sf[:], 0.0)
            else:
                P.dma([(Sf[:], s_src)], "Sld%d" % (k.si % 2))
            P.cp(Sbb[:], Sf[:], eng="pool")
            yield
        qc_ = qT[:, c0:c0 + n]; k32 = kT32[:, c0:c0 + n]
        Pfin = P32[pb] if n > 1 else ident_f
        P.mm(pC[3][0:n, 0:128], k32, Sf[:])
        yield
        P.stt(R32[0:n, :], pC[3][0:n, 0:128], sc[:, 40 + h:41 + h], vtok[pb][0:n, :], ALU.mult, ALU.add)
        yield
        P.mm(pC[2][0:n, 0:128], Pfin[0:n, 0:n], R32[0:n, :])
        yield
        P.actf(u32[0:n, :], pC[2][0:n, 0:128], AF.Copy, scale=sc[:, h:h + 1])
        yield
        P.mm(pC[3][:, 128:256], kdtok[pb][0:n, :], u32[0:n, :])
        P.mm(pC[2][0:n, 256:384], qc_, Sbb[:])
        P.mm(pC[2][0:n, 384:512], pTm[pb][0:n, 0:n], u32[0:n, :])
        yield
        P.stt(Sf[:], Sf[:], GL[:, g, h:h + 1], pC[3][:, 128:256], ALU.mult, ALU.add)
        yield
        if not last:
            P.cp(Sbb[:], Sf[:], eng="pool")
        else:
            P.dma([(s_dst, Sf[:])], "Sst%d" % (k.si % 2))
        P.cp(o2s[0:n, :], pC[2][0:n, 384:512], eng="act")
        yield
        P.stt(osb[0:n, :], pC[2][0:n, 256:384], sc[:, 32 + h:33 + h], o2s[0:n, :], ALU.mult, ALU.add)
        P.memset(oss[0:n, :], 0.0, eng="dve")
        yield
        P.actf(onb[0:n, :], osb[0:n, :], AF.Square, accum_out=oss[0:n, :])
        rstd_of(oss[0:n, :], 1.0 / 128)
        P.actf(osb[0:n, :], osb[0:n, :], AF.Copy, scale=oss[0:n, :])
        yield
        P.tr(pS[:, 0:n], osb[0:n, :], ident_f[0:n, 0:n])
        yield
        P.tt(oT[:, h, c0:c0 + n], pS[:, 0:n], gsT[:, c0:c0 + n], ALU.mult)
        yield

    def gdn_head_segs(h):
        segl = []
        for g, (c0, n) in enumerate(SEGS[:16]):
            segl.append((g, c0, n, g == 0, g == 15, None, o_gdn_p[h]))
        preps = [gdn_prep(h, sg[0], sg[1], sg[2], i % 3, i % 2) for i, sg in enumerate(segl)]
        done = [False] * len(segl)

        def step(i):
            if i < len(segl) and not done[i]:
                try:
                    next(preps[i])
                except StopIteration:
                    done[i] = True

        while not done[0]:
            step(0)
        for i, (g, c0, n, fi, la, src, dst) in enumerate(segl):
            sq_ = gdn_seq(h, g, c0, n, fi, la, src, dst, i % 3)
            sq_done = False
            while not (sq_done and (i + 1 >= len(segl) or done[i + 1])):
                if not sq_done:
                    try:
                        next(sq_)
                    except StopIteration:
                        sq_done = True
                step(i + 1)
                step(i + 2)

    Sall = P.sb("Sall", [128, 16, 128], F32)
    Km = P.sb("Km", [128, 16, 16], F32); Qm = P.sb("Qm", [128, 16, 16], F32); q32c = P.sb("q32c", [128, 16], F32)
    vts = P.sb("vts", [16, 128], BF16); Rs = P.sb("Rs", [16, 128], F32); us = P.sb("us", [16, 128], F32)
    dg = P.sb("dg", [16, 16], F32); egbc = P.sb("egbc", [128, 16, 1], F32)
    osbs = P.sb("osbs", [16, 128], F32); onbs = P.sb("onbs", [16, 128], F32); osss = P.sb("osss", [16, 1], F32)

    def gdn_sample(h):
        kc32 = kT32[:, 2048:2064]
        P.cp(q32c[:], qT[:, 2048:2064], eng="pool")
        P.tt(Km[:], kc32.unsqueeze(2).to_broadcast([128, 16, 16]), E16[:], ALU.mult, eng="pool")
        P.tt(Qm[:], q32c[:].unsqueeze(2).to_broadcast([128, 16, 16]), E16[:], ALU.mult, eng="pool")
        P.tr(pT[0:16, 0:128], vT[:, 2048:2064], ident_b[:])
        P.cp(vts[:], pT[0:16, 0:128], eng="act")
        P.ts(dg[:], ident_f[0:16, 0:16], SCs[:, 16 + h:17 + h], ALU.mult)
        P.mm(pC[1][:, 0:16], ones_f[0:16, :], dg[:])
        P.cp(egbc[:].rearrange("p s o -> p (s o)"), pC[1][:, 0:16], eng="act")
        for s in range(16):
            P.mm(pC[0][0:16, 0:128], Km[:, s, :], Sall[:, s, :], start=(s == 0), stop=(s == 15))
        P.stt(Rs[:], pC[0][0:16, 0:128], SCs[:, 24 + h:25 + h], vts[:], ALU.mult, ALU.add)
        P.ts(us[:], Rs[:], SCs[:, h:h + 1], ALU.mult)
        P.tt(Sall[:], Sall[:], egbc[:].to_broadcast([128, 16, 128]), ALU.mult)
        for s in range(16):
            bk = pC[2] if (s // 4) % 2 == 0 else pC[3]
            r0 = (s % 4) * 128
            P.mm(bk[:, r0:r0 + 128], ident_f[0:16, s:s + 1].to_broadcast([16, 128]), us[:])
            P.stt(Sall[:, s, :], bk[:, r0:r0 + 128], kc32[:, s:s + 1], Sall[:, s, :], ALU.mult, ALU.add)
        for s in range(16):
            P.mm(pC[0][0:16, 128:256], Qm[:, s, :], Sall[:, s, :], start=(s == 0), stop=(s == 15))
        P.dma([(o_gdn_s[:, h].rearrange("s d e -> d s e"), Sall[:])], "Sall")
        P.cp(osbs[:], pC[0][0:16, 128:256], eng="act")
        P.memset(osss[:], 0.0, eng="dve")
        P.actf(onbs[:], osbs[:], AF.Square, accum_out=osss[:])
        rstd_of(osss[:], 1.0 / 128)
        P.actf(onbs[:], osbs[:], AF.Copy, scale=osss[:])
        P.tr(pS[:, 0:16], onbs[:], ident_f[0:16, 0:16])
        P.tt(oT[:, h, 2048:2064], pS[:, 0:16], gsT[:, 2048:2064], ALU.mult)

    wq0 = WQ([(w0, b_, None) for h in range(8) for b_ in (h, 8 + h, 16 + h, 24 + h)])
    for h in range(8 if stage >= 1 else 0):
        conv_tile(wq0, h, o_gconv_p, o_gconv_s); l2norm_to([qT], 128 ** -0.5)
        conv_tile(wq0, 8 + h, o_gconv_p, o_gconv_s); l2norm_to([kT32], 1.0)
        conv_tile(wq0, 16 + h, o_gconv_p, o_gconv_s); P.cp(vT[:], Yc[:], eng="pool")
        wb = wq0.get()
        proj_fm(wb, lambda pa, t0, n: P.actf(Yc[:, t0:t0 + n], pa, AF.Silu))
        P.ts(gsT[:], Yc[:], ng_col[:], ALU.mult)
        P.dma([(Sall[:], st_gdn[:, h].rearrange("s d e -> d s e"))], "Sall")
        gdn_head_segs(h)
        gdn_sample(h)
    P.pop()
    if stage <= 1:
        P.pop()
        print('sbuf_remaining', nc.sbuf_bytes_remaining)
        print('counts', P.cnt, 'waits', P.nwaits, 'dma slots', len(P.dma_cnt))
        P.finish(); return nc, cn

    P.push()
    wlr = P.sb("wlr", [17, 512], BF16)
    P.dma([(wst[k.wi % 2][0:17, 0:512], l_wlr[:])], "wst%d" % (k.wi % 2))
    P.cp(wlr[:], wst[k.wi % 2][0:17, 0:512], eng="pool")
    k.wi += 1
    lng = P.sb("lng", [128, 2], F32)
    P.dma([(lng[:], l_ng[:])], "c_misc")
    lqT = P.sb("lqT", [128, NT], BF16); lkT = P.sb("lkT", [128, NT], BF16)
    gs2 = P.sb("gs2", [128, 2, NT], BF16)
    wv = P.sb("wv", [128, 8, 256], BF16)
    S2 = [P.sb("S2a", [128, 256], F32), P.sb("S2b", [128, 256], F32)]
    S2h = [P.sb("S2ha", [128, 256], BF16), P.sb("S2hb", [128, 256], BF16)]
    sp32 = P.sb("sp32", [128, 128], F32)
    ebt = P.sb("ebt", [128, 128], F32); enbt = P.sb("enbt", [128, 128], F32); ekdt = P.sb("ekdt", [128, 128], F32)
    bl = P.sb("bl", [128, 2], F32)
    qeb = P.sb("qeb", [128, 128], BF16); keb = P.sb("keb", [128, 128], BF16); kdTb = P.sb("kdTb", [128, 128], BF16)
    kdtok2 = P.sb("kdtok2", [128, 128], BF16); pTm2 = P.sb("pTm2", [128, 128], BF16)
    vtok2 = P.sb("vtok2", [128, 256], BF16)
    osb2 = P.sb("osb2", [128, 256], F32); onb2 = P.sb("onb2", [128, 256], BF16); oss2 = P.sb("oss2", [128, 1], F32)
    k.s2 = 0

    def gla_seg(h, g, c0, n, first, last, s_src, s_dst):
        if first:
            k.s2 += 1
        Sf = S2[k.s2 % 2]; Sbb = S2h[k.s2 % 2]
        if first:
            if s_src is None:
                P.memset(Sf[:], 0.0)
            else:
                P.dma([(Sf[:], s_src)], "S2ld%d" % (k.s2 % 2))
            P.cp(Sbb[:], Sf[:], eng="pool")
        P.mm(pS[0:n, 0:128], lrT[0:17, c0:c0 + n], wlr[0:17, h * 128:(h + 1) * 128])
        P.actf(sp32[0:n, :], pS[0:n, 0:128], AF.Exp, scale=-1.0)
        P.actf(sp32[0:n, :], sp32[0:n, :], AF.Ln, bias=1.0)
        P.mm(pC[0][:, 0:n], sp32[0:n, :], Un16[0:n, 0:n])
        P.actf(ebt[:, 0:n], pC[0][:, 0:n], AF.Exp)
        P.actf(enbt[:, 0:n], pC[0][:, 0:n], AF.Exp, scale=-1.0)
        P.cp(bl[:, 0:1], pC[0][:, n - 1:n])
        P.actf(ekdt[:, 0:n], pC[0][:, 0:n], AF.Exp, scale=-1.0, bias=bl[:, 0:1])
        P.actf(bl[:, 1:2], bl[:, 0:1], AF.Exp)
        P.stt(qeb[:, 0:n], lqT[:, c0:c0 + n], 128 ** -0.5, ebt[:, 0:n], ALU.mult, ALU.mult)
        P.tt(keb[:, 0:n], lkT[:, c0:c0 + n], enbt[:, 0:n], ALU.mult)
        P.tt(kdTb[:, 0:n], lkT[:, c0:c0 + n], ekdt[:, 0:n], ALU.mult, eng="pool")
        P.tr(pT[0:n, 0:128], kdTb[:, 0:n], ident_b[:])
        P.cp(kdtok2[0:n, :], pT[0:n, 0:128], eng="act")
        P.mm(pC[1][0:n, 0:n], keb[:, 0:n], qeb[:, 0:n])
        P.tt(pTm2[0:n, 0:n], pC[1][0:n, 0:n], imask[0:n, 0:n], ALU.mult)
        for kc in range(8):
            P.mm(pA[0][0:n, 0:256], uT[:, kc, c0:c0 + n], wv[:, kc, :], start=(kc == 0), stop=(kc == 7))
        P.cp(vtok2[0:n, :], pA[0][0:n, 0:256], eng="act")
        P.mm(pC[2][0:n, 0:256], qeb[:, 0:n], Sbb[:], start=True, stop=False)
        P.mm(pC[2][0:n, 0:256], pTm2[0:n, 0:n], vtok2[0:n, :], start=False, stop=True)
        P.mm(pC[3][:, 0:256], kdtok2[0:n, :], vtok2[0:n, :])
        P.stt(Sf[:], Sf[:], bl[:, 1:2], pC[3][:, 0:256], ALU.mult, ALU.add)
        if not last:
            P.cp(Sbb[:], Sf[:], eng="pool")
        else:
            P.dma([(s_dst, Sf[:])], "S2st%d" % (k.s2 % 2))
        P.cp(osb2[0:n, :], pC[2][0:n, 0:256], eng="act")
        P.memset(oss2[0:n, :], 0.0, eng="dve")
        P.actf(onb2[0:n, :], osb2[0:n, :], AF.Square, accum_out=oss2[0:n, :])
        rstd_of(oss2[0:n, :], 1.0 / 256)
        P.actf(onb2[0:n, :], osb2[0:n, :], AF.Copy, scale=oss2[0:n, :])
        for j in range(2):
            P.tr(pT[:, 256 + j * 128:256 + j * 128 + n], onb2[0:n, j * 128:(j + 1) * 128], ident_b[0:n, 0:n])
        for j in range(2):
            P.tt(oT[:, 8 + 2 * h + j, c0:c0 + n], pT[:, 256 + j * 128:256 + j * 128 + n], gs2[:, j, c0:c0 + n], ALU.mult)

    Sgs = [P.sb("Sg0", [128, 8, 256], F32), P.sb("Sg1", [128, 8, 256], F32)]
    Qm2 = P.sb("Qm2", [128, 16, 16], F32); q32g = P.sb("q32g", [128, 16], F32); k32g = P.sb("k32g", [128, 16], F32)
    aTs = P.sb("aTs", [128, 16, 1], F32); sps = P.sb("sps", [16, 128], F32)
    vts2 = P.sb("vts2", [16, 256], BF16)
    osg = P.sb("osg", [16, 256], F32); ong = P.sb("ong", [16, 256], F32); ossg = P.sb("ossg", [16, 1], F32)

    def gla_sample(h):
        P.mm(pS[0:16, 0:128], lrT[0:17, 2048:2064], wlr[0:17, h * 128:(h + 1) * 128])
        P.actf(sps[:], pS[0:16, 0:128], AF.Exp, scale=-1.0)
        P.actf(sps[:], sps[:], AF.Ln, bias=1.0)
        P.actf(sps[:], sps[:], AF.Exp, scale=-1.0 / 16)
        P.tr(pS[:, 128:144], sps[:], ident_f[0:16, 0:16])
        P.cp(aTs[:].rearrange("p s o -> p (s o)"), pS[:, 128:144], eng="act")
        P.cp(k32g[:], lkT[:, 2048:2064], eng="pool")
        P.ts(q32g[:], lqT[:, 2048:2064], 128 ** -0.5, ALU.mult)
        P.tt(Qm2[:], q32g[:].unsqueeze(2).to_broadcast([128, 16, 16]), E16[:], ALU.mult, eng="pool")
        for kc in range(8):
            P.mm(pA[0][0:16, 0:256], uT[:, kc, 2048:2064], wv[:, kc, :], start=(kc == 0), stop=(kc == 7))
        P.cp(vts2[:], pA[0][0:16, 0:256], eng="act")
        for hf in range(2):
            Sg = Sgs[hf]
            for s8 in range(8):
                s = hf * 8 + s8
                bk = pC[2] if s % 2 == 0 else pC[3]
                P.mm(bk[:, 0:256], ident_b[0:16, s:s + 1].to_broadcast([16, 128]), vts2[:])
                P.tt(Sg[:, s8, :], Sg[:, s8, :], aTs[:, s, :].to_broadcast([128, 256]), ALU.mult, eng="pool")
                P.stt(Sg[:, s8, :], bk[:, 0:256], k32g[:, s:s + 1], Sg[:, s8, :], ALU.mult, ALU.add)
            for s8 in range(8):
                s = hf * 8 + s8
                P.mm(pC[0][0:16, 0:256], Qm2[:, s, :], Sg[:, s8, :], start=(s == 0), stop=(s == 15), )
            P.dma([(o_gla_s[hf * 8:(hf + 1) * 8, h].rearrange("s d e -> d s e"), Sg[:])], "Sg%d" % hf)
        P.cp(osg[:], pC[0][0:16, 0:256], eng="act")
        P.memset(ossg[:], 0.0, eng="dve")
        P.actf(ong[:], osg[:], AF.Square, accum_out=ossg[:])
        rstd_of(ossg[:], 1.0 / 256)
        P.actf(ong[:], osg[:], AF.Copy, scale=ossg[:])
        for j in range(2):
            P.tr(pS[:, 256 + j * 16:256 + (j + 1) * 16], ong[:, j * 128:(j + 1) * 128], ident_f[0:16, 0:16])
        for j in range(2):
            P.tt(oT[:, 8 + 2 * h + j, 2048:2064], pS[:, 256 + j * 16:256 + (j + 1) * 16], gs2[:, j, 2048:2064], ALU.mult)

    gl_items = []
    for h in range(4):
        gl_items += [(w0, 32 + h, None), (w0, 36 + h, None)]
        for j in range(2):
            gl_items += [(w0, 48 + 2 * h + j, None), (w0, 40 + 2 * h + j, wv[:, :, j * 128:(j + 1) * 128])]
    wqg = WQ(gl_items)
    for h in range(4):
        proj_fm(wqg.get(), lambda pa, t0, n: P.cp(lqT[:, t0:t0 + n], pa, eng="act"))
        proj_fm(wqg.get(), lambda pa, t0, n: P.cp(lkT[:, t0:t0 + n], pa, eng="act"))
        for j in range(2):
            proj_fm(wqg.get(), lambda pa, t0, n: P.actf(Yc[:, t0:t0 + n], pa, AF.Silu))
            P.ts(gs2[:, j, :], Yc[:], lng[:, j:j + 1], ALU.mult)
            wqg.get()
        for hf in range(2):
            P.dma([(Sgs[hf][:], st_gla[hf * 8:(hf + 1) * 8, h].rearrange("s d e -> d s e"))], "Sg%d" % hf)
        for g, (c0, n) in enumerate(SEGS[:16]):
            gla_seg(h, g, c0, n, g == 0, g == 15, None, o_gla_p[h])
        gla_sample(h)
    P.pop()
    P.pop()
    if stage <= 2:
        print('sbuf_remaining', nc.sbuf_bytes_remaining)
        print('counts', P.cnt, 'waits', P.nwaits, 'dma slots', len(P.dma_cnt))
        P.finish(); return nc, cn

    def out_proj(wo_src, post_g_src, res_src, dst_fn, next_pre_g):
        P.push()
        tag = "L%d" % k.ui
        wo = P.sb("wo" + tag, [128, 16, 1024], BF16)
        gpo = P.sb("gpo" + tag, [128, 1024], F32)
        h1t = [P.sb("h1a" + tag, [128, 1024], F32), P.sb("h1b" + tag, [128, 1024], F32)]
        xt = [P.sb("xta" + tag, [128, 1024], F32), P.sb("xtb" + tag, [128, 1024], F32)]
        ss2 = P.sb("ss2" + tag, [128, 4], F32)
        junk2 = [P.sb("jka" + tag, [128, 512], BF16), P.sb("jkb" + tag, [128, 512], BF16)]
        P.dma([(gpo[:], bcast(post_g_src[0:1, :]))], "c_gpo")
        if next_pre_g is not None:
            gbc = P.sb("gbc" + tag, [128, 1024], F32)
            P.dma([(gbc[:], bcast(next_pre_g[0:1, :]))], "c_gbc")
        for f in range(16):
            i = k.wi % 2; k.wi += 1
            P.dma([(wst[i][:], wo_src[f])], "wst%d" % i)
            P.cp(wo[:, f, :], wst[i][:], eng="pool")
        rows = [(i * 128, 128, i * 128) for i in range(16)] + [(2048, 16, 2048)]
        for i, (r0, n, col0) in enumerate(rows):
            ht = h1t[i % 2]; xr = xt[i % 2]
            P.dma([(xr[0:n, :], res_src(r0, n))], "xr%d" % (i % 2))
            ssx = ss2[0:n, 2 * (i % 2):2 * (i % 2) + 2]
            pb2 = (pA[0], pA[1]) if i % 2 == 0 else (pC[0], pC[1])
            P.memset(ssx, 0.0, eng="dve")
            for hf in range(2):
                for f in range(16):
                    P.mm(pb2[hf][0:n, :], oT[:, f, col0:col0 + n], wo[:, f, hf * 512:(hf + 1) * 512], start=(f == 0), stop=(f == 15))
                P.actf(junk2[i % 2][0:n, :], pb2[hf][0:n, :], AF.Square, accum_out=ssx[:, hf:hf + 1])
            P.tt(ssx[:, 0:1], ssx[:, 0:1], ssx[:, 1:2], ALU.add)
            rstd_of(ssx[:, 0:1], 1.0 / 1024)
            for hf in range(2):
                sl = slice(hf * 512, (hf + 1) * 512)
                P.stt(ht[0:n, sl], pb2[hf][0:n, :], ssx[:, 0:1], gpo[0:n, sl], ALU.mult, ALU.mult)
            P.tt(ht[0:n, :], ht[0:n, :], xr[0:n, :], ALU.add)
            dst_fn(ht, r0, n, i)
            if next_pre_g is not None:
                norm_to_uT(ht, n, col0, gbc)
        P.pop()

    def l0_dst(ht, r0, n, i):
        P.dma([(h1scr[r0:r0 + n, :], ht[0:n, :])], "h1st%d" % (i % 2))

    out_proj(wo0, e_post_g, lambda r0, n: (xp[r0:r0 + n, :] if r0 < 2048 else xs[:, :]), l0_dst, o_pre_g)
    if stage <= 3:
        print('sbuf_remaining', nc.sbuf_bytes_remaining)
        print('counts', P.cnt, 'waits', P.nwaits, 'dma slots', len(P.dma_cnt))
        P.finish(); return nc, cn

    P.dma([(cw[:], s_cw[:]), (cb[:], s_cb[:])], "c_cw")
    load_hist(hist, st_sconv)
    P.push()
    wsm1 = P.sb("wsm1", [128, 8, 128], BF16)
    load_w(w1, 40, wsm1[:])
    sAl = P.sb("sAl", [128, 32], F32); sdtb = P.sb("sdtb", [128, 32], F32); negA1 = P.sb("negA1", [128, 32], F32)
    sD = P.sb("sD", [128, 32, 1], F32)
    sng = P.sb("sng", [128, 512], F32)
    P.dma([(sAl[:], bcast(s_alog[0:1, :])), (sdtb[:], bcast(s_dtb[0:1, :])),
           (sD[:].rearrange("p h o -> p (h o)"), bcast(s_d[0:1, :]))], "c_misc")
    P.actf(negA1[:], sAl[:], AF.Exp)
    P.ts(negA1[:], negA1[:], -1.0, ALU.mult)
    BcT = P.sb("BcT", [128, NT], BF16); CcT = P.sb("CcT", [128, NT], BF16)
    xT = P.sb("xT", [128, 4, NT], BF16)
    wz = P.sb("wz", [128, 8, 512], BF16)
    hT = [P.sb("hTa", [128, 8, 64], F32), P.sb("hTb", [128, 8, 64], F32)]
    hTh = [P.sb("hTha", [128, 512], BF16), P.sb("hThb", [128, 512], BF16)]
    hin = [P.sb("hin0", [128, 4, 128], F32), P.sb("hin1", [128, 4, 128], F32)]
    hld = hin[0]
    sc1s = [P.sb("sc1_%d" % i, [128, 48, 1], F32) for i in range(2)]
    lls = [P.sb("ll_%d" % i, [128, 16, 1], F32) for i in range(2)]
    t8s = [P.sb("t8_%d" % i, [128, 8], F32) for i in range(2)]
    xdt = P.sb("xdt", [128, 8, 64], BF16); xdec = P.sb("xdec", [128, 8, 64], BF16); xtk = P.sb("xtk", [128, 8, 64], BF16)
    Btok = P.sb("Btok", [128, 128], BF16)
    cbT = P.sb("cbT", [128, 128], F32)
    dec1 = [P.sb("dec1_%d" % i, [128, 128], BF16) for i in range(8)]
    mT = [P.sb("mT_%d" % i, [128, 128], BF16) for i in range(8)]
    yis = P.sb("yis", [128, 8, 64], F32); yv = P.sb("yv", [128, 8, 64], F32); zs = P.sb("zs", [128, 512], F32)
    ynb = P.sb("ynb", [128, 512], BF16); oss3 = P.sb("oss3", [128, 1], F32)
    k.s3 = 0

    def bc3(ap_n81, n):
        return ap_n81.to_broadcast([n, 8, 64])

    def ssd_scal(g, c0, n, pb):
        h0 = 8 * g
        sc = sc1s[pb][0:n]; ll = lls[pb]; t8 = t8s[pb]
        for kc in range(8):
            P.mm(pC[0][0:n, 256:264], uT[:, kc, c0:c0 + n], wsm1[:, kc, h0:h0 + 8], start=(kc == 0), stop=(kc == 7))
        yield
        P.tt(t8[0:n, :], pC[0][0:n, 256:264], sdtb[0:n, h0:h0 + 8], ALU.add)
        yield
        P.actf(t8[0:n, :], t8[0:n, :], AF.Exp)
        P.actf(sc[:, 0:8, 0], t8[0:n, :], AF.Ln, bias=1.0)
        yield
        P.tt(sc[:, 8:16, 0], sc[:, 0:8, 0], negA1[0:n, h0:h0 + 8], ALU.mult)
        yield
        P.mm(pC[0][0:n, 288:296], U_f[0:n, 0:n], sc[:, 8:16, 0])
        P.mm(pC[0][:, 296:304], ones_f[0:n, :], sc[:, 8:16, 0])
        yield
        P.cp(sc[:, 16:24, 0], pC[0][0:n, 288:296])
        P.ts(sc[:, 24:32, 0], pC[0][0:n, 288:296], -1.0, ALU.mult)
        P.cp(ll[:, 0:8, 0], pC[0][:, 296:304])
        yield
        P.actf(sc[:, 32:40, 0], pC[0][0:n, 288:296], AF.Exp)
        P.actf(ll[:, 8:16, 0], pC[0][:, 296:304], AF.Exp)
        yield
        P.tt(t8[0:n, :], ll[0:n, 0:8, 0], sc[:, 16:24, 0], ALU.subtract)
        yield
        P.actf(sc[:, 40:48, 0], t8[0:n, :], AF.Exp)
        yield

    def ssd_seg(g, gi, c0, n, first, last, s_src, s_dst, pb):
        if first:
            k.s3 += 1
        hf = hT[k.s3 % 2]; hb = hTh[k.s3 % 2]
        hf2 = hf[:].rearrange("p h q -> p (h q)")
        if first:
            if s_src is None:
                P.memset(hf[:], 0.0)
            else:
                P.dma([(hld[:], s_src.rearrange("(j p) n -> p j n", p=128))], "hld")
                for j in range(4):
                    P.tr(pC[0][:, j * 128:(j + 1) * 128], hld[:, j, :], ident_f[:])
                P.cp(hf2, pC[0][:, :], eng="act")
            P.cp(hb[:], hf2, eng="pool")
        h0 = 8 * g
        sc = sc1s[pb][0:n]; ll = lls[pb]
        yield
        for j in range(4):
            P.tr(pT[0:n, j * 128:(j + 1) * 128], xT[:, j, c0:c0 + n], ident_b[:])
        P.tr(pT[0:n, 512:640], BcT[:, c0:c0 + n], ident_b[:])
        ptx = pT[0:n, 0:512].rearrange("p (h q) -> p h q", h=8)
        P.cp(xtk[0:n], ptx, eng="act")
        P.cp(Btok[0:n, :], pT[0:n, 512:640], eng="act")
        yield
        P.tt(xdt[0:n], xtk[0:n], bc3(sc[:, 0:8, :], n), ALU.mult)
        P.tt(xdec[0:n], xdt[0:n], bc3(sc[:, 40:48, :], n), ALU.mult, eng="pool")
        yield
        P.mm(pC[0][0:n, 0:n], BcT[:, c0:c0 + n], CcT[:, c0:c0 + n])
        P.cp(cbT[0:n, 0:n], pC[0][0:n, 0:n], eng="act")
        P.mm(pC[3][0:n, :], CcT[:, c0:c0 + n], hb[:])
        P.cp(yis[0:n].rearrange("p h q -> p (h q)"), pC[3][0:n, :], eng="act")
        for kc in range(8):
            P.mm(pA[0][0:n, :], uT[:, kc, c0:c0 + n], wz[:, kc, :], start=(kc == 0), stop=(kc == 7))
        P.actf(zs[0:n, :], pA[0][0:n, :], AF.Silu)
        yield
        P.mm(pA[1][:, :], Btok[0:n, :], xdec[0:n].rearrange("p h q -> p (h q)"))
        P.tt(hf[:], hf[:], ll[:, 8:16, :].to_broadcast([128, 8, 64]), ALU.mult)
        P.tt(hf2, hf2, pA[1][:, :], ALU.add)
        if not last:
            P.cp(hb[:], hf2, eng="pool")
        else:
            for j in range(4):
                P.tr(pC[0][:, j * 128:(j + 1) * 128], hf2[:, j * 128:(j + 1) * 128], ident_f[:])
            P.cp(hld[:].rearrange("p j n -> p (j n)"), pC[0][:, :], eng="act")
            P.dma([(s_dst.rearrange("(j p) n -> p j n", p=128), hld[:])], "hst")
        yield
        if n == 1:
            P.cp(mT[0][0:1, 0:1], cbT[0:1, 0:1], eng="pool")
            P.mm(pC[2][0:1, :], mT[0][0:1, 0:1], xdt[0:1].rearrange("p h q -> p (h q)"))
        else:
            for hh in range(8):
                gb = pC[1] if hh < 4 else pS
                r0 = (hh % 4) * 128
                P.mm(gb[:, r0:r0 + 128], sc[:, 8 + hh, :].to_broadcast([128, 128]), U_f[:], start=True, stop=False)
                P.mm(gb[:, r0:r0 + 128], ident_f[:], negmask[:], start=False, stop=True)
            for hh in range(8):
                gb = pC[1] if hh < 4 else pS
                r0 = (hh % 4) * 128
                P.actf(dec1[hh][:], gb[:, r0:r0 + 128], AF.Exp, bias=sc[:, 24 + hh, :])
            for hh in range(8):
                P.tt(mT[hh][:], cbT[:], dec1[hh][:], ALU.mult, eng=("pool" if hh % 2 else "dve"))
            for hh in range(8):
                P.mm(pC[2][:, hh * 64:(hh + 1) * 64], mT[hh][:], xdt[:, hh, :])
        yield
        P.tt(yis[0:n], yis[0:n], bc3(sc[:, 32:40, :], n), ALU.mult)
        P.tt(yv[0:n].rearrange("p h q -> p (h q)"), yis[0:n].rearrange("p h q -> p (h q)"), pC[2][0:n, :], ALU.add)
        P.tt(yis[0:n], xtk[0:n], sD[0:n, h0:h0 + 8, :].to_broadcast([n, 8, 64]), ALU.mult, eng="pool")
        P.tt(yv[0:n], yv[0:n], yis[0:n], ALU.add)
        yv2 = yv[0:n].rearrange("p h q -> p (h q)")
        P.tt(yv2, yv2, zs[0:n, :], ALU.mult)
        P.memset(oss3[0:n, :], 0.0, eng="dve")
        P.actf(ynb[0:n, :], yv2, AF.Square, accum_out=oss3[0:n, :])
        rstd_of(oss3[0:n, :], 1.0 / 512)
        P.stt(ynb[0:n, :], yv2, oss3[0:n, :], sng[0:n, :], ALU.mult, ALU.mult)
        for j in range(4):
            P.tr(pT[:, j * 128:j * 128 + n], ynb[0:n, j * 128:(j + 1) * 128], ident_b[0:n, 0:n])
        P.cp(oT[:, 4 * g:4 * g + 4, c0:c0 + n], pT[:, 0:512].rearrange("p (j t) -> p j t", j=4)[:, :, 0:n], eng="act")

    dcol = P.sb("dcol", [128, 16], F32); ngcol = P.sb("ngcol", [128, 16], F32)
    P.dma([(dcol[:], s_dcol[:]), (ngcol[:], s_ngcol[:])], "c_cols")
    sdt = P.sb("sdt", [16, 32, 1], F32); sea = P.sb("sea", [16, 32, 1], F32)
    for kc in range(8):
        P.mm(pS[0:16, 0:32], uT[:, kc, 2048:2064], wsm1[:, kc, 0:32], start=(kc == 0), stop=(kc == 7))
    sdt2 = sdt[:].rearrange("p h o -> p (h o)"); sea2 = sea[:].rearrange("p h o -> p (h o)")
    P.tt(sdt2, pS[0:16, 0:32], sdtb[0:16, :], ALU.add)
    P.actf(sdt2, sdt2, AF.Exp)
    P.actf(sdt2, sdt2, AF.Ln, bias=1.0)
    P.tt(sea2, sdt2, negA1[0:16, :], ALU.mult)
    P.actf(sea2, sea2, AF.Exp)
    exE = P.sb("exE", [16, 8, 64], F32)
    DTc = P.sb("DTc", [128, 4, 16], F32); EAc = P.sb("EAc", [128, 4, 16], F32); DTX = P.sb("DTX", [128, 4, 16], F32)
    BCt = P.sb("BCt", [16, 256], BF16)
    t1s = yis[:].rearrange("p h q -> p (h q)").rearrange("p (j n) -> p j n", j=4)
    ycol = P.sb("ycol", [128, 4, 16], F32); yz = P.sb("yz", [128, 4, 16], F32); zsT = P.sb("zsT", [128, 4, 16], F32)
    rs16 = P.sb("rs16", [128, 16], F32)

    def ssd_sample(g):
        h0 = 8 * g
        for w, src_ in enumerate((sdt, sea)):
            P.cp(exE[:], src_[:, h0:h0 + 8, :].to_broadcast([16, 8, 64]), eng="pool")
            ex2 = exE[:].rearrange("p h q -> p (h q)")
            for j in range(4):
                P.tr(pS[:, (w * 4 + j) * 16:(w * 4 + j + 1) * 16], ex2[:, j * 128:(j + 1) * 128], ident_f[0:16, 0:16])
        P.cp(DTc[:].rearrange("p j s -> p (j s)"), pS[:, 0:64], eng="act")
        P.cp(EAc[:].rearrange("p j s -> p (j s)"), pS[:, 64:128], eng="act")
        P.tt(DTX[:], xT[:, :, 2048:2064], DTc[:], ALU.mult)
        P.tr(pT[0:16, 0:128], BcT[:, 2048:2064], ident_b[:])
        P.tr(pT[0:16, 128:256], CcT[:, 2048:2064], ident_b[:])
        P.cp(BCt[:], pT[0:16, 0:256], eng="act")
        for j in range(4):
            for kc in range(8):
                P.mm(pA[0][:, j * 16:(j + 1) * 16], wz[:, kc, j * 128:(j + 1) * 128], uT[:, kc, 2048:2064], start=(kc == 0), stop=(kc == 7))
        P.actf(zsT[:].rearrange("p j s -> p (j s)"), pA[0][:, 0:64], AF.Silu)
        for s in range(16):
            hi = hin[s % 2]; ho = hi
            pb = pC[s % 4]
            P.dma([(hi[:], st_ssd[s, g * 512:(g + 1) * 512, :].rearrange("(j p) n -> p j n", p=128))], "hin%d" % (s % 2))
            P.mm(pb[:, 0:256], ident_b[0:16, s:s + 1].to_broadcast([16, 128]), BCt[:])
            P.tt(t1s, pb[:, 0:128].unsqueeze(1).to_broadcast([128, 4, 128]), DTX[:, :, s:s + 1].to_broadcast([128, 4, 128]), ALU.mult)
            P.tt(hi[:], hi[:], EAc[:, :, s:s + 1].to_broadcast([128, 4, 128]), ALU.mult, eng="pool")
            P.tt(ho[:], hi[:], t1s, ALU.add, eng="pool")
            P.dma([(o_ssd_s[s, g * 512:(g + 1) * 512, :].rearrange("(j p) n -> p j n", p=128), ho[:])], "hin%d" % (s % 2))
            P.tt(t1s, ho[:], pb[:, 128:256].unsqueeze(1).to_broadcast([128, 4, 128]), ALU.mult)
            P.red(ycol[:, :, s], t1s)
        for j in range(4):
            P.stt(ycol[:, j, :], xT[:, j, 2048:2064], dcol[:, 4 * g + j:4 * g + j + 1], ycol[:, j, :], ALU.mult, ALU.add)
        P.tt(yz[:], ycol[:], zsT[:], ALU.mult)
        P.tt(ycol[:], yz[:], yz[:], ALU.mult)
        for j in range(4):
            P.mm(pS[:, 256:272], ones_f[:], ycol[:, j, :], start=(j == 0), stop=(j == 3))
        P.actf(rs16[:], pS[:, 256:272], AF.Ln, bias=EPS, scale=1.0 / 512)
        P.actf(rs16[:], rs16[:], AF.Exp, scale=-0.5)
        for j in range(4):
            P.stt(oT[:, 4 * g + j, 2048:2064], yz[:, j, :], ngcol[:, 4 * g + j:4 * g + j + 1], rs16[:], ALU.mult, ALU.mult)

    s_items = []
    for g in range(4):
        s_items += [(w1, 32 + g, None), (w1, 36 + g, None)]
        for j in range(4):
            s_items += [(w1, 16 + 4 * g + j, None), (w1, 4 * g + j, wz[:, :, j * 128:(j + 1) * 128])]
    wq1 = WQ(s_items)
    for g in range(4):
        P.dma([(sng[:], bcast(s_ng[0:1, g * 512:(g + 1) * 512]))], "c_sng")
        conv_tile(wq1, 16 + g, o_sconv_p, o_sconv_s); P.cp(BcT[:], Yc[:], eng="pool")
        conv_tile(wq1, 20 + g, o_sconv_p, o_sconv_s); P.cp(CcT[:], Yc[:], eng="pool")
        for j in range(4):
            conv_tile(wq1, 4 * g + j, o_sconv_p, o_sconv_s); P.cp(xT[:, j, :], Yc[:], eng="pool")
            wq1.get()
        interleave([ssd_scal(g, 0, 128, 0)])
        for gi, (c0, n) in enumerate(SEGS[:16]):
            nxt = ssd_scal(g, SEGS[gi + 1][0], 128, (gi + 1) % 2) if gi < 15 else None
            interleave([ssd_seg(g, gi, c0, n, gi == 0, gi == 15, None, o_ssd_p[g * 512:(g + 1) * 512, :], gi % 2), nxt])
        ssd_sample(g)
    P.pop()
    if stage <= 4:
        print('sbuf_remaining', nc.sbuf_bytes_remaining)
        print('counts', P.cnt, 'waits', P.nwaits, 'dma slots', len(P.dma_cnt))
        P.finish(); return nc, cn

    def l1_dst(ht, r0, n, i):
        if r0 < 2048:
            P.dma([(o_yp[r0:r0 + n, :], ht[0:n, :])], "yst%d" % (i % 2))
        else:
            P.dma([(o_ys[:, :], ht[0:n, :])], "yst%d" % (i % 2))

    out_proj(wo1, o_post_g, lambda r0, n: h1scr[r0:r0 + n, :], l1_dst, None)
    print('sbuf_remaining', nc.sbuf_bytes_remaining)
    print('counts', P.cnt, 'waits', P.nwaits, 'dma slots', len(P.dma_cnt))
    P.finish()
    return nc, cn

def _blocks(Wc):
    nb = Wc.shape[1] // 128
    return Wc.reshape(8, 128, nb, 128).transpose(2, 1, 0, 3).reshape(nb, 128, 1024)


def shared_inputs(inp):
    f = np.float32
    m = {}
    W = inp["e_w_in"][0]
    cols = list(range(0, 4096)) + list(range(4112, 7184))
    sm = np.zeros((1024, 128), f)
    sm[:, 0:16] = W[:, 4096:4112]
    sm[:, 16:32] = W[:, 7184:7200]
    m["w0"] = np.ascontiguousarray(np.concatenate([_blocks(W[:, cols]), _blocks(sm)], 0))
    W1 = inp["o_w_in"][0]
    sm1 = np.zeros((1024, 128), f)
    sm1[:, 0:32] = W1[:, 5120:5152]
    m["w1"] = np.ascontiguousarray(np.concatenate([_blocks(W1[:, 0:5120]), _blocks(sm1)], 0))
    m["wo0"] = np.ascontiguousarray(inp["e_w_out"][0].reshape(16, 128, 1024))
    m["wo1"] = np.ascontiguousarray(inp["o_w_out"][0].reshape(16, 128, 1024))
    for nme in ("e_pre_g", "e_post_g", "o_pre_g", "o_post_g"):
        m[nme] = np.ascontiguousarray(inp[nme].reshape(1, 1024))
    m["e_cw"] = np.ascontiguousarray(inp["e_conv_w"][0].T.reshape(24, 128, 4).transpose(1, 0, 2).reshape(128, 96))
    m["e_cb"] = np.ascontiguousarray(inp["e_conv_b"][0].reshape(24, 128).T)
    m["s_cw"] = np.ascontiguousarray(inp["ssd_conv_w"][0].T.reshape(24, 128, 4).transpose(1, 0, 2).reshape(128, 96))
    m["s_cb"] = np.ascontiguousarray(inp["ssd_conv_b"][0].reshape(24, 128).T)
    m["g_alog"] = np.ascontiguousarray(inp["gdn_a_log"].reshape(1, 8))
    m["g_dtb"] = np.ascontiguousarray(inp["gdn_dt_bias"].reshape(1, 8))
    m["g_ng"] = np.ascontiguousarray(inp["gdn_norm_g"].reshape(128, 1))
    m["l_wlr"] = np.ascontiguousarray(np.concatenate([inp["gla_w_lr"][0], inp["gla_b_lr"][0][None]], 0))
    m["l_ng"] = np.ascontiguousarray(inp["gla_norm_g"].reshape(2, 128).T)
    m["s_alog"] = np.ascontiguousarray(inp["ssd_a_log"].reshape(1, 32))
    m["s_dtb"] = np.ascontiguousarray(inp["ssd_dt_bias"].reshape(1, 32))
    m["s_d"] = np.ascontiguousarray(inp["ssd_d"].reshape(1, 32))
    m["s_ng"] = np.ascontiguousarray(inp["ssd_norm_g"].reshape(1, 2048))
    m["s_dcol"] = np.ascontiguousarray(np.repeat(inp["ssd_d"].reshape(32), 64).reshape(16, 128).T)
    m["s_ngcol"] = np.ascontiguousarray(inp["ssd_norm_g"].reshape(16, 128).T)
    for n_, a in consts_np().items():
        m["c_" + n_] = a
    return m


def make_in_maps(inp):
    sh = shared_inputs(inp)
    in_maps = []
    for c in range(8):
        m = dict(sh)
        sl = slice(c * 16, (c + 1) * 16)
        m["xp"] = np.ascontiguousarray(inp["x_prompt"][c])
        m["xs"] = np.ascontiguousarray(inp["x_sample"][sl, 0])
        m["st_gconv"] = np.ascontiguousarray(inp["state_gdn_conv"][0, sl].reshape(48, 3072))
        m["st_gdn"] = np.ascontiguousarray(inp["state_gdn"][0, sl])
        m["st_gla"] = np.ascontiguousarray(inp["state_gla"][0, sl])
        m["st_sconv"] = np.ascontiguousarray(inp["state_ssd_conv"][0, sl].reshape(48, 3072))
        m["st_ssd"] = np.ascontiguousarray(inp["state_ssd"][0, sl].reshape(16, 2048, 128))
        in_maps.append(m)
    return in_maps


def gather(R):
    f = np.float32
    st = lambda k_: np.stack([np.asarray(R[c][k_], f) for c in range(8)])
    cat = lambda k_: np.concatenate([np.asarray(R[c][k_], f) for c in range(8)])
    return (st("o_yp"), cat("o_ys").reshape(128, 1, 1024),
            st("o_gconv_p")[None], st("o_gdn_p")[None], st("o_gla_p")[None],
            st("o_sconv_p")[None], st("o_ssd_p").reshape(1, 8, 32, 64, 128),
            cat("o_gconv_s").reshape(1, 128, 3, 3072), cat("o_gdn_s")[None], cat("o_gla_s")[None],
            cat("o_sconv_s").reshape(1, 128, 3, 3072), cat("o_ssd_s").reshape(1, 128, 32, 64, 128))


def kernel(**inp):
    inp = {k_: np.asarray(v) for k_, v in inp.items()}
    nc, cn = build()
    res = run_bass_kernel_spmd(nc, make_in_maps(inp), core_ids=list(range(8)))
    return gather(res.results)
```

```python
import numpy as np
from contextlib import ExitStack
import concourse.bass as bass
import concourse.mybir as mybir
from concourse.bass_utils import run_bass_kernel_spmd

F32 = mybir.dt.float32
BF16 = mybir.dt.bfloat16
AF = mybir.ActivationFunctionType
ALU = mybir.AluOpType
AX = mybir.AxisListType
CENGS = ["pe", "act", "dve", "pool"]
ENGS = CENGS + ["sp"]
EPS = 1e-6
NT = 2064
GRPS = [(0, 512), (512, 512), (1024, 512), (1536, 512), (2048, 16)]
SEGS = [(128 * i, 128) for i in range(16)] + [(2048 + s, 1) for s in range(16)]
class Prog:
    def __init__(self, nc):
        self.nc = nc
        self.es = ExitStack()
        self.streams = {e: [] for e in ENGS}
        self.cnt = {e: 0 for e in CENGS}
        self.waited = {e: {} for e in ENGS}
        self.recs = {}
        self.fsz = {}
        self.dma_cnt = {}
        self.nwaits = 0
        self.psum_names = set()
        self.scopes = []

    def sb(self, name, shape, dt):
        st = self.scopes[-1][0] if self.scopes else self.es
        t = st.enter_context(self.nc.sbuf_tensor(name, list(shape), dt))
        self.fsz[name] = int(np.prod(shape[1:]))
        if self.scopes:
            self.scopes[-1][1].append(name)
        return t

    def push(self):
        self.scopes.append((ExitStack(), []))

    def pop(self):
        st, names = self.scopes.pop()
        for e in ENGS:
            waits = {}
            for o in CENGS:
                if o != e and self.cnt[o] > 0 and self.waited[e].get(("E", o), 0) < self.cnt[o]:
                    waits[("E", o)] = self.cnt[o]
            for slot, n in self.dma_cnt.items():
                if self.waited[e].get(("D", slot), 0) < 16 * n:
                    waits[("D", slot)] = 16 * n
            for k_, v in waits.items():
                self.waited[e][k_] = v
            if waits:
                self.streams[e].append((waits, None, None, 0))
        for nm in names:
            self.recs.pop(nm, None)
        st.close()

    def ps(self, name, shape, dt=F32):
        t = self.es.enter_context(self.nc.psum_tensor(name, list(shape), dt))
        self.fsz[name] = int(np.prod(shape[1:]))
        self.psum_names.add(name)
        return t

    def box(self, a):
        name = a.tensor.name
        ap = a.ap
        off = a.offset
        if name in self.fsz:
            F = self.fsz[name]
            p0 = off // F
            f0 = off % F
            ext = sum((c - 1) * abs(s) for s, c in ap[1:])
            return (name, p0, p0 + ap[0][1], f0, f0 + ext + 1)
        ext = sum((c - 1) * abs(s) for s, c in ap)
        return (name, 0, 1, off, off + ext + 1)

    def _deps(self, eng, reads, writes):
        waits = {}
        boxes = []
        for a in reads:
            b = self.box(a)
            if b[0] in self.psum_names:
                boxes.append(((b[0], 0, 128, 0, self.fsz[b[0]]), True))
            else:
                boxes.append((b, False))
        for a in writes:
            b = self.box(a)
            if b[0] in self.psum_names:
                b = (b[0], 0, 128, 0, self.fsz[b[0]])
            boxes.append((b, True))
        for (name, p0, p1, f0, f1), isw in boxes:
            for rec in self.recs.get(name, ()):
                rk, rv, risw, rp0, rp1, rf0, rf1, reng = rec
                if rp0 < p1 and p0 < rp1 and rf0 < f1 and f0 < rf1 and (isw or risw):
                    if reng == eng:
                        if eng == "pe":
                            continue
                        if not (risw and not isw):
                            continue
                    if waits.get(rk, 0) < rv:
                        waits[rk] = rv
        out = {}
        w = self.waited[eng]
        for k, v in waits.items():
            if w.get(k, 0) < v:
                w[k] = v
                out[k] = v
        self.nwaits += len(out)
        return out, boxes

    def _record(self, boxes, semkey, val, eng):
        for (name, p0, p1, f0, f1), isw in boxes:
            lst = self.recs.setdefault(name, [])
            if isw:
                lst[:] = [r for r in lst if not (p0 <= r[3] and r[4] <= p1 and f0 <= r[5] and r[6] <= f1)]
                lst.append((semkey, val, True, p0, p1, f0, f1, eng))
            else:
                for i, r in enumerate(lst):
                    if (not r[2]) and r[7] == eng and r[0] == semkey and r[3:7] == (p0, p1, f0, f1):
                        lst[i] = (semkey, val, False, p0, p1, f0, f1, eng)
                        break
                else:
                    lst.append((semkey, val, False, p0, p1, f0, f1, eng))

    def op(self, eng, fn, reads, writes):
        waits, boxes = self._deps(eng, reads, writes)
        self.cnt[eng] += 1
        val = self.cnt[eng]
        key = ("E", eng)
        self.streams[eng].append((waits, fn, key, 1))
        self._record(boxes, key, val, eng)

    def dma(self, pairs, slot, queue="sp"):
        key = ("D", slot)
        n0 = self.dma_cnt.get(slot, 0)
        final = 16 * (n0 + len(pairs))
        self.dma_cnt[slot] = n0 + len(pairs)
        for o, i in pairs:
            waits, boxes = self._deps(queue, [i], [o])
            self.streams[queue].append(
                (waits, (lambda e, o=o, i=i: e.dma_start(out=o, in_=i)), key, 16))
            self._record(boxes, key, final, "dma")

    def mm(self, out, lhsT, rhs, start=True, stop=True):
        self.op("pe", lambda e: e.matmul(out, lhsT, rhs, start=start, stop=stop), [lhsT, rhs], [out])

    def tr(self, out, in_, ident):
        self.op("pe", lambda e: e.transpose(out, in_, ident), [in_, ident], [out])

    def actf(self, out, in_, func, bias=None, scale=None, eng="act", accum_out=None):
        kw = {}
        rd = [in_]
        wr = [out]
        if bias is not None:
            kw["bias"] = bias
            if not isinstance(bias, (int, float)):
                rd.append(bias)
        if scale is not None:
            kw["scale"] = scale
            if not isinstance(scale, (int, float)):
                rd.append(scale)
        if accum_out is not None:
            kw["accum_out"] = accum_out
            wr.append(accum_out)
        self.op("act", lambda e: e.activation(out, in_, func, **kw), rd, wr)

    def tt(self, out, in0, in1, op, eng="dve"):
        self.op(eng, lambda e: e.tensor_tensor(out, in0, in1, op), [in0, in1], [out])

    def ts(self, out, in0, s1, op0, s2=None, op1=None, eng="dve", accum_out=None):
        rd = [in0] + [s for s in (s1, s2) if s is not None and not isinstance(s, (int, float))]
        wr = [out] + ([accum_out] if accum_out is not None else [])
        kw = {}
        if op1 is not None:
            kw["op1"] = op1
        if accum_out is not None:
            kw["accum_out"] = accum_out
        self.op(eng, lambda e: e.tensor_scalar(out, in0, s1, s2, op0, **kw), rd, wr)

    def stt(self, out, in0, scalar, in1, op0, op1, eng="dve", accum_out=None):
        rd = [in0, in1] + ([scalar] if not isinstance(scalar, (int, float)) else [])
        wr = [out] + ([accum_out] if accum_out is not None else [])
        kw = {}
        if accum_out is not None:
            kw["accum_out"] = accum_out
        self.op(eng, lambda e: e.scalar_tensor_tensor(out, in0, scalar, in1, op0, op1, **kw), rd, wr)

    def cp(self, out, in_, eng="dve"):
        if eng == "act":
            self.op("act", lambda e: e.copy(out, in_), [in_], [out])
        else:
            self.op(eng, lambda e: e.tensor_copy(out, in_), [in_], [out])

    def memset(self, out, val, eng="pool"):
        self.op(eng, lambda e: e.memset(out, val), [], [out])

    def red(self, out, in_, op=None, eng="dve"):
        self.op(eng, lambda e: e.tensor_reduce(out, in_, AX.X, op if op is not None else ALU.add), [in_], [out])

    def recip(self, out, in_):
        self.op("dve", lambda e: e.reciprocal(out, in_), [in_], [out])

    def finish(self):
        for slot, n in self.dma_cnt.items():
            k = ("D", slot)
            v = 16 * n
            if self.waited["sp"].get(k, 0) < v:
                self.waited["sp"][k] = v
                self.streams["sp"].append(({k: v}, None, None, 0))
        nc = self.nc
        keys = [("E", e) for e in CENGS] + [("D", s) for s in self.dma_cnt]
        sems = {}
        for i, k in enumerate(keys):
            sems[k] = self.es.enter_context(nc.semaphore("s%d" % i))
        emap = {"pe": "tensor", "act": "scalar", "dve": "vector", "pool": "gpsimd", "sp": "sync"}

        def replay(name, e):
            for waits, fn, key, inc in self.streams[name]:
                for k, v in waits.items():
                    e.wait_ge(sems[k], v)
                if fn is not None:
                    fn(e).then_inc(sems[key], inc)

        with nc.Block() as block:
            for name in ENGS:
                if not self.streams[name]:
                    continue
                getattr(block, emap[name])(lambda e, name=name: replay(name, e))
        self.es.close()


def consts_np():
    j = np.arange(128)
    c = {}
    c["ident_f"] = np.eye(128, dtype=np.float32)
    c["U_f"] = (j[:, None] <= j[None, :]).astype(np.float32)
    c["Un16_f"] = ((j[:, None] <= j[None, :]).astype(np.float32) / -16.0).astype(np.float32)
    c["negmask_f"] = np.where(j[:, None] <= j[None, :], 0.0, -30000.0).astype(np.float32)
    c["nsmask_f"] = np.where(j[:, None] < j[None, :], -1.0, 0.0).astype(np.float32)
    c["imask_f"] = (j[:, None] <= j[None, :]).astype(np.float32)
    c["ones_f"] = np.ones((128, 128), np.float32)
    c["e16_f"] = np.tile(np.eye(16, dtype=np.float32).reshape(1, 256), (128, 1))
    return c


class K:
    pass


def build(stage=99):
    nc = bass.Bass("TRN2", target_bir_lowering=False)
    P = Prog(nc)
    k = K()

    def din(name, shape):
        return nc.dram_tensor(name, list(shape), F32, kind="ExternalInput").ap()

    def dout(name, shape):
        return nc.dram_tensor(name, list(shape), F32, kind="ExternalOutput").ap()

    def bcast(ap1):
        return ap1.partition_broadcast(128).rearrange("p o f -> p (o f)")

    xp = din("xp", [2048, 1024]); xs = din("xs", [16, 1024])
    w0 = din("w0", [57, 128, 1024]); w1 = din("w1", [41, 128, 1024])
    wo0 = din("wo0", [16, 128, 1024]); wo1 = din("wo1", [16, 128, 1024])
    e_pre_g = din("e_pre_g", [1, 1024]); e_post_g = din("e_post_g", [1, 1024])
    o_pre_g = din("o_pre_g", [1, 1024]); o_post_g = din("o_post_g", [1, 1024])
    e_cw = din("e_cw", [128, 96]); e_cb = din("e_cb", [128, 24])
    s_cw = din("s_cw", [128, 96]); s_cb = din("s_cb", [128, 24])
    g_alog = din("g_alog", [1, 8]); g_dtb = din("g_dtb", [1, 8]); g_ng = din("g_ng", [128, 1])
    l_wlr = din("l_wlr", [17, 512]); l_ng = din("l_ng", [128, 2])
    s_alog = din("s_alog", [1, 32]); s_dtb = din("s_dtb", [1, 32]); s_d = din("s_d", [1, 32])
    s_ng = din("s_ng", [1, 2048])
    s_dcol = din("s_dcol", [128, 16]); s_ngcol = din("s_ngcol", [128, 16])
    st_gconv = din("st_gconv", [48, 3072]); st_gdn = din("st_gdn", [16, 8, 128, 128])
    st_gla = din("st_gla", [16, 4, 128, 256])
    st_sconv = din("st_sconv", [48, 3072]); st_ssd = din("st_ssd", [16, 32 * 64, 128])
    cn = consts_np()
    cd = {n: din("c_" + n, list(a.shape)) for n, a in cn.items()}
    h1scr = nc.dram_tensor("h1scr", [NT, 1024], F32, kind="Internal").ap()

    o_yp = dout("o_yp", [2048, 1024]); o_ys = dout("o_ys", [16, 1024])
    o_gconv_p = dout("o_gconv_p", [3, 3072]); o_gdn_p = dout("o_gdn_p", [8, 128, 128])
    o_gla_p = dout("o_gla_p", [4, 128, 256])
    o_sconv_p = dout("o_sconv_p", [3, 3072]); o_ssd_p = dout("o_ssd_p", [32 * 64, 128])
    o_gconv_s = dout("o_gconv_s", [48, 3072]); o_gdn_s = dout("o_gdn_s", [16, 8, 128, 128])
    o_gla_s = dout("o_gla_s", [16, 4, 128, 256])
    o_sconv_s = dout("o_sconv_s", [48, 3072]); o_ssd_s = dout("o_ssd_s", [16, 32 * 64, 128])

    ident_f = P.sb("ident_f", [128, 128], F32); U_f = P.sb("U_f", [128, 128], F32)
    Un16 = P.sb("Un16_f", [128, 128], F32)
    negmask = P.sb("negmask_f", [128, 128], F32); nsmask = P.sb("nsmask_f", [128, 128], F32)
    imask = P.sb("imask_f", [128, 128], F32)
    ones_f = P.sb("ones_f", [128, 128], F32)
    ident_b = P.sb("ident_b", [128, 128], BF16); ones_b = P.sb("ones_b", [128, 128], BF16)
    E16 = P.sb("e16_f", [128, 16, 16], F32)
    P.dma([(E16[:].rearrange("p a b -> p (a b)"), cd["e16_f"][:])], "c_e16")
    for t, n in [(ident_f, "ident_f"), (U_f, "U_f"), (Un16, "Un16_f"), (negmask, "negmask_f"), (nsmask, "nsmask_f"),
                 (imask, "imask_f"), (ones_f, "ones_f")]:
        P.dma([(t[:], cd[n][:])], "c_" + n)
    P.cp(ident_b[:], ident_f[:], eng="pool"); P.cp(ones_b[:], ones_f[:], eng="pool")

    cw = P.sb("cw", [128, 96], F32); cb = P.sb("cb", [128, 24], F32)
    P.dma([(cw[:], e_cw[:]), (cb[:], e_cb[:])], "c_cw")

    pA = [P.ps("pA0", [128, 512]), P.ps("pA1", [128, 512])]
    pT = P.ps("pT", [128, 1024], BF16)
    pS = P.ps("pS", [128, 512])
    pC = [P.ps("pC%d" % i, [128, 512]) for i in range(4)]

    uT = P.sb("uT", [128, 8, NT], BF16)
    oT = P.sb("oT", [128, 16, NT], BF16)
    ub = [P.sb("ub0", [128, 1024], BF16), P.sb("ub1", [128, 1024], BF16)]
    ss = P.sb("ss", [128, 4], F32)
    wst = [P.sb("wst0", [128, 1024], F32), P.sb("wst1", [128, 1024], F32)]
    wbf = [P.sb("wbf0", [128, 8, 128], BF16), P.sb("wbf1", [128, 8, 128], BF16)]
    k.ui = 0

    def rstd_of(s_, scale):
        P.actf(s_, s_, AF.Ln, bias=EPS, scale=scale)
        P.actf(s_, s_, AF.Exp, scale=-0.5)

    def norm_to_uT(x_, n, col0, gbc):
        i = k.ui % 2; k.ui += 1
        u_ = ub[i]
        s_ = ss[0:n, i:i + 1]
        P.memset(s_, 0.0, eng="dve")
        P.actf(u_[0:n, :], x_[0:n, :], AF.Square, accum_out=s_)
        rstd_of(s_, 1.0 / 1024)
        P.stt(u_[0:n, :], x_[0:n, :], s_, gbc[0:n, :], ALU.mult, ALU.mult)
        for kc in range(8):
            P.tr(pT[:, kc * 128:kc * 128 + n], u_[0:n, kc * 128:(kc + 1) * 128], ident_b[0:n, 0:n])
        P.cp(uT[:, :, col0:col0 + n], pT[:].rearrange("p (k t) -> p k t", k=8)[:, :, 0:n], eng="act")

    ROWT = [(xp[i * 128:(i + 1) * 128, :], 128, i * 128) for i in range(16)] + [(xs[:, :], 16, 2048)]
    P.push()
    xt = [P.sb("xt0", [128, 1024], F32), P.sb("xt1", [128, 1024], F32)]
    gbc0 = P.sb("gbc0", [128, 1024], F32)
    P.dma([(gbc0[:], bcast(e_pre_g[0:1, :]))], "c_gbc")
    for i, (src, n, col0) in enumerate(ROWT):
        P.dma([(xt[i % 2][0:n, :], src)], "xt%d" % (i % 2))
        norm_to_uT(xt[i % 2], n, col0, gbc0)
    P.pop()

    if stage <= -3:
        P.finish(); return nc, cn
    k.wi = 0

    def load_w(wsrc, blk, dst=None):
        i = k.wi % 2; k.wi += 1
        P.dma([(wst[i][:], wsrc[blk])], "wst%d" % i)
        if dst is None:
            dst = wbf[i][:]
        P.cp(dst, wst[i][:].rearrange("p (k c) -> p k c", k=8), eng="pool")
        return wbf[i]

    class WQ:
        def __init__(self, items):
            self.items = list(items); self.nd = 0; self.nc_ = 0; self.dq = []; self.cq = []

        def _dma(self):
            if self.nd < len(self.items):
                wsrc, blk, dst = self.items[self.nd]; self.nd += 1
                bi = k.wi % 2; k.wi += 1
                P.dma([(wst[bi][:], wsrc[blk])], "wst%d" % bi)
                self.dq.append((bi, dst))

        def _cast(self):
            if self.dq:
                bi, dst = self.dq.pop(0)
                d_ = wbf[bi][:] if dst is None else dst
                P.cp(d_, wst[bi][:].rearrange("p (k c) -> p k c", k=8), eng="pool")
                self.cq.append(wbf[bi])

        def get(self):
            if not self.cq:
                self._dma(); self._cast()
            cur = self.cq.pop(0)
            if not self.dq:
                self._dma()
            self._cast()
            self._dma()
            return cur

    k.pi = 0

    def proj_fm(wb, evac, M=128, c0=0):
        for (t0, n) in GRPS:
            pa = pA[k.pi % 2]; k.pi += 1
            for kc in range(8):
                P.mm(pa[0:M, 0:n], wb[:, kc, c0:c0 + M], uT[:, kc, t0:t0 + n], start=(kc == 0), stop=(kc == 7))
            evac(pa[0:M, 0:n], t0, n)

    def load_hist(hist_, src):
        for pc in range(3):
            hb = wst[k.wi % 2]
            P.dma([(hb[0:48, :], src[:, pc * 1024:(pc + 1) * 1024])], "wst%d" % (k.wi % 2))
            k.wi += 1
            for c8 in range(8):
                ct = pc * 8 + c8
                P.tr(pS[:, 256:304], hb[0:48, c8 * 128:(c8 + 1) * 128], ident_f[0:48, 0:48])
                P.cp(hist_[:, ct, :], pS[:, 256:304])

    hist = P.sb("hist", [128, 24, 48], F32)
    Xc = P.sb("Xc", [128, 3 + 2048], F32)
    Xs = P.sb("Xs", [128, 16, 4], F32)
    Yc = P.sb("Yc", [128, NT], F32)
    nst = P.sb("nst", [128, 51], F32)
    nso = [P.sb("nso0", [51, 128], F32), P.sb("nso1", [51, 128], F32)]
    k.ni = 0
    P.memset(Xc[:, 0:3], 0.0)

    def conv_tile(wq, ct, ocp, ocs):
        wb = wq.get()

        def ev(pa, t0, n):
            if t0 < 2048:
                P.cp(Xc[:, 3 + t0:3 + t0 + n], pa, eng="act")
            else:
                P.cp(Xs[:, :, 3], pa, eng="act")
        proj_fm(wb, ev)
        P.cp(Xs[:, :, 0:3], hist[:, ct, :].rearrange("p (s r) -> p s r", r=3), eng="pool")
        for (xin, yout) in [(lambda i: Xc[:, i:i + 2048], Yc[:, 0:2048]), (lambda i: Xs[:, :, i], Yc[:, 2048:2064])]:
            P.ts(yout, xin(0), cw[:, ct * 4:ct * 4 + 1], ALU.mult, cb[:, ct:ct + 1], ALU.add)
            for i in (1, 2, 3):
                P.stt(yout, xin(i), cw[:, ct * 4 + i:ct * 4 + i + 1], yout, ALU.mult, ALU.add)
        P.actf(Yc[:], Yc[:], AF.Silu)
        P.cp(nst[:, 0:3], Xc[:, 2048:2051], eng="pool")
        P.cp(nst[:, 3:51].rearrange("p (s r) -> p s r", r=3), Xs[:, :, 1:4], eng="pool")
        P.tr(pS[0:51, 384:512], nst[:], ident_f[:])
        no = nso[k.ni % 2]
        P.cp(no[:], pS[0:51, 384:512])
        P.dma([(ocp[:, ct * 128:(ct + 1) * 128], no[0:3, :]), (ocs[:, ct * 128:(ct + 1) * 128], no[3:51, :])], "nso%d" % (k.ni % 2))
        k.ni += 1

    load_hist(hist, st_gconv)

    P.push()
    lrT = P.sb("lrT", [17, NT], BF16)
    P.push()
    wsm = P.sb("wsm", [128, 8, 128], BF16)
    load_w(w0, 56, wsm[:])
    alog_bc = P.sb("alog_bc", [128, 8], F32); dtb_bc = P.sb("dtb_bc", [128, 8], F32)
    negA = P.sb("negA", [128, 8], F32); ng_col = P.sb("ng_col", [128, 1], F32)
    P.dma([(ng_col[:], g_ng[:]), (alog_bc[:], bcast(g_alog[0:1, :])), (dtb_bc[:], bcast(g_dtb[0:1, :]))], "c_misc")
    P.actf(negA[:], alog_bc[:], AF.Exp)
    P.ts(negA[:], negA[:], -1.0, ALU.mult)
    SC = P.sb("SC", [128, 16, 64], F32)
    SCs = P.sb("SCs", [16, 32], F32)
    GL = P.sb("GL", [128, 16, 16], F32)
    tmp8 = P.sb("tmp8", [128, 16], F32)
    for kc in range(8):
        P.mm(pS[0:16, 0:32], uT[:, kc, 2048:2064], wsm[:, kc, 0:32], start=(kc == 0), stop=(kc == 7))
    P.actf(SCs[:, 0:8], pS[0:16, 0:8], AF.Exp, scale=-1.0)
    P.ts(SCs[:, 0:8], SCs[:, 0:8], 1.0, ALU.add)
    P.recip(SCs[:, 0:8], SCs[:, 0:8])
    P.tt(tmp8[0:16, 0:8], pS[0:16, 8:16], dtb_bc[0:16, :], ALU.add)
    P.actf(tmp8[0:16, 0:8], tmp8[0:16, 0:8], AF.Exp)
    P.actf(tmp8[0:16, 0:8], tmp8[0:16, 0:8], AF.Ln, bias=1.0)
    P.tt(SCs[:, 8:16], tmp8[0:16, 0:8], negA[0:16, :], ALU.mult)
    P.actf(SCs[:, 16:24], SCs[:, 8:16], AF.Exp)
    P.ts(SCs[:, 24:32], SCs[:, 16:24], -1.0, ALU.mult)
    for g, (c0, n) in enumerate(SEGS[:16]):
        ps = pS[0:n, 0:32]
        for kc in range(8):
            P.mm(ps, uT[:, kc, c0:c0 + n], wsm[:, kc, 0:32], start=(kc == 0), stop=(kc == 7))
        sc = SC[0:n, g, :]
        P.actf(sc[:, 0:8], ps[:, 0:8], AF.Exp, scale=-1.0)
        P.ts(sc[:, 0:8], sc[:, 0:8], 1.0, ALU.add)
        P.recip(sc[:, 0:8], sc[:, 0:8])
        P.tt(tmp8[0:n, 0:8], ps[:, 8:16], dtb_bc[0:n, :], ALU.add)
        P.actf(tmp8[0:n, 0:8], tmp8[0:n, 0:8], AF.Exp)
        P.actf(tmp8[0:n, 0:8], tmp8[0:n, 0:8], AF.Ln, bias=1.0)
        P.tt(sc[:, 8:16], tmp8[0:n, 0:8], negA[0:n, :], ALU.mult)
        P.mm(pS[0:n, 32:40], U_f[0:n, 0:n], sc[:, 8:16])
        P.mm(pS[:, 40:48], ones_f[0:n, :], sc[:, 8:16])
        P.cp(sc[:, 16:24], pS[0:n, 32:40])
        P.ts(sc[:, 24:32], pS[0:n, 32:40], -1.0, ALU.mult)
        P.actf(sc[:, 32:40], pS[0:n, 32:40], AF.Exp)
        P.ts(sc[:, 40:48], sc[:, 32:40], -1.0, ALU.mult)
        P.cp(GL[:, g, 8:16], pS[:, 40:48])
        P.actf(GL[:, g, 0:8], pS[:, 40:48], AF.Exp)
        P.tt(tmp8[0:n, 8:16], GL[0:n, g, 8:16], sc[:, 16:24], ALU.subtract)
        P.actf(sc[:, 48:56], tmp8[0:n, 8:16], AF.Exp)
    P.memset(lrT[:], 1.0)
    for (t0, n) in GRPS:
        for kc in range(8):
            P.mm(pS[0:16, 0:n], wsm[:, kc, 16:32], uT[:, kc, t0:t0 + n], start=(kc == 0), stop=(kc == 7))
        P.cp(lrT[0:16, t0:t0 + n], pS[0:16, 0:n], eng="act")

    sq = P.sb("sq", [128, NT], BF16)
    rn = P.sb("rn", [128, 512], F32)

    def l2norm_to(dsts, scale):
        P.tt(sq[:], Yc[:], Yc[:], ALU.mult, eng="pool")
        for (t0, n) in GRPS:
            P.mm(pS[:, 0:n], ones_b[:], sq[:, t0:t0 + n])
            P.actf(rn[:, 0:n], pS[:, 0:n], AF.Ln, bias=EPS)
            P.actf(rn[:, 0:n], rn[:, 0:n], AF.Exp, scale=-0.5)
            for d_ in dsts:
                P.stt(d_[:, t0:t0 + n], Yc[:, t0:t0 + n], scale, rn[:, 0:n], ALU.mult, ALU.mult)

    qT = P.sb("qT", [128, NT], BF16); vT = P.sb("vT", [128, NT], BF16)
    kT32 = P.sb("kT32", [128, NT], F32)
    gsT = P.sb("gsT", [128, NT], BF16)
    S = [P.sb("S0", [128, 128], F32), P.sb("S1", [128, 128], F32)]
    Sb = [P.sb("Sb0", [128, 128], BF16), P.sb("Sb1", [128, 128], BF16)]
    vtok = [P.sb("vtk_%d" % i, [128, 128], BF16) for i in range(3)]; kdtok = [P.sb("kdtk_%d" % i, [128, 128], F32) for i in range(3)]
    decTs = [P.sb("decT%d" % i, [128, 128], F32) for i in range(2)]; pTm = [P.sb("pTmm_%d" % i, [128, 128], F32) for i in range(3)]
    Mis = [P.sb("Mi%d" % i, [128, 128], F32) for i in range(2)]; kcbs = [P.sb("kcb%d" % i, [128, 128], BF16) for i in range(2)]
    Abs = [[P.sb("Ab%d_%d" % (t, i), [128, 128], F32) for i in range(2)] for t in range(2)]
    ATbs = [[P.sb("ATb%d_%d" % (t, i), [128, 128], F32) for i in range(2)] for t in range(2)]
    P32 = [P.sb("P32_%d" % i, [128, 128], F32) for i in range(3)]
    R32 = P.sb("R32", [128, 128], F32); u32 = P.sb("u32", [128, 128], F32)
    o2s = P.sb("o2s", [128, 128], F32); osb = P.sb("osb", [128, 128], F32); onb = P.sb("onb", [128, 128], BF16)
    oss = P.sb("oss", [128, 1], F32)
    k.si = 0

    def interleave(gens):
        gens = [g_ for g_ in gens if g_ is not None]
        while gens:
            for g_ in list(gens):
                try:
                    next(g_)
                except StopIteration:
                    gens.remove(g_)

    def gdn_prep(h, g, c0, n, pb, tb):
        sc = SC[0:n, g, :]
        Ab = Abs[tb]; ATb = ATbs[tb]; decT = decTs[tb]; Mi = Mis[tb]; kcb = kcbs[tb]
        if tb == 0:
            r_kk = pC[0][:, 0:128]; r_qk = pC[0][:, 128:256]; r_G = pC[0][:, 256:384]
            r_s0 = pC[1][:, 0:128]; r_s1 = pC[1][:, 128:256]; r_P = pA[0][:, 0:128]
            r_kt = pA[1][:, 0:128]; r_vt = pT[:, 0:128]
        else:
            r_kk = pC[1][:, 256:384]; r_qk = pC[1][:, 384:512]; r_G = pA[0][:, 128:256]
            r_s0 = pA[1][:, 128:256]; r_s1 = pA[1][:, 256:384]; r_P = pA[0][:, 256:384]
            r_kt = pA[1][:, 384:512]; r_vt = pT[:, 128:256]
        qc_ = qT[:, c0:c0 + n]; k32 = kT32[:, c0:c0 + n]
        kc_ = kcb[:, 0:n]
        P.cp(kc_, k32, eng="pool")
        P.tr(r_vt[0:n, :], vT[:, c0:c0 + n], ident_b[:])
        P.cp(vtok[pb][0:n, :], r_vt[0:n, :], eng="act")
        P.tr(r_kt[0:n, :], k32, ident_f[:])
        P.actf(kdtok[pb][0:n, :], r_kt[0:n, :], AF.Copy, scale=sc[:, 48 + h:49 + h])
        yield
        P.mm(r_qk[0:n, 0:n], kc_, qc_)
        P.mm(r_G[0:n, 0:n], sc[:, 8 + h:9 + h].to_broadcast([n, n]), U_f[0:n, 0:n], start=True, stop=False)
        P.mm(r_G[0:n, 0:n], ident_f[0:n, 0:n], negmask[0:n, 0:n], start=False, stop=True)
        if n > 1:
            P.mm(r_kk, kc_, kc_)
        yield
        P.actf(decT[0:n, 0:n], r_G[0:n, 0:n], AF.Exp, bias=sc[:, 24 + h:25 + h])
        P.tt(pTm[pb][0:n, 0:n], r_qk[0:n, 0:n], decT[0:n, 0:n], ALU.mult)
        yield
        if n > 1:
            Pf = P32[pb]
            P.stt(Mi[:], r_kk, sc[:, h:h + 1], decT[:], ALU.mult, ALU.mult)
            P.tt(Ab[0][:], Mi[:], nsmask[:], ALU.mult, eng="pool")
            yield
            P.tr(r_s0, Ab[0][:], ident_f[:])
            P.cp(ATb[0][:], r_s0, eng="act")
            P.tt(Pf[:], Ab[0][:], ident_f[:], ALU.add, eng="pool")
            yield
            cur = 0
            for it in range(6):
                nx = 1 - cur
                P.mm(r_s0, Ab[cur][:], ATb[cur][:])
                if it < 5:
                    P.mm(r_s1, ATb[cur][:], Ab[cur][:])
                yield
                P.cp(ATb[nx][:], r_s0, eng="act")
                if it < 5:
                    P.cp(Ab[nx][:], r_s1, eng="dve")
                yield
                P.mm(r_P, ATb[nx][:], Pf[:])
                yield
                P.tt(Pf[:], r_P, Pf[:], ALU.add)
                yield
                cur = nx

    def gdn_seq(h, g, c0, n, first, last, s_src, s_dst, pb):
        sc = SC[0:n, g, :]
        if first:
            k.si += 1
        Sf = S[k.si % 2]; Sbb = Sb[k.si % 2]
        if first:
            if s_src is None:
                P.memset(Sf[:], 0.0)
            else:
                P.dma([(Sf[:], s_src)], "Sld%d" % (k.si % 2))
            P.cp(Sbb[:], Sf[:], eng="pool")
            yield
        qc_ = qT[:, c0:c0 + n]; k32 = kT32[:, c0:c0 + n]
        Pfin = P32[pb] if n > 1 else ident_f
        P.mm(pC[3][0:n, 0:128], k32, Sf[:])
        yield
        P.stt(R32[0:n, :], pC[3][0:n, 0:128], sc[:, 40 + h:41 + h], vtok[pb][0:n, :], ALU.mult, ALU.add)
        yield
        P.mm(pC[2][0:n, 0:128], Pfin[0:n, 0:n], R32[0:n, :])
        yield
        P.actf(u32[0:n, :], pC[2][0:n, 0:128], AF.Copy, scale=sc[:, h:h + 1])
        yield
        P.mm(pC[3][:, 128:256], kdtok[pb][0:n, :], u32[0:n, :])
        P.mm(pC[2][0:n, 256:384], qc_, Sbb[:])
        P.mm(pC[2][0:n, 384:512], pTm[pb][0:n, 0:n], u32[0:n, :])
        yield
        P.stt(Sf[:], Sf[:], GL[:, g, h:h + 1], pC[3][:, 128:256], ALU.mult, ALU.add)
        yield
        if not last:
            P.cp(Sbb[:], Sf[:], eng="pool")
        else:
            P.dma([(s_dst, Sf[:])], "Sst%d" % (k.si % 2))
        P.cp(o2s[0:n, :], pC[2][0:n, 384:512], eng="act")
        yield
        P.stt(osb[0:n, :], pC[2][0:n, 256:384], sc[:, 32 + h:33 + h], o2s[0:n, :], ALU.mult, ALU.add)
        P.memset(oss[0:n, :], 0.0, eng="dve")
        yield
        P.actf(onb[0:n, :], osb[0:n, :], AF.Square, accum_out=oss[0:n, :])
        rstd_of(oss[0:n, :], 1.0 / 128)
        P.actf(osb[0:n, :], osb[0:n, :], AF.Copy, scale=oss[0:n, :])
        yield
        P.tr(pS[:, 0:n], osb[0:n, :], ident_f[0:n, 0:n])
        yield
        P.tt(oT[:, h, c0:c0 + n], pS[:, 0:n], gsT[:, c0:c0 + n], ALU.mult)
        yield

    def gdn_head_segs(h):
        segl = []
        for g, (c0, n) in enumerate(SEGS[:16]):
            segl.append((g, c0, n, g == 0, g == 15, None, o_gdn_p[h]))
        preps = [gdn_prep(h, sg[0], sg[1], sg[2], i % 3, i % 2) for i, sg in enumerate(segl)]
        done = [False] * len(segl)

        def step(i):
            if i < len(segl) and not done[i]:
                try:
                    next(preps[i])
                except StopIteration:
                    done[i] = True

        while not done[0]:
            step(0)
        for i, (g, c0, n, fi, la, src, dst) in enumerate(segl):
            sq_ = gdn_seq(h, g, c0, n, fi, la, src, dst, i % 3)
            sq_done = False
            while not (sq_done and (i + 1 >= len(segl) or done[i + 1])):
                if not sq_done:
                    try:
                        next(sq_)
                    except StopIteration:
                        sq_done = True
                step(i + 1)
                step(i + 2)

    Sall = P.sb("Sall", [128, 16, 128], F32)
    Km = P.sb("Km", [128, 16, 16], F32); Qm = P.sb("Qm", [128, 16, 16], F32); q32c = P.sb("q32c", [128, 16], F32)
    vts = P.sb("vts", [16, 128], BF16); Rs = P.sb("Rs", [16, 128], F32); us = P.sb("us", [16, 128], F32)
    dg = P.sb("dg", [16, 16], F32); egbc = P.sb("egbc", [128, 16, 1], F32)
    osbs = P.sb("osbs", [16, 128], F32); onbs = P.sb("onbs", [16, 128], F32); osss = P.sb("osss", [16, 1], F32)

    def gdn_sample(h):
        kc32 = kT32[:, 2048:2064]
        P.cp(q32c[:], qT[:, 2048:2064], eng="pool")
        P.tt(Km[:], kc32.unsqueeze(2).to_broadcast([128, 16, 16]), E16[:], ALU.mult, eng="pool")
        P.tt(Qm[:], q32c[:].unsqueeze(2).to_broadcast([128, 16, 16]), E16[:], ALU.mult, eng="pool")
        P.tr(pT[0:16, 0:128], vT[:, 2048:2064], ident_b[:])
        P.cp(vts[:], pT[0:16, 0:128], eng="act")
        P.ts(dg[:], ident_f[0:16, 0:16], SCs[:, 16 + h:17 + h], ALU.mult)
        P.mm(pC[1][:, 0:16], ones_f[0:16, :], dg[:])
        P.cp(egbc[:].rearrange("p s o -> p (s o)"), pC[1][:, 0:16], eng="act")
        for s in range(16):
            P.mm(pC[0][0:16, 0:128], Km[:, s, :], Sall[:, s, :], start=(s == 0), stop=(s == 15))
        P.stt(Rs[:], pC[0][0:16, 0:128], SCs[:, 24 + h:25 + h], vts[:], ALU.mult, ALU.add)
        P.ts(us[:], Rs[:], SCs[:, h:h + 1], ALU.mult)
        P.tt(Sall[:], Sall[:], egbc[:].to_broadcast([128, 16, 128]), ALU.mult)
        for s in range(16):
            bk = pC[2] if (s // 4) % 2 == 0 else pC[3]
            r0 = (s % 4) * 128
            P.mm(bk[:, r0:r0 + 128], ident_f[0:16, s:s + 1].to_broadcast([16, 128]), us[:])
            P.stt(Sall[:, s, :], bk[:, r0:r0 + 128], kc32[:, s:s + 1], Sall[:, s, :], ALU.mult, ALU.add)
        for s in range(16):
            P.mm(pC[0][0:16, 128:256], Qm[:, s, :], Sall[:, s, :], start=(s == 0), stop=(s == 15))
        P.dma([(o_gdn_s[:, h].rearrange("s d e -> d s e"), Sall[:])], "Sall")
        P.cp(osbs[:], pC[0][0:16, 128:256], eng="act")
        P.memset(osss[:], 0.0, eng="dve")
        P.actf(onbs[:], osbs[:], AF.Square, accum_out=osss[:])
        rstd_of(osss[:], 1.0 / 128)
        P.actf(onbs[:], osbs[:], AF.Copy, scale=osss[:])
        P.tr(pS[:, 0:16], onbs[:], ident_f[0:16, 0:16])
        P.tt(oT[:, h, 2048:2064], pS[:, 0:16], gsT[:, 2048:2064], ALU.mult)

    wq0 = WQ([(w0, b_, None) for h in range(8) for b_ in (h, 8 + h, 16 + h, 24 + h)])
    for h in range(8 if stage >= 1 else 0):
        conv_tile(wq0, h, o_gconv_p, o_gconv_s); l2norm_to([qT], 128 ** -0.5)
        conv_tile(wq0, 8 + h, o_gconv_p, o_gconv_s); l2norm_to([kT32], 1.0)
        conv_tile(wq0, 16 + h, o_gconv_p, o_gconv_s); P.cp(vT[:], Yc[:], eng="pool")
        wb = wq0.get()
        proj_fm(wb, lambda pa, t0, n: P.actf(Yc[:, t0:t0 + n], pa, AF.Silu))
        P.ts(gsT[:], Yc[:], ng_col[:], ALU.mult)
        P.dma([(Sall[:], st_gdn[:, h].rearrange("s d e -> d s e"))], "Sall")
        gdn_head_segs(h)
        gdn_sample(h)
    P.pop()
    if stage <= 1:
        P.pop()
        print('sbuf_remaining', nc.sbuf_bytes_remaining)
        print('counts', P.cnt, 'waits', P.nwaits, 'dma slots', len(P.dma_cnt))
        P.finish(); return nc, cn

    P.push()
    wlr = P.sb("wlr", [17, 512], BF16)
    P.dma([(wst[k.wi % 2][0:17, 0:512], l_wlr[:])], "wst%d" % (k.wi % 2))
    P.cp(wlr[:], wst[k.wi % 2][0:17, 0:512], eng="pool")
    k.wi += 1
    lng = P.sb("lng", [128, 2], F32)
    P.dma([(lng[:], l_ng[:])], "c_misc")
    lqT = P.sb("lqT", [128, NT], BF16); lkT = P.sb("lkT", [128, NT], BF16)
    gs2 = P.sb("gs2", [128, 2, NT], BF16)
    wv = P.sb("wv", [128, 8, 256], BF16)
    S2 = [P.sb("S2a", [128, 256], F32), P.sb("S2b", [128, 256], F32)]
    S2h = [P.sb("S2ha", [128, 256], BF16), P.sb("S2hb", [128, 256], BF16)]
    sp32 = P.sb("sp32", [128, 128], F32)
    ebt = P.sb("ebt", [128, 128], F32); enbt = P.sb("enbt", [128, 128], F32); ekdt = P.sb("ekdt", [128, 128], F32)
    bl = P.sb("bl", [128, 2], F32)
    qeb = P.sb("qeb", [128, 128], BF16); keb = P.sb("keb", [128, 128], BF16); kdTb = P.sb("kdTb", [128, 128], BF16)
    kdtok2 = P.sb("kdtok2", [128, 128], BF16); pTm2 = P.sb("pTm2", [128, 128], BF16)
    vtok2 = P.sb("vtok2", [128, 256], BF16)
    osb2 = P.sb("osb2", [128, 256], F32); onb2 = P.sb("onb2", [128, 256], BF16); oss2 = P.sb("oss2", [128, 1], F32)
    k.s2 = 0

    def gla_seg(h, g, c0, n, first, last, s_src, s_dst):
        if first:
            k.s2 += 1
        Sf = S2[k.s2 % 2]; Sbb = S2h[k.s2 % 2]
        if first:
            if s_src is None:
                P.memset(Sf[:], 0.0)
            else:
                P.dma([(Sf[:], s_src)], "S2ld%d" % (k.s2 % 2))
            P.cp(Sbb[:], Sf[:], eng="pool")
        P.mm(pS[0:n, 0:128], lrT[0:17, c0:c0 + n], wlr[0:17, h * 128:(h + 1) * 128])
        P.actf(sp32[0:n, :], pS[0:n, 0:128], AF.Exp, scale=-1.0)
        P.actf(sp32[0:n, :], sp32[0:n, :], AF.Ln, bias=1.0)
        P.mm(pC[0][:, 0:n], sp32[0:n, :], Un16[0:n, 0:n])
        P.actf(ebt[:, 0:n], pC[0][:, 0:n], AF.Exp)
        P.actf(enbt[:, 0:n], pC[0][:, 0:n], AF.Exp, scale=-1.0)
        P.cp(bl[:, 0:1], pC[0][:, n - 1:n])
        P.actf(ekdt[:, 0:n], pC[0][:, 0:n], AF.Exp, scale=-1.0, bias=bl[:, 0:1])
        P.actf(bl[:, 1:2], bl[:, 0:1], AF.Exp)
        P.stt(qeb[:, 0:n], lqT[:, c0:c0 + n], 128 ** -0.5, ebt[:, 0:n], ALU.mult, ALU.mult)
        P.tt(keb[:, 0:n], lkT[:, c0:c0 + n], enbt[:, 0:n], ALU.mult)
        P.tt(kdTb[:, 0:n], lkT[:, c0:c0 + n], ekdt[:, 0:n], ALU.mult, eng="pool")
        P.tr(pT[0:n, 0:128], kdTb[:, 0:n], ident_b[:])
        P.cp(kdtok2[0:n, :], pT[0:n, 0:128], eng="act")
        P.mm(pC[1][0:n, 0:n], keb[:, 0:n], qeb[:, 0:n])
        P.tt(pTm2[0:n, 0:n], pC[1][0:n, 0:n], imask[0:n, 0:n], ALU.mult)
        for kc in range(8):
            P.mm(pA[0][0:n, 0:256], uT[:, kc, c0:c0 + n], wv[:, kc, :], start=(kc == 0), stop=(kc == 7))
        P.cp(vtok2[0:n, :], pA[0][0:n, 0:256], eng="act")
        P.mm(pC[2][0:n, 0:256], qeb[:, 0:n], Sbb[:], start=True, stop=False)
        P.mm(pC[2][0:n, 0:256], pTm2[0:n, 0:n], vtok2[0:n, :], start=False, stop=True)
        P.mm(pC[3][:, 0:256], kdtok2[0:n, :], vtok2[0:n, :])
        P.stt(Sf[:], Sf[:], bl[:, 1:2], pC[3][:, 0:256], ALU.mult, ALU.add)
        if not last:
            P.cp(Sbb[:], Sf[:], eng="pool")
        else:
            P.dma([(s_dst, Sf[:])], "S2st%d" % (k.s2 % 2))
        P.cp(osb2[0:n, :], pC[2][0:n, 0:256], eng="act")
        P.memset(oss2[0:n, :], 0.0, eng="dve")
        P.actf(onb2[0:n, :], osb2[0:n, :], AF.Square, accum_out=oss2[0:n, :])
        rstd_of(oss2[0:n, :], 1.0 / 256)
        P.actf(onb2[0:n, :], osb2[0:n, :], AF.Copy, scale=oss2[0:n, :])
        for j in range(2):
            P.tr(pT[:, 256 + j * 128:256 + j * 128 + n], onb2[0:n, j * 128:(j + 1) * 128], ident_b[0:n, 0:n])
        for j in range(2):
            P.tt(oT[:, 8 + 2 * h + j, c0:c0 + n], pT[:, 256 + j * 128:256 + j * 128 + n], gs2[:, j, c0:c0 + n], ALU.mult)

    Sgs = [P.sb("Sg0", [128, 8, 256], F32), P.sb("Sg1", [128, 8, 256], F32)]
    Qm2 = P.sb("Qm2", [128, 16, 16], F32); q32g = P.sb("q32g", [128, 16], F32); k32g = P.sb("k32g", [128, 16], F32)
    aTs = P.sb("aTs", [128, 16, 1], F32); sps = P.sb("sps", [16, 128], F32)
    vts2 = P.sb("vts2", [16, 256], BF16)
    osg = P.sb("osg", [16, 256], F32); ong = P.sb("ong", [16, 256], F32); ossg = P.sb("ossg", [16, 1], F32)

    def gla_sample(h):
        P.mm(pS[0:16, 0:128], lrT[0:17, 2048:2064], wlr[0:17, h * 128:(h + 1) * 128])
        P.actf(sps[:], pS[0:16, 0:128], AF.Exp, scale=-1.0)
        P.actf(sps[:], sps[:], AF.Ln, bias=1.0)
        P.actf(sps[:], sps[:], AF.Exp, scale=-1.0 / 16)
        P.tr(pS[:, 128:144], sps[:], ident_f[0:16, 0:16])
        P.cp(aTs[:].rearrange("p s o -> p (s o)"), pS[:, 128:144], eng="act")
        P.cp(k32g[:], lkT[:, 2048:2064], eng="pool")
        P.ts(q32g[:], lqT[:, 2048:2064], 128 ** -0.5, ALU.mult)
        P.tt(Qm2[:], q32g[:].unsqueeze(2).to_broadcast([128, 16, 16]), E16[:], ALU.mult, eng="pool")
        for kc in range(8):
            P.mm(pA[0][0:16, 0:256], uT[:, kc, 2048:2064], wv[:, kc, :], start=(kc == 0), stop=(kc == 7))
        P.cp(vts2[:], pA[0][0:16, 0:256], eng="act")
        for hf in range(2):
            Sg = Sgs[hf]
            for s8 in range(8):
                s = hf * 8 + s8
                bk = pC[2] if s % 2 == 0 else pC[3]
                P.mm(bk[:, 0:256], ident_b[0:16, s:s + 1].to_broadcast([16, 128]), vts2[:])
                P.tt(Sg[:, s8, :], Sg[:, s8, :], aTs[:, s, :].to_broadcast([128, 256]), ALU.mult, eng="pool")
                P.stt(Sg[:, s8, :], bk[:, 0:256], k32g[:, s:s + 1], Sg[:, s8, :], ALU.mult, ALU.add)
            for s8 in range(8):
                s = hf * 8 + s8
                P.mm(pC[0][0:16, 0:256], Qm2[:, s, :], Sg[:, s8, :], start=(s == 0), stop=(s == 15), )
            P.dma([(o_gla_s[hf * 8:(hf + 1) * 8, h].rearrange("s d e -> d s e"), Sg[:])], "Sg%d" % hf)
        P.cp(osg[:], pC[0][0:16, 0:256], eng="act")
        P.memset(ossg[:], 0.0, eng="dve")
        P.actf(ong[:], osg[:], AF.Square, accum_out=ossg[:])
        rstd_of(ossg[:], 1.0 / 256)
        P.actf(ong[:], osg[:], AF.Copy, scale=ossg[:])
        for j in range(2):
            P.tr(pS[:, 256 + j * 16:256 + (j + 1) * 16], ong[:, j * 128:(j + 1) * 128], ident_f[0:16, 0:16])
        for j in range(2):
            P.tt(oT[:, 8 + 2 * h + j, 2048:2064], pS[:, 256 + j * 16:256 + (j + 1) * 16], gs2[:, j, 2048:2064], ALU.mult)

    gl_items = []
    for h in range(4):
        gl_items += [(w0, 32 + h, None), (w0, 36 + h, None)]
        for j in range(2):
            gl_items += [(w0, 48 + 2 * h + j, None), (w0, 40 + 2 * h + j, wv[:, :, j * 128:(j + 1) * 128])]
    wqg = WQ(gl_items)
    for h in range(4):
        proj_fm(wqg.get(), lambda pa, t0, n: P.cp(lqT[:, t0:t0 + n], pa, eng="act"))
        proj_fm(wqg.get(), lambda pa, t0, n: P.cp(lkT[:, t0:t0 + n], pa, eng="act"))
        for j in range(2):
            proj_fm(wqg.get(), lambda pa, t0, n: P.actf(Yc[:, t0:t0 + n], pa, AF.Silu))
            P.ts(gs2[:, j, :], Yc[:], lng[:, j:j + 1], ALU.mult)
            wqg.get()
        for hf in range(2):
            P.dma([(Sgs[hf][:], st_gla[hf * 8:(hf + 1) * 8, h].rearrange("s d e -> d s e"))], "Sg%d" % hf)
        for g, (c0, n) in enumerate(SEGS[:16]):
            gla_seg(h, g, c0, n, g == 0, g == 15, None, o_gla_p[h])
        gla_sample(h)
    P.pop()
    P.pop()
    if stage <= 2:
        print('sbuf_remaining', nc.sbuf_bytes_remaining)
        print('counts', P.cnt, 'waits', P.nwaits, 'dma slots', len(P.dma_cnt))
        P.finish(); return nc, cn

    def out_proj(wo_src, post_g_src, res_src, dst_fn, next_pre_g):
        P.push()
        tag = "L%d" % k.ui
        wo = P.sb("wo" + tag, [128, 16, 1024], BF16)
        gpo = P.sb("gpo" + tag, [128, 1024], F32)
        h1t = [P.sb("h1a" + tag, [128, 1024], F32), P.sb("h1b" + tag, [128, 1024], F32)]
        xt = [P.sb("xta" + tag, [128, 1024], F32), P.sb("xtb" + tag, [128, 1024], F32)]
        ss2 = P.sb("ss2" + tag, [128, 4], F32)
        junk2 = [P.sb("jka" + tag, [128, 512], BF16), P.sb("jkb" + tag, [128, 512], BF16)]
        P.dma([(gpo[:], bcast(post_g_src[0:1, :]))], "c_gpo")
        if next_pre_g is not None:
            gbc = P.sb("gbc" + tag, [128, 1024], F32)
            P.dma([(gbc[:], bcast(next_pre_g[0:1, :]))], "c_gbc")
        for f in range(16):
            i = k.wi % 2; k.wi += 1
            P.dma([(wst[i][:], wo_src[f])], "wst%d" % i)
            P.cp(wo[:, f, :], wst[i][:], eng="pool")
        rows = [(i * 128, 128, i * 128) for i in range(16)] + [(2048, 16, 2048)]
        for i, (r0, n, col0) in enumerate(rows):
            ht = h1t[i % 2]; xr = xt[i % 2]
            P.dma([(xr[0:n, :], res_src(r0, n))], "xr%d" % (i % 2))
            ssx = ss2[0:n, 2 * (i % 2):2 * (i % 2) + 2]
            pb2 = (pA[0], pA[1]) if i % 2 == 0 else (pC[0], pC[1])
            P.memset(ssx, 0.0, eng="dve")
            for hf in range(2):
                for f in range(16):
                    P.mm(pb2[hf][0:n, :], oT[:, f, col0:col0 + n], wo[:, f, hf * 512:(hf + 1) * 512], start=(f == 0), stop=(f == 15))
                P.actf(junk2[i % 2][0:n, :], pb2[hf][0:n, :], AF.Square, accum_out=ssx[:, hf:hf + 1])
            P.tt(ssx[:, 0:1], ssx[:, 0:1], ssx[:, 1:2], ALU.add)
            rstd_of(ssx[:, 0:1], 1.0 / 1024)
            for hf in range(2):
                sl = slice(hf * 512, (hf + 1) * 512)
                P.stt(ht[0:n, sl], pb2[hf][0:n, :], ssx[:, 0:1], gpo[0:n, sl], ALU.mult, ALU.mult)
            P.tt(ht[0:n, :], ht[0:n, :], xr[0:n, :], ALU.add)
            dst_fn(ht, r0, n, i)
            if next_pre_g is not None:
                norm_to_uT(ht, n, col0, gbc)
        P.pop()

    def l0_dst(ht, r0, n, i):
        P.dma([(h1scr[r0:r0 + n, :], ht[0:n, :])], "h1st%d" % (i % 2))

    out_proj(wo0, e_post_g, lambda r0, n: (xp[r0:r0 + n, :] if r0 < 2048 else xs[:, :]), l0_dst, o_pre_g)
    if stage <= 3:
        print('sbuf_remaining', nc.sbuf_bytes_remaining)
        print('counts', P.cnt, 'waits', P.nwaits, 'dma slots', len(P.dma_cnt))
        P.finish(); return nc, cn

    P.dma([(cw[:], s_cw[:]), (cb[:], s_cb[:])], "c_cw")
    load_hist(hist, st_sconv)
    P.push()
    wsm1 = P.sb("wsm1", [128, 8, 128], BF16)
    load_w(w1, 40, wsm1[:])
    sAl = P.sb("sAl", [128, 32], F32); sdtb = P.sb("sdtb", [128, 32], F32); negA1 = P.sb("negA1", [128, 32], F32)
    sD = P.sb("sD", [128, 32, 1], F32)
    sng = P.sb("sng", [128, 512], F32)
    P.dma([(sAl[:], bcast(s_alog[0:1, :])), (sdtb[:], bcast(s_dtb[0:1, :])),
           (sD[:].rearrange("p h o -> p (h o)"), bcast(s_d[0:1, :]))], "c_misc")
    P.actf(negA1[:], sAl[:], AF.Exp)
    P.ts(negA1[:], negA1[:], -1.0, ALU.mult)
    BcT = P.sb("BcT", [128, NT], BF16); CcT = P.sb("CcT", [128, NT], BF16)
    xT = P.sb("xT", [128, 4, NT], BF16)
    wz = P.sb("wz", [128, 8, 512], BF16)
    hT = [P.sb("hTa", [128, 8, 64], F32), P.sb("hTb", [128, 8, 64], F32)]
    hTh = [P.sb("hTha", [128, 512], BF16), P.sb("hThb", [128, 512], BF16)]
    hin = [P.sb("hin0", [128, 4, 128], F32), P.sb("hin1", [128, 4, 128], F32)]
    hld = hin[0]
    sc1s = [P.sb("sc1_%d" % i, [128, 48, 1], F32) for i in range(2)]
    lls = [P.sb("ll_%d" % i, [128, 16, 1], F32) for i in range(2)]
    t8s = [P.sb("t8_%d" % i, [128, 8], F32) for i in range(2)]
    xdt = P.sb("xdt", [128, 8, 64], BF16); xdec = P.sb("xdec", [128, 8, 64], BF16); xtk = P.sb("xtk", [128, 8, 64], BF16)
    Btok = P.sb("Btok", [128, 128], BF16)
    cbT = P.sb("cbT", [128, 128], F32)
    dec1 = [P.sb("dec1_%d" % i, [128, 128], BF16) for i in range(8)]
    mT = [P.sb("mT_%d" % i, [128, 128], BF16) for i in range(8)]
    yis = P.sb("yis", [128, 8, 64], F32); yv = P.sb("yv", [128, 8, 64], F32); zs = P.sb("zs", [128, 512], F32)
    ynb = P.sb("ynb", [128, 512], BF16); oss3 = P.sb("oss3", [128, 1], F32)
    k.s3 = 0

    def bc3(ap_n81, n):
        return ap_n81.to_broadcast([n, 8, 64])

    def ssd_scal(g, c0, n, pb):
        h0 = 8 * g
        sc = sc1s[pb][0:n]; ll = lls[pb]; t8 = t8s[pb]
        for kc in range(8):
            P.mm(pC[0][0:n, 256:264], uT[:, kc, c0:c0 + n], wsm1[:, kc, h0:h0 + 8], start=(kc == 0), stop=(kc == 7))
        yield
        P.tt(t8[0:n, :], pC[0][0:n, 256:264], sdtb[0:n, h0:h0 + 8], ALU.add)
        yield
        P.actf(t8[0:n, :], t8[0:n, :], AF.Exp)
        P.actf(sc[:, 0:8, 0], t8[0:n, :], AF.Ln, bias=1.0)
        yield
        P.tt(sc[:, 8:16, 0], sc[:, 0:8, 0], negA1[0:n, h0:h0 + 8], ALU.mult)
        yield
        P.mm(pC[0][0:n, 288:296], U_f[0:n, 0:n], sc[:, 8:16, 0])
        P.mm(pC[0][:, 296:304], ones_f[0:n, :], sc[:, 8:16, 0])
        yield
        P.cp(sc[:, 16:24, 0], pC[0][0:n, 288:296])
        P.ts(sc[:, 24:32, 0], pC[0][0:n, 288:296], -1.0, ALU.mult)
        P.cp(ll[:, 0:8, 0], pC[0][:, 296:304])
        yield
        P.actf(sc[:, 32:40, 0], pC[0][0:n, 288:296], AF.Exp)
        P.actf(ll[:, 8:16, 0], pC[0][:, 296:304], AF.Exp)
        yield
        P.tt(t8[0:n, :], ll[0:n, 0:8, 0], sc[:, 16:24, 0], ALU.subtract)
        yield
        P.actf(sc[:, 40:48, 0], t8[0:n, :], AF.Exp)
        yield

    def ssd_seg(g, gi, c0, n, first, last, s_src, s_dst, pb):
        if first:
            k.s3 += 1
        hf = hT[k.s3 % 2]; hb = hTh[k.s3 % 2]
        hf2 = hf[:].rearrange("p h q -> p (h q)")
        if first:
            if s_src is None:
                P.memset(hf[:], 0.0)
            else:
                P.dma([(hld[:], s_src.rearrange("(j p) n -> p j n", p=128))], "hld")
                for j in range(4):
                    P.tr(pC[0][:, j * 128:(j + 1) * 128], hld[:, j, :], ident_f[:])
                P.cp(hf2, pC[0][:, :], eng="act")
            P.cp(hb[:], hf2, eng="pool")
        h0 = 8 * g
        sc = sc1s[pb][0:n]; ll = lls[pb]
        yield
        for j in range(4):
            P.tr(pT[0:n, j * 128:(j + 1) * 128], xT[:, j, c0:c0 + n], ident_b[:])
        P.tr(pT[0:n, 512:640], BcT[:, c0:c0 + n], ident_b[:])
        ptx = pT[0:n, 0:512].rearrange("p (h q) -> p h q", h=8)
        P.cp(xtk[0:n], ptx, eng="act")
        P.cp(Btok[0:n, :], pT[0:n, 512:640], eng="act")
        yield
        P.tt(xdt[0:n], xtk[0:n], bc3(sc[:, 0:8, :], n), ALU.mult)
        P.tt(xdec[0:n], xdt[0:n], bc3(sc[:, 40:48, :], n), ALU.mult, eng="pool")
        yield
        P.mm(pC[0][0:n, 0:n], BcT[:, c0:c0 + n], CcT[:, c0:c0 + n])
        P.cp(cbT[0:n, 0:n], pC[0][0:n, 0:n], eng="act")
        P.mm(pC[3][0:n, :], CcT[:, c0:c0 + n], hb[:])
        P.cp(yis[0:n].rearrange("p h q -> p (h q)"), pC[3][0:n, :], eng="act")
        for kc in range(8):
            P.mm(pA[0][0:n, :], uT[:, kc, c0:c0 + n], wz[:, kc, :], start=(kc == 0), stop=(kc == 7))
        P.actf(zs[0:n, :], pA[0][0:n, :], AF.Silu)
        yield
        P.mm(pA[1][:, :], Btok[0:n, :], xdec[0:n].rearrange("p h q -> p (h q)"))
        P.tt(hf[:], hf[:], ll[:, 8:16, :].to_broadcast([128, 8, 64]), ALU.mult)
        P.tt(hf2, hf2, pA[1][:, :], ALU.add)
        if not last:
            P.cp(hb[:], hf2, eng="pool")
        else:
            for j in range(4):
                P.tr(pC[0][:, j * 128:(j + 1) * 128], hf2[:, j * 128:(j + 1) * 128], ident_f[:])
            P.cp(hld[:].rearrange("p j n -> p (j n)"), pC[0][:, :], eng="act")
            P.dma([(s_dst.rearrange("(j p) n -> p j n", p=128), hld[:])], "hst")
        yield
        if n == 1:
            P.cp(mT[0][0:1, 0:1], cbT[0:1, 0:1], eng="pool")
            P.mm(pC[2][0:1, :], mT[0][0:1, 0:1], xdt[0:1].rearrange("p h q -> p (h q)"))
        else:
            for hh in range(8):
                gb = pC[1] if hh < 4 else pS
                r0 = (hh % 4) * 128
                P.mm(gb[:, r0:r0 + 128], sc[:, 8 + hh, :].to_broadcast([128, 128]), U_f[:], start=True, stop=False)
                P.mm(gb[:, r0:r0 + 128], ident_f[:], negmask[:], start=False, stop=True)
            for hh in range(8):
                gb = pC[1] if hh < 4 else pS
                r0 = (hh % 4) * 128
                P.actf(dec1[hh][:], gb[:, r0:r0 + 128], AF.Exp, bias=sc[:, 24 + hh, :])
            for hh in range(8):
                P.tt(mT[hh][:], cbT[:], dec1[hh][:], ALU.mult, eng=("pool" if hh % 2 else "dve"))
            for hh in range(8):
                P.mm(pC[2][:, hh * 64:(hh + 1) * 64], mT[hh][:], xdt[:, hh, :])
        yield
        P.tt(yis[0:n], yis[0:n], bc3(sc[:, 32:40, :], n), ALU.mult)
        P.tt(yv[0:n].rearrange("p h q -> p (h q)"), yis[0:n].rearrange("p h q -> p (h q)"), pC[2][0:n, :], ALU.add)
        P.tt(yis[0:n], xtk[0:n], sD[0:n, h0:h0 + 8, :].to_broadcast([n, 8, 64]), ALU.mult, eng="pool")
        P.tt(yv[0:n], yv[0:n], yis[0:n], ALU.add)
        yv2 = yv[0:n].rearrange("p h q -> p (h q)")
        P.tt(yv2, yv2, zs[0:n, :], ALU.mult)
        P.memset(oss3[0:n, :], 0.0, eng="dve")
        P.actf(ynb[0:n, :], yv2, AF.Square, accum_out=oss3[0:n, :])
        rstd_of(oss3[0:n, :], 1.0 / 512)
        P.stt(ynb[0:n, :], yv2, oss3[0:n, :], sng[0:n, :], ALU.mult, ALU.mult)
        for j in range(4):
            P.tr(pT[:, j * 128:j * 128 + n], ynb[0:n, j * 128:(j + 1) * 128], ident_b[0:n, 0:n])
        P.cp(oT[:, 4 * g:4 * g + 4, c0:c0 + n], pT[:, 0:512].rearrange("p (j t) -> p j t", j=4)[:, :, 0:n], eng="act")

    dcol = P.sb("dcol", [128, 16], F32); ngcol = P.sb("ngcol", [128, 16], F32)
    P.dma([(dcol[:], s_dcol[:]), (ngcol[:], s_ngcol[:])], "c_cols")
    sdt = P.sb("sdt", [16, 32, 1], F32); sea = P.sb("sea", [16, 32, 1], F32)
    for kc in range(8):
        P.mm(pS[0:16, 0:32], uT[:, kc, 2048:2064], wsm1[:, kc, 0:32], start=(kc == 0), stop=(kc == 7))
    sdt2 = sdt[:].rearrange("p h o -> p (h o)"); sea2 = sea[:].rearrange("p h o -> p (h o)")
    P.tt(sdt2, pS[0:16, 0:32], sdtb[0:16, :], ALU.add)
    P.actf(sdt2, sdt2, AF.Exp)
    P.actf(sdt2, sdt2, AF.Ln, bias=1.0)
    P.tt(sea2, sdt2, negA1[0:16, :], ALU.mult)
    P.actf(sea2, sea2, AF.Exp)
    exE = P.sb("exE", [16, 8, 64], F32)
    DTc = P.sb("DTc", [128, 4, 16], F32); EAc = P.sb("EAc", [128, 4, 16], F32); DTX = P.sb("DTX", [128, 4, 16], F32)
    BCt = P.sb("BCt", [16, 256], BF16)
    t1s = yis[:].rearrange("p h q -> p (h q)").rearrange("p (j n) -> p j n", j=4)
    ycol = P.sb("ycol", [128, 4, 16], F32); yz = P.sb("yz", [128, 4, 16], F32); zsT = P.sb("zsT", [128, 4, 16], F32)
    rs16 = P.sb("rs16", [128, 16], F32)

    def ssd_sample(g):
        h0 = 8 * g
        for w, src_ in enumerate((sdt, sea)):
            P.cp(exE[:], src_[:, h0:h0 + 8, :].to_broadcast([16, 8, 64]), eng="pool")
            ex2 = exE[:].rearrange("p h q -> p (h q)")
            for j in range(4):
                P.tr(pS[:, (w * 4 + j) * 16:(w * 4 + j + 1) * 16], ex2[:, j * 128:(j + 1) * 128], ident_f[0:16, 0:16])
        P.cp(DTc[:].rearrange("p j s -> p (j s)"), pS[:, 0:64], eng="act")
        P.cp(EAc[:].rearrange("p j s -> p (j s)"), pS[:, 64:128], eng="act")
        P.tt(DTX[:], xT[:, :, 2048:2064], DTc[:], ALU.mult)
        P.tr(pT[0:16, 0:128], BcT[:, 2048:2064], ident_b[:])
        P.tr(pT[0:16, 128:256], CcT[:, 2048:2064], ident_b[:])
        P.cp(BCt[:], pT[0:16, 0:256], eng="act")
        for j in range(4):
            for kc in range(8):
                P.mm(pA[0][:, j * 16:(j + 1) * 16], wz[:, kc, j * 128:(j + 1) * 128], uT[:, kc, 2048:2064], start=(kc == 0), stop=(kc == 7))
        P.actf(zsT[:].rearrange("p j s -> p (j s)"), pA[0][:, 0:64], AF.Silu)
        for s in range(16):
            hi = hin[s % 2]; ho = hi
            pb = pC[s % 4]
            P.dma([(hi[:], st_ssd[s, g * 512:(g + 1) * 512, :].rearrange("(j p) n -> p j n", p=128))], "hin%d" % (s % 2))
            P.mm(pb[:, 0:256], ident_b[0:16, s:s + 1].to_broadcast([16, 128]), BCt[:])
            P.tt(t1s, pb[:, 0:128].unsqueeze(1).to_broadcast([128, 4, 128]), DTX[:, :, s:s + 1].to_broadcast([128, 4, 128]), ALU.mult)
            P.tt(hi[:], hi[:], EAc[:, :, s:s + 1].to_broadcast([128, 4, 128]), ALU.mult, eng="pool")
            P.tt(ho[:], hi[:], t1s, ALU.add, eng="pool")
            P.dma([(o_ssd_s[s, g * 512:(g + 1) * 512, :].rearrange("(j p) n -> p j n", p=128), ho[:])], "hin%d" % (s % 2))
            P.tt(t1s, ho[:], pb[:, 128:256].unsqueeze(1).to_broadcast([128, 4, 128]), ALU.mult)
            P.red(ycol[:, :, s], t1s)
        for j in range(4):
            P.stt(ycol[:, j, :], xT[:, j, 2048:2064], dcol[:, 4 * g + j:4 * g + j + 1], ycol[:, j, :], ALU.mult, ALU.add)
        P.tt(yz[:], ycol[:], zsT[:], ALU.mult)
        P.tt(ycol[:], yz[:], yz[:], ALU.mult)
        for j in range(4):
            P.mm(pS[:, 256:272], ones_f[:], ycol[:, j, :], start=(j == 0), stop=(j == 3))
        P.actf(rs16[:], pS[:, 256:272], AF.Ln, bias=EPS, scale=1.0 / 512)
        P.actf(rs16[:], rs16[:], AF.Exp, scale=-0.5)
        for j in range(4):
            P.stt(oT[:, 4 * g + j, 2048:2064], yz[:, j, :], ngcol[:, 4 * g + j:4 * g + j + 1], rs16[:], ALU.mult, ALU.mult)

    s_items = []
    for g in range(4):
        s_items += [(w1, 32 + g, None), (w1, 36 + g, None)]
        for j in range(4):
            s_items += [(w1, 16 + 4 * g + j, None), (w1, 4 * g + j, wz[:, :, j * 128:(j + 1) * 128])]
    wq1 = WQ(s_items)
    for g in range(4):
        P.dma([(sng[:], bcast(s_ng[0:1, g * 512:(g + 1) * 512]))], "c_sng")
        conv_tile(wq1, 16 + g, o_sconv_p, o_sconv_s); P.cp(BcT[:], Yc[:], eng="pool")
        conv_tile(wq1, 20 + g, o_sconv_p, o_sconv_s); P.cp(CcT[:], Yc[:], eng="pool")
        for j in range(4):
            conv_tile(wq1, 4 * g + j, o_sconv_p, o_sconv_s); P.cp(xT[:, j, :], Yc[:], eng="pool")
            wq1.get()
        interleave([ssd_scal(g, 0, 128, 0)])
        for gi, (c0, n) in enumerate(SEGS[:16]):
            nxt = ssd_scal(g, SEGS[gi + 1][0], 128, (gi + 1) % 2) if gi < 15 else None
            interleave([ssd_seg(g, gi, c0, n, gi == 0, gi == 15, None, o_ssd_p[g * 512:(g + 1) * 512, :], gi % 2), nxt])
        ssd_sample(g)
    P.pop()
    if stage <= 4:
        print('sbuf_remaining', nc.sbuf_bytes_remaining)
        print('counts', P.cnt, 'waits', P.nwaits, 'dma slots', len(P.dma_cnt))
        P.finish(); return nc, cn

    def l1_dst(ht, r0, n, i):
        if r0 < 2048:
            P.dma([(o_yp[r0:r0 + n, :], ht[0:n, :])], "yst%d" % (i % 2))
        else:
            P.dma([(o_ys[:, :], ht[0:n, :])], "yst%d" % (i % 2))

    out_proj(wo1, o_post_g, lambda r0, n: h1scr[r0:r0 + n, :], l1_dst, None)
    print('sbuf_remaining', nc.sbuf_bytes_remaining)
    print('counts', P.cnt, 'waits', P.nwaits, 'dma slots', len(P.dma_cnt))
    P.finish()
    return nc, cn

def _blocks(Wc):
    nb = Wc.shape[1] // 128
    return Wc.reshape(8, 128, nb, 128).transpose(2, 1, 0, 3).reshape(nb, 128, 1024)


def shared_inputs(inp):
    f = np.float32
    m = {}
    W = inp["e_w_in"][0]
    cols = list(range(0, 4096)) + list(range(4112, 7184))
    sm = np.zeros((1024, 128), f)
    sm[:, 0:16] = W[:, 4096:4112]
    sm[:, 16:32] = W[:, 7184:7200]
    m["w0"] = np.ascontiguousarray(np.concatenate([_blocks(W[:, cols]), _blocks(sm)], 0))
    W1 = inp["o_w_in"][0]
    sm1 = np.zeros((1024, 128), f)
    sm1[:, 0:32] = W1[:, 5120:5152]
    m["w1"] = np.ascontiguousarray(np.concatenate([_blocks(W1[:, 0:5120]), _blocks(sm1)], 0))
    m["wo0"] = np.ascontiguousarray(inp["e_w_out"][0].reshape(16, 128, 1024))
    m["wo1"] = np.ascontiguousarray(inp["o_w_out"][0].reshape(16, 128, 1024))
    for nme in ("e_pre_g", "e_post_g", "o_pre_g", "o_post_g"):
        m[nme] = np.ascontiguousarray(inp[nme].reshape(1, 1024))
    m["e_cw"] = np.ascontiguousarray(inp["e_conv_w"][0].T.reshape(24, 128, 4).transpose(1, 0, 2).reshape(128, 96))
    m["e_cb"] = np.ascontiguousarray(inp["e_conv_b"][0].reshape(24, 128).T)
    m["s_cw"] = np.ascontiguousarray(inp["ssd_conv_w"][0].T.reshape(24, 128, 4).transpose(1, 0, 2).reshape(128, 96))
    m["s_cb"] = np.ascontiguousarray(inp["ssd_conv_b"][0].reshape(24, 128).T)
    m["g_alog"] = np.ascontiguousarray(inp["gdn_a_log"].reshape(1, 8))
    m["g_dtb"] = np.ascontiguousarray(inp["gdn_dt_bias"].reshape(1, 8))
    m["g_ng"] = np.ascontiguousarray(inp["gdn_norm_g"].reshape(128, 1))
    m["l_wlr"] = np.ascontiguousarray(np.concatenate([inp["gla_w_lr"][0], inp["gla_b_lr"][0][None]], 0))
    m["l_ng"] = np.ascontiguousarray(inp["gla_norm_g"].reshape(2, 128).T)
    m["s_alog"] = np.ascontiguousarray(inp["ssd_a_log"].reshape(1, 32))
    m["s_dtb"] = np.ascontiguousarray(inp["ssd_dt_bias"].reshape(1, 32))
    m["s_d"] = np.ascontiguousarray(inp["ssd_d"].reshape(1, 32))
    m["s_ng"] = np.ascontiguousarray(inp["ssd_norm_g"].reshape(1, 2048))
    m["s_dcol"] = np.ascontiguousarray(np.repeat(inp["ssd_d"].reshape(32), 64).reshape(16, 128).T)
    m["s_ngcol"] = np.ascontiguousarray(inp["ssd_norm_g"].reshape(16, 128).T)
    for n_, a in consts_np().items():
        m["c_" + n_] = a
    return m


def make_in_maps(inp):
    sh = shared_inputs(inp)
    in_maps = []
    for c in range(8):
        m = dict(sh)
        sl = slice(c * 16, (c + 1) * 16)
        m["xp"] = np.ascontiguousarray(inp["x_prompt"][c])
        m["xs"] = np.ascontiguousarray(inp["x_sample"][sl, 0])
        m["st_gconv"] = np.ascontiguousarray(inp["state_gdn_conv"][0, sl].reshape(48, 3072))
        m["st_gdn"] = np.ascontiguousarray(inp["state_gdn"][0, sl])
        m["st_gla"] = np.ascontiguousarray(inp["state_gla"][0, sl])
        m["st_sconv"] = np.ascontiguousarray(inp["state_ssd_conv"][0, sl].reshape(48, 3072))
        m["st_ssd"] = np.ascontiguousarray(inp["state_ssd"][0, sl].reshape(16, 2048, 128))
        in_maps.append(m)
    return in_maps


def gather(R):
    f = np.float32
    st = lambda k_: np.stack([np.asarray(R[c][k_], f) for c in range(8)])
    cat = lambda k_: np.concatenate([np.asarray(R[c][k_], f) for c in range(8)])
    return (st("o_yp"), cat("o_ys").reshape(128, 1, 1024),
            st("o_gconv_p")[None], st("o_gdn_p")[None], st("o_gla_p")[None],
            st("o_sconv_p")[None], st("o_ssd_p").reshape(1, 8, 32, 64, 128),
            cat("o_gconv_s").reshape(1, 128, 3, 3072), cat("o_gdn_s")[None], cat("o_gla_s")[None],
            cat("o_sconv_s").reshape(1, 128, 3, 3072), cat("o_ssd_s").reshape(1, 128, 32, 64, 128))


def kernel(**inp):
    inp = {k_: np.asarray(v) for k_, v in inp.items()}
    nc, cn = build()
    res = run_bass_kernel_spmd(nc, make_in_maps(inp), core_ids=list(range(8)))
    return gather(res.results)
```
